# Optimizing a Trainium2 kernel written in Bass

```python
import math
import numpy as np
import jax
import jax.numpy as jnp
from jax import lax

D_MODEL = 1024
BATCH = 2
SEQ = 8192
DEPTH = 2

NORM_EPS = 1e-6
N_MOD = 6
Q_BLOCK = 128
NEG_INF = -1e30

SSD_HEADS = 16
SSD_HEAD_DIM = 64
SSD_INNER = SSD_HEADS * SSD_HEAD_DIM
SSD_STATE = 128
SSD_GROUPS = 2
SSD_CONV = 4
SSD_CONV_CH = SSD_INNER + 2 * SSD_GROUPS * SSD_STATE
SSD_CHUNK = 128

MLA_HEADS = 8
MLA_Q_RANK = 384
MLA_KV_RANK = 256
MLA_NOPE = 64
MLA_ROPE = 32
MLA_V = 64
ROPE_THETA = 10000.0

GLA_HEADS = 4
GLA_DK = 128
GLA_DV = 256
GLA_GATE_RANK = 16
GLA_GATE_NORM = 16.0
GLA_CHUNK = 64

NSA_HEADS = 8
NSA_GROUPS = 2
NSA_HEAD_DIM = 64
NSA_CMP_BLOCK = 32
NSA_CMP_STRIDE = 16
NSA_CMP_HIDDEN = 256
NSA_SEL_BLOCK = 64
NSA_N_SEL = 16
NSA_WINDOW = 512
NSA_FORCE = 1e4

FFN_HIDDEN = -(-8 * D_MODEL // (3 * 256)) * 256

L0_SIZES = (SSD_INNER, SSD_CONV_CH, SSD_HEADS, MLA_Q_RANK, MLA_KV_RANK, MLA_ROPE)
L0_IN = sum(L0_SIZES)
L0_CAT = SSD_INNER + MLA_HEADS * MLA_V
L1_SIZES = (GLA_HEADS * GLA_DK, GLA_HEADS * GLA_DK, GLA_HEADS * GLA_DV, GLA_GATE_RANK,
            GLA_HEADS * GLA_DV, NSA_HEADS * NSA_HEAD_DIM, 6 * NSA_GROUPS * NSA_HEAD_DIM, 3 * NSA_HEADS)
L1_IN = sum(L1_SIZES)
L1_CAT = GLA_HEADS * GLA_DV + NSA_HEADS * NSA_HEAD_DIM

kernel_name = 'hybrid_ssd_mla_gla_nsa_trunk'


def rmsnorm(x, g):
    xf = x.astype(jnp.float32)
    y = xf * lax.rsqrt(jnp.mean(xf * xf, axis=-1, keepdims=True) + NORM_EPS)
    return (y * g.astype(jnp.float32)).astype(x.dtype)


def split_cols(h, sizes):
    return jnp.split(h, [int(v) for v in np.cumsum(sizes)[:-1]], axis=-1)


def masked_softmax(s, mask):
    s = jnp.where(mask, s.astype(jnp.float32), NEG_INF)
    m = jnp.max(s, axis=-1, keepdims=True)
    e = jnp.where(mask, jnp.exp(s - m), 0.0)
    return e / jnp.maximum(jnp.sum(e, axis=-1, keepdims=True), 1e-30)


def alibi_slopes(n):
    return jnp.asarray(2.0 ** (-8.0 * np.arange(1, n + 1) / n), jnp.float32)


def rope_cos_sin(positions, dim):
    inv = jnp.asarray(ROPE_THETA ** (-np.arange(0, dim, 2) / dim), jnp.float32)
    ang = positions.astype(jnp.float32)[..., None] * inv
    return jnp.cos(ang), jnp.sin(ang)


def apply_rope(x, cos, sin):
    x1, x2 = jnp.split(x, 2, axis=-1)
    return jnp.concatenate([x1 * cos - x2 * sin, x2 * cos + x1 * sin], axis=-1).astype(x.dtype)


def causal_depthwise_conv(x, w, b):
    k_w, ch = w.shape
    y = lax.conv_general_dilated(x, w[:, None, :].astype(x.dtype), window_strides=(1,),
                                 padding=[(k_w - 1, 0)], dimension_numbers=('NWC', 'WIO', 'NWC'),
                                 feature_group_count=ch)
    return y + b


def segsum(a):
    n = a.shape[-1]
    cs = jnp.cumsum(a, axis=-1)
    diff = cs[..., :, None] - cs[..., None, :]
    return jnp.where(jnp.tril(jnp.ones((n, n), bool)), diff, -jnp.inf)


def ssd_chunked(x, dt, a, bmat, cmat):
    b, t, nh, p = x.shape
    g, n = bmat.shape[2], bmat.shape[3]
    hg = nh // g
    L = SSD_CHUNK
    c = t // L
    xd = (x.astype(jnp.float32) * dt[..., None]).reshape(b, c, L, g, hg, p)
    adt = (dt * a).reshape(b, c, L, g, hg).transpose(0, 3, 4, 1, 2)
    bm = bmat.astype(jnp.float32).reshape(b, c, L, g, n)
    cm = cmat.astype(jnp.float32).reshape(b, c, L, g, n)
    a_cs = jnp.cumsum(adt, axis=-1)
    lmat = jnp.exp(segsum(adt))
    y_diag = jnp.einsum('bclgn,bcsgn,bghcls,bcsghp->bclghp', cm, bm, lmat, xd)
    decay_states = jnp.exp(a_cs[..., -1:] - a_cs)
    states = jnp.einsum('bcsgn,bghcs,bcsghp->bcghpn', bm, decay_states, xd)
    states = jnp.concatenate([jnp.zeros_like(states[:, :1]), states], axis=1)
    a_chunk = jnp.pad(a_cs[..., -1], ((0, 0), (0, 0), (0, 0), (1, 0)))
    decay_chunk = jnp.exp(segsum(a_chunk))
    states = jnp.einsum('bghzc,bcghpn->bzghpn', decay_chunk, states)[:, :-1]
    y_off = jnp.einsum('bclgn,bcghpn,bghcl->bclghp', cm, states, jnp.exp(a_cs))
    return (y_diag + y_off).reshape(b, t, nh, p)


def gla_chunked(q, k, v, log_a):
    b, t, nh, dk = q.shape
    dv = v.shape[-1]
    L = GLA_CHUNK
    c = t // L

    def to_chunks(z):
        return z.astype(jnp.float32).reshape(b, c, L, nh, -1).transpose(1, 0, 3, 2, 4)

    qc = to_chunks(q) * (dk ** -0.5)
    kc, vc = to_chunks(k), to_chunks(v)
    bc = jnp.cumsum(to_chunks(log_a), axis=3)
    causal = jnp.tril(jnp.ones((L, L), bool))[:, :, None]

    def step(state, inp):
        qi, ki, vi, bi = inp
        o_inter = jnp.einsum('bhtk,bhkv->bhtv', qi * jnp.exp(bi), state)
        decay = jnp.exp(jnp.where(causal, bi[:, :, :, None, :] - bi[:, :, None, :, :], -jnp.inf))
        att = jnp.einsum('bhtk,bhsk,bhtsk->bhts', qi, ki, decay)
        o = o_inter + jnp.einsum('bhts,bhsv->bhtv', att, vi)
        b_last = bi[:, :, -1]
        state = jnp.exp(b_last)[..., None] * state + jnp.einsum(
            'bhsk,bhsv->bhkv', ki * jnp.exp(b_last[:, :, None] - bi), vi)
        return state, o

    s0 = jnp.zeros((b, nh, dk, dv), jnp.float32)
    _, o = lax.scan(step, s0, (qc, kc, vc, bc))
    return o.transpose(1, 0, 3, 2, 4).reshape(b, t, nh, dv)


def causal_block_attention(q, k, v, scale):
    b, t, nh, _ = q.shape
    dv = v.shape[-1]
    key_pos = jnp.arange(t)

    def one_block(i):
        q0 = i * Q_BLOCK
        qb = lax.dynamic_slice_in_dim(q, q0, Q_BLOCK, 1)
        qpos = q0 + jnp.arange(Q_BLOCK)
        s = jnp.einsum('bqhd,bkhd->bhqk', qb, k) * scale
        p = masked_softmax(s, key_pos[None, :] <= qpos[:, None])
        return jnp.einsum('bhqk,bkhd->bqhd', p.astype(v.dtype), v)

    out = lax.map(one_block, jnp.arange(t // Q_BLOCK))
    return out.transpose(1, 0, 2, 3, 4).reshape(b, t, nh * dv)


def nsa_attention(nq, nkv, ngate, cmp_pos, cmp_k_w1, cmp_k_w2, cmp_v_w1, cmp_v_w2):
    b, t, _ = nq.shape
    g, hg, dh = NSA_GROUPS, NSA_HEADS // NSA_GROUPS, NSA_HEAD_DIM
    q = nq.reshape(b, t, g, hg, dh)
    kc, vc, ks, vs, kw, vw = [z.reshape(b, t, g, dh) for z in split_cols(nkv, (g * dh,) * 6)]
    gates = jax.nn.sigmoid(ngate.astype(jnp.float32)).reshape(b, t, g, hg, 3)
    slopes = alibi_slopes(NSA_HEADS).reshape(g, hg)
    scale = dh ** -0.5

    n_cmp = (t - NSA_CMP_BLOCK) // NSA_CMP_STRIDE + 1
    win_idx = np.arange(n_cmp)[:, None] * NSA_CMP_STRIDE + np.arange(NSA_CMP_BLOCK)[None, :]

    def compress(z, w1, w2):
        zb = z[:, win_idx] + cmp_pos[None, None, :, None, :]
        zb = zb.transpose(0, 1, 3, 2, 4).reshape(b, n_cmp, g, NSA_CMP_BLOCK * dh)
        return jax.nn.silu(zb @ w1) @ w2

    k_cmp = compress(kc, cmp_k_w1, cmp_k_w2)
    v_cmp = compress(vc, cmp_v_w1, cmp_v_w2)
    cmp_end = np.arange(n_cmp) * NSA_CMP_STRIDE + NSA_CMP_BLOCK - 1

    n_slc = t // NSA_SEL_BLOCK
    n_sel = min(NSA_N_SEL, n_slc)
    c_start = np.arange(n_cmp) * NSA_CMP_STRIDE
    s_start = np.arange(n_slc) * NSA_SEL_BLOCK
    overlap = jnp.asarray((c_start[:, None] < s_start[None, :] + NSA_SEL_BLOCK)
                          & (c_start[:, None] + NSA_CMP_BLOCK > s_start[None, :]), jnp.float32)
    ks_blk = ks.reshape(b, n_slc, NSA_SEL_BLOCK, g, dh).transpose(0, 3, 1, 2, 4)
    vs_blk = vs.reshape(b, n_slc, NSA_SEL_BLOCK, g, dh).transpose(0, 3, 1, 2, 4)
    gather = jax.vmap(jax.vmap(lambda blocks, ix: blocks[ix]))

    pad = ((0, 0), (NSA_WINDOW, 0), (0, 0), (0, 0))
    kw_pad, vw_pad = jnp.pad(kw, pad), jnp.pad(vw, pad)
    blk = jnp.arange(n_slc)
    m_sel = n_sel * NSA_SEL_BLOCK

    def one_block(i):
        q0 = i * Q_BLOCK
        qpos = q0 + jnp.arange(Q_BLOCK)
        qb = lax.dynamic_slice_in_dim(q, q0, Q_BLOCK, 1)
        gb = lax.dynamic_slice_in_dim(gates, q0, Q_BLOCK, 1)
        dist_c = (qpos[:, None] - cmp_end[None, :]).astype(jnp.float32)
        s_c = jnp.einsum('bqghd,bngd->bghqn', qb, k_cmp) * scale - slopes[:, :, None, None] * dist_c
        p_c = masked_softmax(s_c, dist_c >= 0)
        o_c = jnp.einsum('bghqn,bngd->bqghd', p_c, v_cmp)
        imp = jnp.einsum('bghqn,nj->bgqj', p_c, overlap)
        forced = (blk[None, :] == 0) | (blk[None, :] == (qpos // NSA_SEL_BLOCK)[:, None])
        avail = blk[None, :] * NSA_SEL_BLOCK <= qpos[:, None]
        imp = jnp.where(forced, NSA_FORCE, jnp.where(avail, imp, -1.0))
        _, idx = lax.top_k(imp, n_sel)
        k_sel = gather(ks_blk, idx).reshape(b, g, Q_BLOCK, m_sel, dh)
        v_sel = gather(vs_blk, idx).reshape(b, g, Q_BLOCK, m_sel, dh)
        spos = (idx[..., None] * NSA_SEL_BLOCK + jnp.arange(NSA_SEL_BLOCK)).reshape(b, g, 1, Q_BLOCK, m_sel)
        dist_s = (qpos[:, None] - spos).astype(jnp.float32)
        s_s = jnp.einsum('bqghd,bgqmd->bghqm', qb, k_sel) * scale - slopes[None, :, :, None, None] * dist_s
        p_s = masked_softmax(s_s, dist_s >= 0)
        o_s = jnp.einsum('bghqm,bgqmd->bqghd', p_s, v_sel)
        kwb = lax.dynamic_slice_in_dim(kw_pad, q0, NSA_WINDOW + Q_BLOCK, 1)
        vwb = lax.dynamic_slice_in_dim(vw_pad, q0, NSA_WINDOW + Q_BLOCK, 1)
        kpos = q0 - NSA_WINDOW + jnp.arange(NSA_WINDOW + Q_BLOCK)
        dist_w = qpos[:, None] - kpos[None, :]
        mask_w = (dist_w >= 0) & (dist_w < NSA_WINDOW) & (kpos[None, :] >= 0)
        s_w = jnp.einsum('bqghd,bkgd->bghqk', qb, kwb) * scale - slopes[:, :, None, None] * dist_w.astype(jnp.float32)
        p_w = masked_softmax(s_w, mask_w)
        o_w = jnp.einsum('bghqk,bkgd->bqghd', p_w, vwb)
        o = gb[..., 0:1] * o_c + gb[..., 1:2] * o_s + gb[..., 2:3] * o_w
        return o.reshape(b, Q_BLOCK, NSA_HEADS * dh)

    out = lax.map(one_block, jnp.arange(t // Q_BLOCK))
    return out.transpose(1, 0, 2, 3).reshape(b, t, NSA_HEADS * dh)


def mixer_ssd_mla(h, positions, w_in, conv_w, conv_b, dt_bias, a_log, d_skip, ssm_norm_g,
                  q_a_norm_g, w_q_b, kv_a_norm_g, w_kv_b, w_out):
    b, t, _ = h.shape
    z, xbc, dt_raw, q_a, kv_a, k_pe = split_cols(h @ w_in, L0_SIZES)
    xbc = jax.nn.silu(causal_depthwise_conv(xbc, conv_w, conv_b))
    xs, bmat, cmat = split_cols(xbc, (SSD_INNER, SSD_GROUPS * SSD_STATE, SSD_GROUPS * SSD_STATE))
    dt = jax.nn.softplus(dt_raw.astype(jnp.float32) + dt_bias.astype(jnp.float32))
    a = -jnp.exp(a_log.astype(jnp.float32))
    xs = xs.reshape(b, t, SSD_HEADS, SSD_HEAD_DIM)
    y = ssd_chunked(xs, dt, a, bmat.reshape(b, t, SSD_GROUPS, SSD_STATE),
                    cmat.reshape(b, t, SSD_GROUPS, SSD_STATE))
    y = y + d_skip.astype(jnp.float32)[:, None] * xs.astype(jnp.float32)
    gsz = SSD_INNER // SSD_GROUPS
    y = y.reshape(b, t, SSD_GROUPS, gsz) * jax.nn.silu(z.astype(jnp.float32).reshape(b, t, SSD_GROUPS, gsz))
    y_ssd = rmsnorm(y, ssm_norm_g.reshape(SSD_GROUPS, gsz)).reshape(b, t, SSD_INNER).astype(h.dtype)
    cos, sin = rope_cos_sin(positions, MLA_ROPE)
    q = (rmsnorm(q_a, q_a_norm_g) @ w_q_b).reshape(b, t, MLA_HEADS, MLA_NOPE + MLA_ROPE)
    q = jnp.concatenate([q[..., :MLA_NOPE],
                         apply_rope(q[..., MLA_NOPE:], cos[:, :, None], sin[:, :, None])], axis=-1)
    kv = (rmsnorm(kv_a, kv_a_norm_g) @ w_kv_b).reshape(b, t, MLA_HEADS, MLA_NOPE + MLA_V)
    k_rot = apply_rope(k_pe, cos, sin)
    k = jnp.concatenate([kv[..., :MLA_NOPE],
                         jnp.broadcast_to(k_rot[:, :, None], (b, t, MLA_HEADS, MLA_ROPE))], axis=-1)
    o_mla = causal_block_attention(q, k, kv[..., MLA_NOPE:], (MLA_NOPE + MLA_ROPE) ** -0.5).astype(h.dtype)
    return jnp.concatenate([y_ssd, o_mla], axis=-1) @ w_out


def mixer_gla_nsa(h, w_in, w_gk2, b_gk, gla_norm_g, cmp_pos, cmp_k_w1, cmp_k_w2, cmp_v_w1, cmp_v_w2, w_out):
    b, t, _ = h.shape
    gq, gk, gv, glr, gg, nq, nkv, ngate = split_cols(h @ w_in, L1_SIZES)
    log_a = jax.nn.log_sigmoid((glr @ w_gk2 + b_gk).astype(jnp.float32)) / GLA_GATE_NORM
    o = gla_chunked(gq.reshape(b, t, GLA_HEADS, GLA_DK), gk.reshape(b, t, GLA_HEADS, GLA_DK),
                    gv.reshape(b, t, GLA_HEADS, GLA_DV), log_a.reshape(b, t, GLA_HEADS, GLA_DK))
    o = rmsnorm(o, gla_norm_g) * jax.nn.silu(gg.astype(jnp.float32).reshape(b, t, GLA_HEADS, GLA_DV))
    o_gla = o.reshape(b, t, GLA_HEADS * GLA_DV).astype(h.dtype)
    o_nsa = nsa_attention(nq, nkv, ngate, cmp_pos, cmp_k_w1, cmp_k_w2, cmp_v_w1, cmp_v_w2).astype(h.dtype)
    return jnp.concatenate([o_gla, o_nsa], axis=-1) @ w_out


def swiglu(h, w_gate, w_up, w_down):
    return (jax.nn.silu(h @ w_gate) * (h @ w_up)) @ w_down


def adaln_modulation(c, w, b):
    mod = jax.nn.silu(c) @ w + b
    return [m[:, None, :] for m in jnp.split(mod, N_MOD, axis=-1)]


def setup_inputs(seed: int = 0) -> dict:
    key = jax.random.key(seed)
    keys = jax.random.split(key, 64)
    counter = [0]
    f32 = jnp.float32
    D = D_MODEL

    def nk():
        counter[0] += 1
        return keys[counter[0] - 1]

    def nrm(shape, fan_in):
        return jax.random.normal(nk(), shape, f32) * (fan_in ** -0.5)

    def gain(n):
        return 1.0 + 0.05 * jax.random.normal(nk(), (n,), f32)

    def small(shape):
        return 0.01 * jax.random.normal(nk(), shape, f32)

    x = jax.random.normal(nk(), (BATCH, SEQ, D), f32)
    c = jax.random.normal(nk(), (BATCH, D), f32)
    offset = jax.random.randint(nk(), (BATCH, 1), 0, 1024, jnp.int32)
    positions = (offset + jnp.arange(SEQ, dtype=jnp.int32)[None, :]).astype(jnp.int32)
    dt0 = jnp.exp(jax.random.uniform(nk(), (SSD_HEADS,), f32) * (math.log(0.1) - math.log(0.001)) + math.log(0.001))
    dt_bias = dt0 + jnp.log(-jnp.expm1(-dt0))
    a_log = jnp.log(jax.random.uniform(nk(), (SSD_HEADS,), f32, minval=1.0, maxval=16.0))
    return {
        'x': x, 'c': c, 'positions': positions,
        'l0_ada_w': nrm((D, N_MOD * D), D), 'l0_ada_b': small((N_MOD * D,)),
        'l0_mix_pre_g': gain(D), 'l0_mix_post_g': gain(D),
        'l0_w_in': nrm((D, L0_IN), D),
        'l0_conv_w': nrm((SSD_CONV, SSD_CONV_CH), SSD_CONV), 'l0_conv_b': small((SSD_CONV_CH,)),
        'l0_dt_bias': dt_bias, 'l0_a_log': a_log,
        'l0_d_skip': 1.0 + 0.1 * jax.random.normal(nk(), (SSD_HEADS,), f32),
        'l0_ssm_norm_g': gain(SSD_INNER),
        'l0_q_a_norm_g': gain(MLA_Q_RANK),
        'l0_w_q_b': nrm((MLA_Q_RANK, MLA_HEADS * (MLA_NOPE + MLA_ROPE)), MLA_Q_RANK),
        'l0_kv_a_norm_g': gain(MLA_KV_RANK),
        'l0_w_kv_b': nrm((MLA_KV_RANK, MLA_HEADS * (MLA_NOPE + MLA_V)), MLA_KV_RANK),
        'l0_w_out': nrm((L0_CAT, D), L0_CAT),
        'l0_ffn_pre_g': gain(D), 'l0_ffn_post_g': gain(D),
        'l0_w_gate': nrm((D, FFN_HIDDEN), D), 'l0_w_up': nrm((D, FFN_HIDDEN), D),
        'l0_w_down': nrm((FFN_HIDDEN, D), FFN_HIDDEN),
        'l1_ada_w': nrm((D, N_MOD * D), D), 'l1_ada_b': small((N_MOD * D,)),
        'l1_mix_pre_g': gain(D), 'l1_mix_post_g': gain(D),
        'l1_w_in': nrm((D, L1_IN), D),
        'l1_w_gk2': nrm((GLA_GATE_RANK, GLA_HEADS * GLA_DK), GLA_GATE_RANK),
        'l1_b_gk': small((GLA_HEADS * GLA_DK,)),
        'l1_gla_norm_g': gain(GLA_DV),
        'l1_cmp_pos': 0.1 * jax.random.normal(nk(), (NSA_CMP_BLOCK, NSA_HEAD_DIM), f32),
        'l1_cmp_k_w1': nrm((NSA_CMP_BLOCK * NSA_HEAD_DIM, NSA_CMP_HIDDEN), NSA_CMP_BLOCK * NSA_HEAD_DIM),
        'l1_cmp_k_w2': nrm((NSA_CMP_HIDDEN, NSA_HEAD_DIM), NSA_CMP_HIDDEN),
        'l1_cmp_v_w1': nrm((NSA_CMP_BLOCK * NSA_HEAD_DIM, NSA_CMP_HIDDEN), NSA_CMP_BLOCK * NSA_HEAD_DIM),
        'l1_cmp_v_w2': nrm((NSA_CMP_HIDDEN, NSA_HEAD_DIM), NSA_CMP_HIDDEN),
        'l1_w_out': nrm((L1_CAT, D), L1_CAT),
        'l1_ffn_pre_g': gain(D), 'l1_ffn_post_g': gain(D),
        'l1_w_gate': nrm((D, FFN_HIDDEN), D), 'l1_w_up': nrm((D, FFN_HIDDEN), D),
        'l1_w_down': nrm((FFN_HIDDEN, D), FFN_HIDDEN),
    }


def reference(x, c, positions,
              l0_ada_w, l0_ada_b, l0_mix_pre_g, l0_mix_post_g, l0_w_in, l0_conv_w, l0_conv_b,
              l0_dt_bias, l0_a_log, l0_d_skip, l0_ssm_norm_g, l0_q_a_norm_g, l0_w_q_b,
              l0_kv_a_norm_g, l0_w_kv_b, l0_w_out, l0_ffn_pre_g, l0_ffn_post_g,
              l0_w_gate, l0_w_up, l0_w_down,
              l1_ada_w, l1_ada_b, l1_mix_pre_g, l1_mix_post_g, l1_w_in, l1_w_gk2, l1_b_gk,
              l1_gla_norm_g, l1_cmp_pos, l1_cmp_k_w1, l1_cmp_k_w2, l1_cmp_v_w1, l1_cmp_v_w2,
              l1_w_out, l1_ffn_pre_g, l1_ffn_post_g, l1_w_gate, l1_w_up, l1_w_down):
    ada = ((l0_ada_w, l0_ada_b), (l1_ada_w, l1_ada_b))
    norms = ((l0_mix_pre_g, l0_mix_post_g, l0_ffn_pre_g, l0_ffn_post_g),
             (l1_mix_pre_g, l1_mix_post_g, l1_ffn_pre_g, l1_ffn_post_g))
    ffns = ((l0_w_gate, l0_w_up, l0_w_down), (l1_w_gate, l1_w_up, l1_w_down))
    mixers = (
        lambda h: mixer_ssd_mla(h, positions, l0_w_in, l0_conv_w, l0_conv_b, l0_dt_bias, l0_a_log,
                                l0_d_skip, l0_ssm_norm_g, l0_q_a_norm_g, l0_w_q_b, l0_kv_a_norm_g,
                                l0_w_kv_b, l0_w_out),
        lambda h: mixer_gla_nsa(h, l1_w_in, l1_w_gk2, l1_b_gk, l1_gla_norm_g, l1_cmp_pos,
                                l1_cmp_k_w1, l1_cmp_k_w2, l1_cmp_v_w1, l1_cmp_v_w2, l1_w_out),
    )
    for layer in range(DEPTH):
        shift_m, scale_m, gate_m, shift_f, scale_f, gate_f = adaln_modulation(c, *ada[layer])
        pre_m, post_m, pre_f, post_f = norms[layer]
        h = rmsnorm(x, pre_m) * (1.0 + scale_m) + shift_m
        x = x + gate_m * rmsnorm(mixers[layer](h), post_m)
        h = rmsnorm(x, pre_f) * (1.0 + scale_f) + shift_f
        x = x + gate_f * rmsnorm(swiglu(h, *ffns[layer]), post_f)
    return x
```

```python
import math
import numpy as np
import ml_dtypes
import concourse.bass as bass
import concourse.mybir as mybir
from concourse.bass_utils import run_bass_kernel_spmd
from contextlib import ExitStack

F32 = mybir.dt.float32
BF16 = mybir.dt.bfloat16
I32 = mybir.dt.int32
AF = mybir.ActivationFunctionType
ALU = mybir.AluOpType
AX = mybir.AxisListType


class Prog:
    ENG = ('pe', 'dve', 'act', 'pool', 'sp')

    def __init__(self, nc, stack, n_dma_sems=24):
        self.nc = nc
        self.stack = stack
        self.q = {e: [] for e in self.ENG}
        self.semobj = {}
        for e in self.ENG:
            self.semobj[e] = stack.enter_context(nc.semaphore('s_' + e))
        for i in range(n_dma_sems):
            self.semobj[('dma', i)] = stack.enter_context(nc.semaphore('s_dma%d' % i))
        self.cnt = {e: 0 for e in self.ENG}
        self.dma_cnt = [0] * n_dma_sems
        self.dma_rr = 0
        self.seen = {e: {} for e in self.ENG}
        self.last_w = {}
        self.readers = {}
        self.n_ins = 0

    def sb(self, name, shape, dt):
        return self.stack.enter_context(self.nc.sbuf_tensor(name, list(shape), dt))

    def arena_init(self, nbytes):
        self.arena = self.stack.enter_context(self.nc.sbuf_tensor("arena", [128, nbytes // 2], BF16))
        self.arena_off = 0

    def arena_reset(self):
        self.arena_off = 0

    def ar(self, name, shape, dt):
        n = 1
        for d in shape[1:]:
            n *= d
        size = {F32: 4, BF16: 2, I32: 4}[dt]
        nb = (n * size + 3) // 4 * 4
        off = self.arena_off
        self.arena_off += nb
        assert self.arena_off <= self.arena.shape[1] * 2, ("arena overflow", name, self.arena_off)
        ap = self.arena[:, off // 2:(off + n * size) // 2]
        if dt != BF16:
            ap = ap.bitcast(dt)
        if len(shape) == 3:
            ap = ap.rearrange("p (a b) -> p a b", a=shape[1])
        elif len(shape) == 4:
            ap = ap.rearrange("p (a b c) -> p a b c", a=shape[1], b=shape[2])
        return ap

    def ps(self, name, shape, dt=F32):
        return self.stack.enter_context(self.nc.psum_tensor(name, list(shape), dt))

    def emit(self, eng, fn, reads=(), writes=(), dma=False):
        waits = {}

        def need(t):
            if t is not None:
                if eng == 'pe' and t[0] == 'pe':
                    return
                if waits.get(t[0], 0) < t[1]:
                    waits[t[0]] = t[1]

        for k in reads:
            need(self.last_w.get(k))
        for k in writes:
            need(self.last_w.get(k))
            for s, v in self.readers.get(k, {}).items():
                need((s, v))
        if dma:
            nsw = 8
            nhw = len(self.dma_cnt) - nsw
            if eng == 'pool':
                self.dma_rr_sw = (getattr(self, 'dma_rr_sw', -1) + 1) % nsw
                i = nhw + self.dma_rr_sw
            else:
                i = self.dma_rr
                self.dma_rr = (self.dma_rr + 1) % nhw
            prev = self.dma_cnt[i]
            if prev > 0:
                need((('dma', i), prev))
            self.dma_cnt[i] += 16
            ticket = (('dma', i), self.dma_cnt[i])
            inc = 16
        else:
            self.cnt[eng] += 1
            ticket = (eng, self.cnt[eng])
            inc = 1
        wl = []
        seen = self.seen[eng]
        for s, v in waits.items():
            if seen.get(s, 0) < v:
                seen[s] = v
                wl.append((s, v))
        self.q[eng].append((wl, fn, ticket[0], inc))
        self.n_ins += 1 + len(wl)
        for k in writes:
            self.last_w[k] = ticket
            self.readers[k] = {}
        for k in reads:
            if k in writes:
                continue
            r = self.readers.setdefault(k, {})
            if r.get(ticket[0], 0) < ticket[1]:
                r[ticket[0]] = ticket[1]
        return ticket

    def dma(self, out, in_, reads=(), writes=(), eng='sp', **kw):
        eng = 'sp'
        return self.emit(eng, lambda e: e.dma_start(out=out, in_=in_, **kw), reads, writes, dma=True)

    def mm(self, out, lhsT, rhs, start=True, stop=True, reads=(), writes=(), **kw):
        kw.setdefault("skip_group_check", True)
        return self.emit("pe", lambda e: e.matmul(out, lhsT, rhs, start=start, stop=stop, **kw), reads, writes)

    def tr(self, out, in_, ident, reads=(), writes=()):
        return self.emit('pe', lambda e: e.transpose(out, in_, ident), reads, writes)

    def finish(self):
        fin = []
        for e in self.ENG:
            if self.cnt[e] > 0:
                fin.append((e, self.cnt[e]))
        for i, c in enumerate(self.dma_cnt):
            if c > 0:
                fin.append((('dma', i), c))
        self.q['sp'].append((fin, None, None, 0))
        nc = self.nc
        semobj = self.semobj
        q = self.q
        waited = {e: set() for e in self.ENG}
        for name in self.ENG:
            for wl, fn, s, inc in q[name]:
                for ws, wv in wl:
                    if ws in waited:
                        waited[ws].add(wv)
        rank = {}
        for e in self.ENG:
            rank[e] = {v: i + 1 for i, v in enumerate(sorted(waited[e]))}

        def replay(name, eng):
            idx = 0
            for wl, fn, s, inc in q[name]:
                for ws, wv in wl:
                    if ws in rank:
                        eng.wait_ge(semobj[ws], rank[ws][wv])
                    else:
                        eng.wait_ge(semobj[ws], wv)
                if fn is not None:
                    ins = fn(eng)
                    if inc == 16:
                        ins.then_inc(semobj[s], 16)
                    else:
                        idx += 1
                        if idx in rank[name]:
                            ins.then_inc(semobj[s], 1)

        with nc.Block() as block:
            @block.tensor
            def _(eng):
                replay('pe', eng)

            @block.vector
            def _(eng):
                replay('dve', eng)

            @block.scalar
            def _(eng):
                replay('act', eng)

            @block.gpsimd
            def _(eng):
                replay('pool', eng)

            @block.sync
            def _(eng):
                replay('sp', eng)


EPS = 1e-6


def consts(P):
    nc = P.nc
    C = {}
    ident = P.sb("ident", [128, 128], BF16)
    P.emit('pool', lambda e: e.memset(ident[:], 0.0), writes=['ident'])
    P.emit('pool', lambda e: e.affine_select(ident[:], ident[:], [[-1, 128]], ALU.not_equal, 1.0, base=0, channel_multiplier=1), reads=['ident'], writes=['ident'])
    C['ident'] = ident
    tri = P.sb("tri", [128, 128], F32)
    P.emit('pool', lambda e: e.memset(tri[:], 1.0), writes=['tri'])
    P.emit('pool', lambda e: e.affine_select(tri[:], tri[:], [[1, 128]], ALU.is_ge, 0.0, base=0, channel_multiplier=-1), reads=['tri'], writes=['tri'])
    C['tri'] = tri
    strict = P.sb("strict", [128, 128], F32)
    P.emit('pool', lambda e: e.memset(strict[:], 1.0), writes=['strict'])
    P.emit('pool', lambda e: e.affine_select(strict[:], strict[:], [[-1, 128]], ALU.is_gt, 0.0, base=0, channel_multiplier=1), reads=['strict'], writes=['strict'])
    C['strict'] = strict
    ones = P.sb("ones", [128, 128], F32)
    P.emit('pool', lambda e: e.memset(ones[:], 1.0), writes=['ones'])
    C['ones'] = ones
    trib = P.sb("trib", [128, 128], BF16)
    P.emit('dve', lambda e: e.tensor_copy(trib[:], tri[:]), reads=['tri'], writes=['trib'])
    C['trib'] = trib
    oneb = P.sb("oneb", [128, 128], BF16)
    P.emit('pool', lambda e: e.memset(oneb[:], 1.0), writes=['oneb'])
    C['oneb'] = oneb
    return C


def ada_mod(P, cT, ada_w, ada_b, ncols, tag, pm, pmk, cb=512, mod=None):
    nc = P.nc
    if mod is None:
        mod = P.sb(tag + "mod", [128, ncols], F32)
    P.dma(mod[:], ada_b.partition_broadcast(128), writes=[tag + 'mod'])
    csb = P.sb(tag + "c", [128, 8], F32)
    P.dma(csb[:], cT, writes=[tag + 'c'])
    P.emit('act', lambda e: e.activation(csb[:], csb[:], AF.Silu), reads=[tag + 'c'], writes=[tag + 'c'])
    scb = P.sb(tag + "scb", [128, 8, 128], BF16)
    for c in range(8):
        P.emit('dve', lambda e, c=c: e.tensor_copy(scb[:, c, :], csb[:, c:c + 1].to_broadcast([128, 128])), reads=[tag + 'c'], writes=[tag + 'scb'])
    wst = P.stg[:, 0:8 * cb].rearrange("p (c n) -> p c n", c=8)
    wbf = P.sb(tag + "wbf", [128, 8, cb], BF16)
    awv = ada_w.rearrange("(p c) n -> p c n", c=8)
    for j in range(ncols // cb):
        P.dma(wst, awv[:, :, j * cb:(j + 1) * cb], writes=['stg'])
        P.emit('pool', lambda e: e.tensor_copy(wbf[:], wst), reads=['stg'], writes=[tag + 'wbf'])
        for c in range(8):
            P.mm(pm[:, 0:cb], scb[:, c, :], wbf[:, c, :], start=(c == 0), stop=(c == 7), reads=[tag + 'scb', tag + 'wbf'], writes=[pmk])
        P.emit('dve', lambda e, j=j: e.tensor_tensor(mod[:, j * cb:(j + 1) * cb], mod[:, j * cb:(j + 1) * cb], pm[:, 0:cb], ALU.add), reads=[pmk, tag + 'mod'], writes=[tag + 'mod'])
    return mod


def load_w(P, dst, dst_key, src_ap, shape, tag, cast_eng='pool'):
    n = 1
    for d in shape[1:]:
        n *= d
    st = P.stg[:, 0:n]
    if len(shape) == 3:
        st = st.rearrange("p (c n) -> p c n", c=shape[1])
    P.dma(st, src_ap, writes=['stg'])
    P.emit(cast_eng, lambda e: e.tensor_copy(dst, st), reads=['stg'], writes=[dst_key])


class PreStage:
    def __init__(self, P, C, G1, SH, xdram):
        self.P, self.C, self.G1, self.SH, self.x = P, C, G1, SH, xdram
        self.xt = [P.sb("pre_x%d" % i, [128, 1024], F32) for i in range(2)]
        self.junk = P.sb("pre_junk", [128, 1024], BF16)
        self.tmp = P.sb("pre_tmp", [128, 1024], F32)
        self.hb = [P.sb("pre_hb%d" % i, [128, 1024], BF16) for i in range(2)]
        self.ss = P.sb("pre_ss", [128, 8], F32)
        self.pT = P.ps("pre_pT", [128, 8, 128], BF16)
        self.n = 0

    def tile(self, t0, hT_dst, hT_key):
        P = self.P
        i = self.n % 2
        self.n += 1
        xt, hb = self.xt[i], self.hb[i]
        kx, kh = 'pre_x%d' % i, 'pre_hb%d' % i
        ss = self.ss
        P.dma(xt[:], self.x[t0:t0 + 128, :], writes=[kx])
        P.emit('act', lambda e: e.activation(self.junk[:], xt[:], AF.Square, accum_out=ss[:, 0:1]), reads=[kx], writes=['pre_junk', 'pre_ss'])
        P.emit('act', lambda e: e.activation(ss[:, 1:2], ss[:, 0:1], AF.Ln, bias=EPS, scale=1.0 / 1024), reads=['pre_ss'], writes=['pre_ss'])
        P.emit('act', lambda e: e.activation(ss[:, 2:3], ss[:, 1:2], AF.Exp, scale=-0.5), reads=['pre_ss'], writes=['pre_ss'])
        P.emit('dve', lambda e: e.scalar_tensor_tensor(self.tmp[:], xt[:], ss[:, 2:3], self.G1, ALU.mult, ALU.mult), reads=[kx, 'pre_ss', 'G1'], writes=['pre_tmp'])
        P.emit('dve', lambda e: e.tensor_tensor(hb[:], self.tmp[:], self.SH, ALU.add), reads=['pre_tmp', 'SH'], writes=[kh])
        for c in range(8):
            P.tr(self.pT[:, c, :], hb[:, c * 128:(c + 1) * 128], self.C['ident'][:], reads=[kh, 'ident'], writes=['pre_pT'])
        P.emit('act', lambda e: e.copy(hT_dst, self.pT[:]), reads=['pre_pT'], writes=[hT_key])


def build_mix0(T, do_ssd=True, do_mla=True):
    nc = bass.Bass("TRN2", target_bir_lowering=False)
    NBLK = T // 512

    def din(name, shape, dt=F32):
        return nc.dram_tensor(name, list(shape), dt, kind="ExternalInput").ap()

    x = din("x", [T, 1024])
    cT = din("cT", [128, 8])
    ada_w = din("ada_w", [1024, 2048])
    ada_b = din("ada_b", [2048])
    pre_g = din("pre_g", [1024])
    w_ssd = din("w_ssd", [1024, 768])
    w_dt = din("w_dt", [1024, 4])
    conv_w = din("conv_w", [128, 4, 4])
    conv_b = din("conv_b", [128, 4])
    dt_bias = din("dt_bias", [4])
    a_log = din("a_log", [4])
    d_skip = din("d_skip", [4])
    if do_ssd:
        yg = nc.dram_tensor("yg", [T, 256], F32, kind="ExternalOutput").ap()
    w_qa = din("w_qa", [1024, 384])
    w_kva = din("w_kva", [1024, 256])
    w_kpe = din("w_kpe", [1024, 32])
    w_qb = din("w_qb", [384, 192])
    w_kvb = din("w_kvb", [256, 256])
    gq = din("gq", [128, 3])
    gkv = din("gkv", [128, 2])
    rope_inv = din("rope_inv", [128, 1])
    pos = din("pos", [T], I32)
    if do_mla:
        omla = nc.dram_tensor("omla", [T, 128], F32, kind="ExternalOutput").ap()

    with ExitStack() as st:
        P = Prog(nc, st)
        C = consts(P)
        P.stg = P.sb('stg', [128, 6144], F32)
        p_in = [P.ps("p_in%d" % i, [128, 512], F32) for i in range(2)]
        mod = ada_mod(P, cT, ada_w, ada_b, 2048, "ada", p_in[0], "p_in0")
        g_bc = P.sb("g_bc", [128, 1024], F32)
        P.dma(g_bc[:], pre_g.partition_broadcast(128), writes=['g_bc'])
        G1 = P.sb("G1", [128, 1024], F32)
        P.emit('dve', lambda e: e.scalar_tensor_tensor(G1[:], mod[:, 1024:2048], 1.0, g_bc[:], ALU.add, ALU.mult), reads=['adamod', 'g_bc'], writes=['G1'])
        SH = mod[:, 0:1024]
        P.last_w['SH'] = P.last_w['adamod']

        if do_ssd:
            wS = P.sb("wS", [128, 8, 768], BF16)
            load_w(P, wS[:], 'wS', w_ssd.rearrange("(c p) n -> p c n", p=128), [128, 8, 768], "wS")
            wD = P.sb("wD", [128, 8, 4], BF16)
            load_w(P, wD[:], 'wD', w_dt.rearrange("(c p) n -> p c n", p=128), [128, 8, 4], "wD")
            cw = P.sb("cw", [128, 4, 4], F32)
            P.dma(cw[:], conv_w, writes=['cw'])
            cb = P.sb("cb", [128, 4], F32)
            P.dma(cb[:], conv_b, writes=['cb'])
            hp = P.sb("hp", [128, 12], F32)
            P.dma(hp[:, 0:4], dt_bias.partition_broadcast(128), writes=['hp'])
            P.dma(hp[:, 4:8], a_log.partition_broadcast(128), writes=['hp'])
            P.dma(hp[:, 8:12], d_skip.partition_broadcast(128), writes=['hp'])
            P.emit('act', lambda e: e.activation(hp[:, 4:8], hp[:, 4:8], AF.Exp), reads=['hp'], writes=['hp'])
            P.emit('dve', lambda e: e.tensor_scalar(hp[:, 4:8], hp[:, 4:8], -1.0, None, ALU.mult), reads=['hp'], writes=['hp'])

        pre = PreStage(P, C, G1[:], SH, x)
        hT = [P.sb("hT%d" % i, [128, 8, 512], BF16) for i in range(2)]
        if do_ssd:
            xcv = P.sb("xcv", [128, 4, 515], F32)
            P.emit('pool', lambda e: e.memset(xcv[:], 0.0), writes=['xcv'])
            acc = P.sb("acc", [128, 4, 512], F32)
            fT = P.sb("fT", [128, 4, 512], BF16)
            state = P.sb("state", [128, 256], F32)
            state_b = P.sb("state_b", [128, 256], BF16)
            P.emit('pool', lambda e: e.memset(state[:], 0.0), writes=['state'])
            P.emit('pool', lambda e: e.memset(state_b[:], 0.0), writes=['state_b'])

        p_sm = P.ps("p_sm", [128, 512], F32)
        p_d = P.ps("p_d", [128, 512], F32)
        p_y = P.ps("p_y", [128, 512], F32)
        p_o = P.ps("p_o", [128, 4, 65], F32)
        p_g = P.ps("p_g", [128, 512], F32)

        if do_ssd:
            zt = P.sb("zt", [128, 256], F32)
            dtt = P.sb("dtt", [128, 24], F32)
            etot = P.sb("etot", [128, 4], F32)
            ncs = P.sb("ncs", [128, 4], F32)
            xtok = P.sb("xtok", [128, 256], BF16)
            btok = P.sb("btok", [128, 128], BF16)
            xd = P.sb("xd", [128, 256], BF16)
            xdd = P.sb("xdd", [128, 256], BF16)
            lhb = P.sb("lhb", [128, 4, 2, 128], BF16)
            adh = P.sb("adh", [128, 8], BF16)
            Gm = P.sb("Gm", [128, 128], F32)
            El = P.sb("El", [128, 4, 128], F32)
            Mh = P.sb("Mh", [128, 4, 128], BF16)
            ycomb = P.sb("ycomb", [128, 256], F32)
            yout = [P.sb("yout%d" % i, [128, 256], F32) for i in range(2)]
            mask_sl = P.sb("mask_sl", [128, 128], F32)

        if do_mla:
            NT = T // 128
            KT = P.sb("KT", [128, 2, T], BF16)
            Vaug = P.sb("Vaug", [128, NT, 2, 65], BF16)
            P.emit('pool', lambda e: e.memset(Vaug[:], 1.0), writes=['Vaug'])
            QT = P.sb("QT", [128, 2, 512], BF16)
            qaT = P.sb("qaT", [128, 3, 512], BF16)
            sqq = P.sb("sqq", [128, 3, 512], BF16)
            kvaT = P.sb("kvaT", [128, 2, 512], BF16)
            sqk = P.sb("sqk", [128, 2, 512], BF16)
            rq = P.sb("rq", [128, 512], F32)
            rk = P.sb("rk", [128, 512], F32)
            cosF = P.sb("cosF", [128, 512], F32)
            sinF = P.sb("sinF", [128, 512], F32)
            cq = P.sb("cq", [128, 512], F32)
            sq_ = P.sb("sq_", [128, 512], F32)
            posi = P.sb("posi", [128, 512], I32)
            ang = P.sb("ang", [128, 512], F32)
            ang2 = P.sb("ang2", [128, 512], F32)
            rki = P.sb("rki", [128, 512], I32)
            rkf = P.sb("rkf", [128, 512], F32)
            rr = P.sb("rr", [128, 512], F32)
            rt1 = P.sb("rt1", [128, 512], F32)
            rt2 = P.sb("rt2", [128, 512], F32)
            rkt = P.sb("rkt", [128, 2], F32)
            PT = [P.sb("PT%d" % i, [128, 512], BF16) for i in range(2)]
            orec = P.sb("orec", [128, 4, 1], F32)
            oout = [P.sb("oout%d" % i, [128, 4, 64], F32) for i in range(2)]
            ones_bf = P.sb("ones_bf", [128, 128], BF16)
            P.emit('pool', lambda e: e.memset(ones_bf[:], 1.0), writes=['ones_bf'])
            negm = P.sb("negm", [128, 128], BF16)
            P.emit('pool', lambda e: e.memset(negm[:], 0.0), writes=['negm'])
            P.emit('pool', lambda e: e.affine_select(negm[:], negm[:], [[1, 128]], ALU.is_ge, -30000.0, base=0, channel_multiplier=-1), reads=['negm'], writes=['negm'])
            rinv = P.sb("rinv", [128, 1], F32)
            P.dma(rinv[:], rope_inv, writes=['rinv'])
            gqs = P.sb("gqs", [128, 3], F32)
            P.dma(gqs[:], gq, writes=['gqs'])
            gks = P.sb("gks", [128, 2], F32)
            P.dma(gks[:], gkv, writes=['gks'])
            wQA = P.sb("wQA", [128, 8, 384], BF16)
            load_w(P, wQA[:], 'wQA', w_qa.rearrange("(c p) n -> p c n", p=128), [128, 8, 384], "wQA")
            wKVA = P.sb("wKVA", [128, 8, 256], BF16)
            load_w(P, wKVA[:], 'wKVA', w_kva.rearrange("(c p) n -> p c n", p=128), [128, 8, 256], "wKVA")
            kpe_st = P.sb("kpe_st", [128, 8, 32], F32)
            P.dma(kpe_st[:], w_kpe.rearrange("(c p) n -> p c n", p=128), writes=['kpe_st'])
            wKPE = P.sb("wKPE", [128, 8, 96], BF16)
            wKPEr = P.sb("wKPEr", [128, 8, 96], BF16)
            P.emit('pool', lambda e: e.memset(wKPE[:], 0.0), writes=['wKPE'])
            P.emit('pool', lambda e: e.memset(wKPEr[:], 0.0), writes=['wKPEr'])
            P.emit('dve', lambda e: e.tensor_copy(wKPE[:, :, 64:96], kpe_st[:]), reads=['kpe_st', 'wKPE'], writes=['wKPE'])
            P.emit('dve', lambda e: e.tensor_scalar(wKPEr[:, :, 64:80], kpe_st[:, :, 16:32], -1.0, None, ALU.mult), reads=['kpe_st', 'wKPEr'], writes=['wKPEr'])
            P.emit('dve', lambda e: e.tensor_copy(wKPEr[:, :, 80:96], kpe_st[:, :, 0:16]), reads=['kpe_st', 'wKPEr'], writes=['wKPEr'])
            qb_st = P.sb("qb_st", [128, 3, 192], F32)
            P.dma(qb_st[:], w_qb.rearrange("(c p) n -> p c n", p=128), writes=['qb_st'])
            for c in range(3):
                P.emit('dve', lambda e, c=c: e.tensor_scalar(qb_st[:, c, :], qb_st[:, c, :], gqs[:, c:c + 1], None, ALU.mult), reads=['qb_st', 'gqs'], writes=['qb_st'])
            wQ = P.sb("wQ", [128, 2, 3, 96], BF16)
            wQr = P.sb("wQr", [128, 2, 3, 96], BF16)
            P.emit('pool', lambda e: e.memset(wQr[:], 0.0), writes=['wQr'])
            for hh in range(2):
                P.emit('dve', lambda e, hh=hh: e.tensor_copy(wQ[:, hh, :, :], qb_st[:, :, hh * 96:(hh + 1) * 96]), reads=['qb_st'], writes=['wQ'])
                P.emit('dve', lambda e, hh=hh: e.tensor_scalar(wQr[:, hh, :, 64:80], qb_st[:, :, hh * 96 + 80:hh * 96 + 96], -1.0, None, ALU.mult), reads=['qb_st', 'wQr'], writes=['wQr'])
                P.emit('dve', lambda e, hh=hh: e.tensor_copy(wQr[:, hh, :, 80:96], qb_st[:, :, hh * 96 + 64:hh * 96 + 80]), reads=['qb_st', 'wQr'], writes=['wQr'])
            kvb_st = P.sb("kvb_st", [128, 2, 256], F32)
            P.dma(kvb_st[:], w_kvb.rearrange("(c p) n -> p c n", p=128), writes=['kvb_st'])
            for c in range(2):
                P.emit('dve', lambda e, c=c: e.tensor_scalar(kvb_st[:, c, :], kvb_st[:, c, :], gks[:, c:c + 1], None, ALU.mult), reads=['kvb_st', 'gks'], writes=['kvb_st'])
            wKN = P.sb("wKN", [128, 2, 2, 64], BF16)
            wV = P.sb("wV", [128, 2, 2, 64], BF16)
            for hh in range(2):
                P.emit('dve', lambda e, hh=hh: e.tensor_copy(wKN[:, hh, :, :], kvb_st[:, :, hh * 128:hh * 128 + 64]), reads=['kvb_st'], writes=['wKN'])
                P.emit('dve', lambda e, hh=hh: e.tensor_copy(wV[:, :, hh, :], kvb_st[:, :, hh * 128 + 64:hh * 128 + 128]), reads=['kvb_st'], writes=['wV'])
            SC = 96.0 ** -0.5
            natt = [0]

            def rope_reduce(src, dst, key):
                R = slice(64, 96)
                P.emit('dve', lambda e: e.tensor_scalar(rki[R, :], src[R, :], 1.0 / (2 * math.pi), None, ALU.mult), reads=[key], writes=['rki'])
                P.emit('dve', lambda e: e.tensor_copy(rkf[R, :], rki[R, :]), reads=['rki'], writes=['rkf'])
                P.emit('dve', lambda e: e.scalar_tensor_tensor(rr[R, :], rkf[R, :], -2 * math.pi, src[R, :], ALU.mult, ALU.add), reads=['rkf', key], writes=['rr'])
                P.emit('dve', lambda e: e.tensor_scalar(rr[R, :], rr[R, :], -math.pi, math.pi, ALU.max, ALU.min), reads=['rr'], writes=['rr'])
                P.emit('act', lambda e: e.activation(dst[R, :], rr[R, :], AF.Sin), reads=['rr'], writes=[dst.tensor.name if False else key + '_out'])

            def mla_block(blk, hTb, kh):
                R = slice(64, 96)
                cs_ = slice(blk * 512, (blk + 1) * 512)
                for (wt, wk, nchunk, dstT, dsq, kd, ksq) in ((wQA, 'wQA', 3, qaT, sqq, 'qaT', 'sqq'), (wKVA, 'wKVA', 2, kvaT, sqk, 'kvaT', 'sqk')):
                    for j in range(nchunk):
                        pi = p_in[j % 2]
                        kp = 'p_in%d' % (j % 2)
                        for c in range(8):
                            P.mm(pi[:], wt[:, c, j * 128:(j + 1) * 128], hTb[:, c, :], start=(c == 0), stop=(c == 7), reads=[wk, kh], writes=[kp])
                        P.emit('act', lambda e, pi=pi, j=j, dstT=dstT: e.copy(dstT[:, j, :], pi[:]), reads=[kp], writes=[kd])
                        P.emit('act', lambda e, pi=pi, j=j, dsq=dsq: e.activation(dsq[:, j, :], pi[:], AF.Square), reads=[kp], writes=[ksq])
                for (dsq, ksq, nchunk, dim, rdst, krd) in ((sqq, 'sqq', 3, 384, rq, 'rq'), (sqk, 'sqk', 2, 256, rk, 'rk')):
                    for j in range(nchunk):
                        P.mm(p_sm[:], ones_bf[:], dsq[:, j, :], start=(j == 0), stop=(j == nchunk - 1), reads=['ones_bf', ksq], writes=['p_sm'])
                    P.emit('act', lambda e, rdst=rdst, dim=dim: e.activation(rdst[:], p_sm[:], AF.Ln, bias=EPS, scale=1.0 / dim), reads=['p_sm'], writes=[krd])
                    P.emit('act', lambda e, rdst=rdst: e.activation(rdst[:], rdst[:], AF.Exp, scale=-0.5), reads=[krd], writes=[krd])
                P.dma(posi[R, :], pos[blk * 512:(blk + 1) * 512].partition_broadcast(32), writes=['posi'])
                P.emit('dve', lambda e: e.tensor_copy(ang[R, :], posi[R, :]), reads=['posi'], writes=['ang'])
                P.emit('dve', lambda e: e.tensor_scalar(ang[R, :], ang[R, :], rinv[R, 0:1], None, ALU.mult), reads=['ang', 'rinv'], writes=['ang'])
                P.emit('dve', lambda e: e.tensor_scalar(ang2[R, :], ang[R, :], math.pi / 2, None, ALU.add), reads=['ang'], writes=['ang2'])
                rope_reduce(ang, sinF, 'ang')
                rope_reduce(ang2, cosF, 'ang2')
                P.emit('dve', lambda e: e.tensor_tensor(cq[R, :], rq[R, :], cosF[R, :], ALU.mult), reads=['rq', 'ang2_out'], writes=['cq'])
                P.emit('dve', lambda e: e.tensor_tensor(sq_[R, :], rq[R, :], sinF[R, :], ALU.mult), reads=['rq', 'ang_out'], writes=['sq_'])
                for c in range(8):
                    P.mm(p_y[0:96, :], wKPE[:, c, :], hTb[:, c, :], start=(c == 0), stop=(c == 7), reads=['wKPE', kh], writes=['p_y'])
                for c in range(8):
                    P.mm(p_d[0:96, :], wKPEr[:, c, :], hTb[:, c, :], start=(c == 0), stop=(c == 7), reads=['wKPEr', kh], writes=['p_d'])
                P.emit('dve', lambda e: e.tensor_tensor(rt1[R, :], p_y[R, :], cosF[R, :], ALU.mult), reads=['p_y', 'ang2_out'], writes=['rt1'])
                P.emit('dve', lambda e: e.tensor_tensor(rt2[R, :], p_d[R, :], sinF[R, :], ALU.mult), reads=['p_d', 'ang_out'], writes=['rt2'])
                P.emit('dve', lambda e: e.tensor_tensor(KT[R, 0, cs_], rt1[R, :], rt2[R, :], ALU.add), reads=['rt1', 'rt2'], writes=['KT'])
                P.emit('dve', lambda e: e.tensor_copy(KT[R, 1, cs_], KT[R, 0, cs_]), reads=['KT'], writes=['KT'])
                for hh in range(2):
                    pi = p_in[hh]
                    kp = 'p_in%d' % hh
                    for c in range(2):
                        P.mm(pi[0:64, :], wKN[:, hh, c, :], kvaT[:, c, :], start=(c == 0), stop=(c == 1), reads=['wKN', 'kvaT'], writes=[kp])
                    P.emit('dve', lambda e, pi=pi, hh=hh: e.tensor_tensor(KT[0:64, hh, cs_], pi[0:64, :], rk[0:64, :], ALU.mult), reads=[kp, 'rk'], writes=['KT'])
                for hh in range(2):
                    for c in range(3):
                        P.mm(p_in[0][0:96, :], wQ[:, hh, c, :], qaT[:, c, :], start=(c == 0), stop=(c == 2), reads=['wQ', 'qaT'], writes=['p_in0'])
                    for c in range(3):
                        P.mm(p_in[1][0:96, :], wQr[:, hh, c, :], qaT[:, c, :], start=(c == 0), stop=(c == 2), reads=['wQr', 'qaT'], writes=['p_in1'])
                    P.emit('dve', lambda e, hh=hh: e.tensor_tensor(QT[0:64, hh, :], p_in[0][0:64, :], rq[0:64, :], ALU.mult), reads=['p_in0', 'rq'], writes=['QT'])
                    P.emit('dve', lambda e: e.tensor_tensor(rt1[R, :], p_in[0][R, :], cq[R, :], ALU.mult), reads=['p_in0', 'cq'], writes=['rt1'])
                    P.emit('dve', lambda e: e.tensor_tensor(rt2[R, :], p_in[1][R, :], sq_[R, :], ALU.mult), reads=['p_in1', 'sq_'], writes=['rt2'])
                    P.emit('dve', lambda e, hh=hh: e.tensor_tensor(QT[R, hh, :], rt1[R, :], rt2[R, :], ALU.add), reads=['rt1', 'rt2'], writes=['QT'])
                for ti in range(4):
                    tg = blk * 4 + ti
                    ts = slice(ti * 128, (ti + 1) * 128)
                    for c in range(2):
                        P.mm(p_g[:, 0:128], kvaT[:, c, ts], wV[:, c, :, :].rearrange("p h d -> p (h d)"), start=(c == 0), stop=(c == 1), reads=['kvaT', 'wV'], writes=['p_g'])
                    for c in range(2):
                        P.mm(p_g[:, 128:129], sqk[:, c, ts], ones_bf[:, 0:1], start=(c == 0), stop=(c == 1), reads=['sqk', 'ones_bf'], writes=['p_g'])
                    P.emit('act', lambda e: e.activation(rkt[:, 0:1], p_g[:, 128:129], AF.Ln, bias=EPS, scale=1.0 / 256), reads=['p_g'], writes=['rkt'])
                    P.emit('act', lambda e: e.activation(rkt[:, 1:2], rkt[:, 0:1], AF.Exp, scale=-0.5), reads=['rkt'], writes=['rkt'])
                    P.emit('dve', lambda e, tg=tg: e.tensor_scalar(Vaug[:, tg, :, 0:64], p_g[:, 0:128].rearrange("p (h d) -> p h d", h=2), rkt[:, 1:2], None, ALU.mult), reads=['p_g', 'rkt'], writes=['Vaug'])
                for hh in range(2):
                    nj = 4 * blk + 4
                    for j in range(nj):
                        jj = j - 4 * blk
                        q0 = max(jj, 0) * 128
                        ps = (p_d, p_y)[natt[0] % 2]
                        kps = ('p_d', 'p_y')[natt[0] % 2]
                        pt = PT[natt[0] % 2]
                        kpt = 'PT%d' % (natt[0] % 2)
                        natt[0] += 1
                        kslice = slice(j * 128, (j + 1) * 128)
                        if jj >= 0:
                            P.mm(ps[:, q0:q0 + 128], C['ident'][:], negm[:], start=True, stop=False, reads=['ident', 'negm'], writes=[kps])
                            P.mm(ps[:, q0:q0 + 128], KT[0:96, hh, kslice], QT[0:96, hh, q0:q0 + 128], start=False, stop=True, reads=['KT', 'QT'], writes=[kps])
                            if q0 + 128 < 512:
                                P.mm(ps[:, q0 + 128:512], KT[0:96, hh, kslice], QT[0:96, hh, q0 + 128:512], start=True, stop=True, reads=['KT', 'QT'], writes=[kps])
                        else:
                            P.mm(ps[:, 0:512], KT[0:96, hh, kslice], QT[0:96, hh, 0:512], start=True, stop=True, reads=['KT', 'QT'], writes=[kps])
                        P.emit('act', lambda e, ps=ps, pt=pt, q0=q0: e.activation(pt[:, q0:512], ps[:, q0:512], AF.Exp, scale=SC), reads=[kps], writes=[kpt])
                        for qi in range(max(jj, 0), 4):
                            P.mm(p_o[:, qi, :], pt[:, qi * 128:(qi + 1) * 128], Vaug[:, j, hh, :], start=(j == 0 and qi == 0), stop=(j == 4 * blk + qi), reads=[kpt, 'Vaug'], writes=['p_o'])
                    oo = oout[hh]
                    ko = 'oout%d' % hh
                    P.emit('dve', lambda e: e.reciprocal(orec[:], p_o[:, :, 64:65]), reads=['p_o'], writes=['orec'])
                    P.emit('dve', lambda e, oo=oo: e.tensor_tensor(oo[:], p_o[:, :, 0:64], orec[:].to_broadcast([128, 4, 64]), ALU.mult), reads=['p_o', 'orec'], writes=[ko])
                    P.dma(omla[blk * 512:(blk + 1) * 512, hh * 64:(hh + 1) * 64].rearrange("(q p) d -> p q d", p=128), oo[:], reads=[ko], eng='pool')

        nt = 0
        for blk in range(NBLK):
            hTb = hT[blk % 2]
            kh = 'hT%d' % (blk % 2)
            for ti in range(4):
                pre.tile(blk * 512 + ti * 128, hTb[:, :, ti * 128:(ti + 1) * 128], kh)
            if do_ssd:
                for j in range(4):
                    pi = p_in[j % 2]
                    kp = 'p_in%d' % (j % 2)
                    for c in range(8):
                        P.mm(pi[:], wS[:, c, 256 + j * 128:256 + (j + 1) * 128], hTb[:, c, :], start=(c == 0), stop=(c == 7), reads=['wS', kh], writes=[kp])
                    P.emit('act', lambda e, j=j, pi=pi: e.copy(xcv[:, j, 3:515], pi[:]), reads=[kp], writes=['xcv'])
                for j in range(4):
                    P.emit('dve', lambda e, j=j: e.tensor_scalar(acc[:, j, :], xcv[:, j, 0:512], cw[:, j, 0:1], cb[:, j:j + 1], ALU.mult, ALU.add), reads=['xcv', 'cw', 'cb'], writes=[('acc', j)])
                    for k in range(1, 4):
                        P.emit('dve', lambda e, j=j, k=k: e.scalar_tensor_tensor(acc[:, j, :], xcv[:, j, k:k + 512], cw[:, j, k:k + 1], acc[:, j, :], ALU.mult, ALU.add), reads=['xcv', 'cw', ('acc', j)], writes=[('acc', j)])
                P.emit('act', lambda e: e.activation(fT[:], acc[:], AF.Silu), reads=[('acc', j) for j in range(4)], writes=['fT'])
                P.emit('pool', lambda e: e.tensor_copy(xcv[:, :, 0:3], xcv[:, :, 512:515]), reads=['xcv'] + [('acc', j) for j in range(4)], writes=['xcv'])
                for ti in range(4):
                    t0 = blk * 512 + ti * 128
                    ts = slice(ti * 128, (ti + 1) * 128)
                    for c in range(8):
                        P.mm(p_sm[:, 0:256], hTb[:, c, ts], wS[:, c, 0:256], start=(c == 0), stop=(c == 7), reads=[kh, 'wS'], writes=['p_sm'])
                    for c in range(8):
                        P.mm(p_sm[:, 256:260], hTb[:, c, ts], wD[:, c, :], start=(c == 0), stop=(c == 7), reads=[kh, 'wD'], writes=['p_sm'])
                    P.emit('act', lambda e: e.activation(zt[:], p_sm[:, 0:256], AF.Silu), reads=['p_sm'], writes=['zt'])
                    P.emit('dve', lambda e: e.tensor_tensor(dtt[:, 0:4], p_sm[:, 256:260], hp[:, 0:4], ALU.add), reads=['p_sm', 'hp'], writes=['dtt'])
                    P.emit('act', lambda e: e.activation(dtt[:, 0:4], dtt[:, 0:4], AF.Exp), reads=['dtt'], writes=['dtt'])
                    P.emit('act', lambda e: e.activation(dtt[:, 0:4], dtt[:, 0:4], AF.Ln, bias=1.0), reads=['dtt'], writes=['dtt'])
                    P.emit('dve', lambda e: e.tensor_tensor(dtt[:, 4:8], dtt[:, 0:4], hp[:, 4:8], ALU.mult), reads=['dtt', 'hp'], writes=['dtt'])
                    P.emit('dve', lambda e: e.tensor_copy(adh[:, 0:4], dtt[:, 4:8]), reads=['dtt'], writes=['adh'])
                    P.emit('dve', lambda e: e.tensor_tensor(adh[:, 4:8], dtt[:, 4:8], adh[:, 0:4], ALU.subtract), reads=['dtt', 'adh'], writes=['adh'])
                    P.mm(p_sm[:, 264:268], C['trib'][:], adh[:, 0:4], start=True, stop=False, reads=['trib', 'adh'], writes=['p_sm2'])
                    P.mm(p_sm[:, 264:268], C['trib'][:], adh[:, 4:8], start=False, stop=True, reads=['trib', 'adh'], writes=['p_sm2'])
                    P.mm(p_sm[:, 268:272], C['oneb'][:], adh[:, 0:4], start=True, stop=False, reads=['oneb', 'adh'], writes=['p_sm2'])
                    P.mm(p_sm[:, 268:272], C['oneb'][:], adh[:, 4:8], start=False, stop=True, reads=['oneb', 'adh'], writes=['p_sm2'])
                    P.emit('dve', lambda e: e.tensor_copy(dtt[:, 8:16], p_sm[:, 264:272]), reads=['p_sm2', 'dtt'], writes=['dtt'])
                    P.emit('act', lambda e: e.activation(dtt[:, 16:20], dtt[:, 8:12], AF.Exp), reads=['dtt'], writes=['dtt2'])
                    P.emit('dve', lambda e: e.tensor_tensor(dtt[:, 20:24], dtt[:, 12:16], dtt[:, 8:12], ALU.subtract), reads=['dtt'], writes=['dtt3'])
                    P.emit('act', lambda e: e.activation(dtt[:, 20:24], dtt[:, 20:24], AF.Exp), reads=['dtt3'], writes=['dtt3'])
                    P.emit('act', lambda e: e.activation(etot[:], dtt[:, 12:16], AF.Exp), reads=['dtt'], writes=['etot'])
                    for j in range(2):
                        P.tr(pre.pT[:, j, :], fT[:, j, ts], C['ident'][:], reads=['fT', 'ident'], writes=['pre_pT'])
                    P.tr(pre.pT[:, 2, :], fT[:, 2, ts], C['ident'][:], reads=['fT', 'ident'], writes=['pre_pT'])
                    P.emit('dve', lambda e: e.tensor_copy(xtok[:], pre.pT[:, 0:2, :].rearrange('p a b -> p (a b)')), reads=['pre_pT'], writes=['xtok'])
                    P.emit('act', lambda e: e.copy(btok[:], pre.pT[:, 2, :]), reads=['pre_pT'], writes=['btok'])
                    P.emit('dve', lambda e: e.tensor_tensor(xd[:].rearrange("p (h d) -> p h d", h=4), xtok[:].rearrange("p (h d) -> p h d", h=4), dtt[:, 0:4].unsqueeze(2).to_broadcast([128, 4, 64]), ALU.mult), reads=['xtok', 'dtt'], writes=['xd'])
                    P.emit('dve', lambda e: e.tensor_tensor(xdd[:].rearrange("p (h d) -> p h d", h=4), xd[:].rearrange("p (h d) -> p h d", h=4), dtt[:, 20:24].unsqueeze(2).to_broadcast([128, 4, 64]), ALU.mult), reads=['xd', 'dtt3'], writes=['xdd'])
                    P.mm(p_g[:, 0:128], fT[:, 2, ts], fT[:, 3, ts], reads=['fT'], writes=['p_g'])
                    P.emit('dve', lambda e: e.tensor_tensor(Gm[:], p_g[:, 0:128], C['tri'][:], ALU.mult), reads=['p_g', 'tri'], writes=['Gm'])
                    for h in range(4):
                        P.emit('pool', lambda e, h=h: e.tensor_scalar(lhb[:, h, 0, :], C['strict'][:], adh[:, h:h + 1], None, ALU.mult), reads=['strict', 'adh'], writes=[('lh', h)])
                        P.emit('pool', lambda e, h=h: e.tensor_scalar(lhb[:, h, 1, :], C['strict'][:], adh[:, 4 + h:5 + h], None, ALU.mult), reads=['strict', 'adh'], writes=[('lh', h)])
                        P.mm(p_d[:, h * 128:(h + 1) * 128], lhb[:, h, 0, :], C['trib'][:], start=True, stop=False, reads=[('lh', h), 'trib'], writes=['p_d'])
                        P.mm(p_d[:, h * 128:(h + 1) * 128], lhb[:, h, 1, :], C['trib'][:], start=False, stop=True, reads=[('lh', h), 'trib'], writes=['p_d'])
                    P.emit('act', lambda e: e.activation(El[:].rearrange("p h l -> p (h l)"), p_d[:], AF.Exp), reads=['p_d'], writes=['El'])
                    P.emit('dve', lambda e: e.tensor_tensor(Mh[:], El[:], Gm[:].unsqueeze(1).to_broadcast([128, 4, 128]), ALU.mult), reads=['El', 'Gm'], writes=['Mh'])
                    for h in range(4):
                        P.mm(p_y[:, h * 64:(h + 1) * 64], Mh[:, h, :], xd[:, h * 64:(h + 1) * 64], reads=['Mh', 'xd'], writes=['p_y'])
                    P.mm(p_y[:, 256:512], fT[:, 3, ts], state_b[:], reads=['fT', 'state_b'], writes=['p_y'])
                    P.emit('dve', lambda e: e.tensor_tensor(ycomb[:].rearrange("p (h d) -> p h d", h=4), p_y[:, 256:512].rearrange("p (h d) -> p h d", h=4), dtt[:, 16:20].unsqueeze(2).to_broadcast([128, 4, 64]), ALU.mult), reads=['p_y', 'dtt2'], writes=['ycomb'])
                    P.emit('dve', lambda e: e.tensor_tensor(ycomb[:], ycomb[:], p_y[:, 0:256], ALU.add), reads=['p_y', 'ycomb'], writes=['ycomb'])
                    yo = yout[nt % 2]
                    ky = 'yout%d' % (nt % 2)
                    P.emit('dve', lambda e, yo=yo: e.tensor_tensor(yo[:].rearrange("p (h d) -> p h d", h=4), xtok[:].rearrange("p (h d) -> p h d", h=4), hp[:, 8:12].unsqueeze(2).to_broadcast([128, 4, 64]), ALU.mult), reads=['xtok', 'hp'], writes=[ky])
                    P.emit('dve', lambda e, yo=yo: e.tensor_tensor(yo[:], yo[:], ycomb[:], ALU.add), reads=['ycomb', ky], writes=[ky])
                    P.emit('dve', lambda e, yo=yo: e.tensor_tensor(yo[:], yo[:], zt[:], ALU.mult), reads=['zt', ky], writes=[ky])
                    P.dma(yg[t0:t0 + 128, :], yo[:], reads=[ky], eng='pool')
                    P.mm(p_g[:, 256:512], btok[:], xdd[:], reads=['btok', 'xdd'], writes=['p_g2'])
                    P.emit('dve', lambda e: e.tensor_tensor(state[:].rearrange("p (h d) -> p h d", h=4), state[:].rearrange("p (h d) -> p h d", h=4), etot[:].unsqueeze(2).to_broadcast([128, 4, 64]), ALU.mult), reads=['state', 'etot'], writes=['state'])
                    P.emit('dve', lambda e: e.tensor_tensor(state[:], state[:], p_g[:, 256:512], ALU.add), reads=['state', 'p_g2'], writes=['state'])
                    P.emit('act', lambda e: e.copy(state_b[:], state[:]), reads=['state'], writes=['state_b'])
                    nt += 1
            if do_mla:
                mla_block(blk, hTb, kh)
        P.finish()
        print("mix0 instructions:", P.n_ins)
    return nc


NTOK = 2048
FH = 2816
NHC = 22


def barrier(P):
    fin = []
    for e in P.ENG:
        if P.cnt[e] > 0:
            fin.append((e, P.cnt[e]))
    for i, c in enumerate(P.dma_cnt):
        if c > 0:
            fin.append((('dma', i), c))
    for e in P.ENG:
        wl = []
        for s, v in fin:
            if P.seen[e].get(s, 0) < v:
                P.seen[e][s] = v
                wl.append((s, v))
        P.q[e].append((wl, None, None, 0))


def build_post(ssd_norm, ntok=NTOK):
    nc = bass.Bass("TRN2", target_bir_lowering=False)

    def din(name, shape, dt=F32):
        return nc.dram_tensor(name, list(shape), dt, kind="ExternalInput").ap()

    x = din("x", [ntok, 1024])
    cat = din("cat", [ntok, 1536])
    cT = din("cT", [128, 8])
    ada_w = din("ada_w", [1024, 4096])
    ada_b = din("ada_b", [4096])
    post_g = din("post_g", [1024])
    fpre_g = din("fpre_g", [1024])
    fpost_g = din("fpost_g", [1024])
    ssm_g = din("ssm_g", [1024])
    w_out = din("w_out", [1536, 1024])
    w_gate = din("w_gate", [1024, FH])
    w_up = din("w_up", [1024, FH])
    w_down = din("w_down", [FH, 1024])
    xo = nc.dram_tensor("xo", [ntok, 1024], F32, kind="ExternalOutput").ap()
    x1d = nc.dram_tensor("x1d", [ntok, 1024], F32).ap()
    hd = nc.dram_tensor("hd", [ntok, 1024], BF16).ap()
    NT = ntok // 128

    with ExitStack() as st:
        P = Prog(nc, st)
        C = consts(P)
        P.stg = P.sb('stg', [128, 2048], F32)
        p_a = [P.ps("p_a%d" % i, [128, 512], F32) for i in range(2)]
        pT = P.ps("pT", [128, 8, 128], BF16)
        p_g = [P.ps("p_g%d" % i, [128, 512], F32) for i in range(2)]
        p_u = [P.ps("p_u%d" % i, [128, 512], F32) for i in range(2)]
        mod = P.sb("adamod", [128, 4096], F32)
        wG = P.sb("wG", [128, 8, FH], BF16)
        wU = P.sb("wU", [128, 8, FH], BF16)
        wD = P.sb("wD", [128, NHC, 1024], BF16)
        P.arena_init(49184 + 64)
        if True:
            P.sb_save = P.sb
            P.sb = P.ar
            ada_mod(P, cT, ada_w, ada_b, 4096, "ada", p_a[0], "p_a0", cb=256, mod=mod)
            gb = P.sb("gb", [128, 1024], F32)
            P.dma(gb[:], post_g.partition_broadcast(128), writes=['gb'])
            P.emit('dve', lambda e: e.tensor_tensor(mod[:, 0:1024], mod[:, 0:1024], gb[:], ALU.mult), reads=['adamod', 'gb'], writes=['adamod'])
            P.dma(gb[:], fpre_g.partition_broadcast(128), writes=['gb'])
            P.emit('dve', lambda e: e.scalar_tensor_tensor(mod[:, 2048:3072], mod[:, 2048:3072], 1.0, gb[:], ALU.add, ALU.mult), reads=['adamod', 'gb'], writes=['adamod'])
            P.dma(gb[:], fpost_g.partition_broadcast(128), writes=['gb'])
            P.emit('dve', lambda e: e.tensor_tensor(mod[:, 3072:4096], mod[:, 3072:4096], gb[:], ALU.mult), reads=['adamod', 'gb'], writes=['adamod'])
            barrier(P)
            P.arena_reset()
        GP, SH2, G2, GF = mod[:, 0:1024], mod[:, 1024:2048], mod[:, 2048:3072], mod[:, 3072:4096]


        def load_ffn():
            ncast = 0
            for (wt, key, src) in ((wG, 'wG', w_gate), (wU, 'wU', w_up)):
                sv = src.rearrange("(c p) n -> p c n", p=128)
                for c in range(8):
                    for n0 in range(0, FH, 2048):
                        n1 = min(FH, n0 + 2048)
                        stv = P.stg[:, 0:n1 - n0]
                        P.dma(stv, sv[:, c, n0:n1], writes=['stg'])
                        P.emit('pool', lambda e, wt=wt, c=c, n0=n0, n1=n1, stv=stv: e.tensor_copy(wt[:, c, n0:n1], stv), reads=['stg'], writes=[key])

        def load_wd():
            sv = w_down.rearrange("(c p) n -> p c n", p=128)
            for c in range(0, NHC, 2):
                stv = P.stg[:, 0:2048].rearrange("p (c n) -> p c n", c=2)
                P.dma(stv, sv[:, c:c + 2, :], writes=['stg'])
                P.emit('pool', lambda e, c=c, stv=stv: e.tensor_copy(wD[:, c:c + 2, :], stv), reads=['stg'], writes=['wD'])

        if True:
            wO = wD[:, 0:12, :]
            if ssd_norm:
                gbs = P.sb("gb2", [128, 1024], F32)
                P.dma(gbs[:], ssm_g.partition_broadcast(128), writes=['gbs'])
            scr = wD[:, 12:22, :].rearrange("p a b -> p (a b)").bitcast(F32)
            sv = w_out.rearrange("(c p) n -> p c n", p=128)
            for c in range(0, 12, 2):
                stv = P.stg[:, 0:2048].rearrange("p (c n) -> p c n", c=2)
                P.dma(stv, sv[:, c:c + 2, :], writes=['stg'])
                P.emit('pool', lambda e, c=c, stv=stv: e.tensor_copy(wD[:, c:c + 2, :], stv), reads=['stg'], writes=['wO'])
            load_ffn()
            catt = [scr[:, 0:1536], scr[:, 1536:3072]]
            xt = [scr[:, 3072:4096]] * 2
            catb = P.sb("catb", [128, 1536], BF16)
            catT = P.sb("catT", [128, 12, 128], BF16)
            junk = P.sb("junk", [128, 1024], BF16)
            ss = P.sb("ss", [128, 16], F32)
            x1 = [scr[:, 4096:5120]] * 2
            tmp = P.sb("tmp", [128, 1024], F32)
            hb = [P.sb("hb0", [128, 1024], BF16)] * 2
            for t in range(NT):
                i = t % 2
                ct, kc = catt[i], 'catt%d' % i
                xx, kx = xt[i], 'xt0'
                P.dma(ct, cat[t * 128:(t + 1) * 128, :], writes=[kc])
                P.dma(xx, x[t * 128:(t + 1) * 128, :], writes=[kx])
                if ssd_norm:
                    for g in range(2):
                        P.emit('act', lambda e, g=g, ct=ct: e.activation(junk[:, 0:512], ct[:, g * 512:(g + 1) * 512], AF.Square, accum_out=ss[:, g:g + 1]), reads=[kc], writes=['junk', 'ss'])
                    P.emit('act', lambda e: e.activation(ss[:, 2:4], ss[:, 0:2], AF.Ln, bias=EPS, scale=1.0 / 512), reads=['ss'], writes=['ss'])
                    P.emit('act', lambda e: e.activation(ss[:, 2:4], ss[:, 2:4], AF.Exp, scale=-0.5), reads=['ss'], writes=['ss'])
                    for g in range(2):
                        P.emit('dve', lambda e, g=g, ct=ct: e.scalar_tensor_tensor(catb[:, g * 512:(g + 1) * 512], ct[:, g * 512:(g + 1) * 512], ss[:, 2 + g:3 + g], gbs[:, g * 512:(g + 1) * 512], ALU.mult, ALU.mult), reads=[kc, 'ss', 'gbs'], writes=['catb'])
                    P.emit('dve', lambda e, ct=ct: e.tensor_copy(catb[:, 1024:1536], ct[:, 1024:1536]), reads=[kc], writes=['catb'])
                else:
                    P.emit('dve', lambda e, ct=ct: e.tensor_copy(catb[:], ct), reads=[kc], writes=['catb'])
                for c0 in (0, 8):
                    n = min(8, 12 - c0)
                    for c in range(n):
                        P.tr(pT[:, c, :], catb[:, (c0 + c) * 128:(c0 + c + 1) * 128], C['ident'][:], reads=['catb', 'ident'], writes=['pT'])
                    P.emit('act', lambda e, c0=c0, n=n: e.copy(catT[:, c0:c0 + n, :], pT[:, 0:n, :]), reads=['pT'], writes=['catT'])
                for nb in range(2):
                    for c in range(12):
                        P.mm(p_a[nb][:], catT[:, c, :], wD[:, c, nb * 512:(nb + 1) * 512], start=(c == 0), stop=(c == 11), reads=['catT', 'wO'], writes=['p_a%d' % nb])
                for nb in range(2):
                    P.emit('act', lambda e, nb=nb: e.activation(junk[:, nb * 512:(nb + 1) * 512], p_a[nb][:], AF.Square, accum_out=ss[:, 4 + nb:5 + nb]), reads=['p_a%d' % nb], writes=['junk', 'ss'])
                P.emit('dve', lambda e: e.tensor_tensor(ss[:, 6:7], ss[:, 4:5], ss[:, 5:6], ALU.add), reads=['ss'], writes=['ss'])
                P.emit('act', lambda e: e.activation(ss[:, 7:8], ss[:, 6:7], AF.Ln, bias=EPS, scale=1.0 / 1024), reads=['ss'], writes=['ss'])
                P.emit('act', lambda e: e.activation(ss[:, 7:8], ss[:, 7:8], AF.Exp, scale=-0.5), reads=['ss'], writes=['ss'])
                x1t, k1 = x1[i], 'x1_0'
                for nb in range(2):
                    cs = slice(nb * 512, (nb + 1) * 512)
                    P.emit('dve', lambda e, nb=nb, cs=cs: e.scalar_tensor_tensor(tmp[:, cs], p_a[nb][:], ss[:, 7:8], GP[:, cs], ALU.mult, ALU.mult), reads=['p_a%d' % nb, 'ss', 'adamod'], writes=['tmp'])
                P.emit('dve', lambda e, x1t=x1t, xx=xx: e.tensor_tensor(x1t, tmp[:], xx, ALU.add), reads=['tmp', kx], writes=[k1])
                P.dma(x1d[t * 128:(t + 1) * 128, :], x1t, reads=[k1], writes=['x1d'], eng='pool')
                P.emit('act', lambda e, x1t=x1t: e.activation(junk[:], x1t, AF.Square, accum_out=ss[:, 8:9]), reads=[k1], writes=['junk', 'ss'])
                P.emit('act', lambda e: e.activation(ss[:, 9:10], ss[:, 8:9], AF.Ln, bias=EPS, scale=1.0 / 1024), reads=['ss'], writes=['ss'])
                P.emit('act', lambda e: e.activation(ss[:, 9:10], ss[:, 9:10], AF.Exp, scale=-0.5), reads=['ss'], writes=['ss'])
                P.emit('dve', lambda e, x1t=x1t: e.scalar_tensor_tensor(tmp[:], x1t, ss[:, 9:10], G2, ALU.mult, ALU.mult), reads=[k1, 'ss', 'adamod'], writes=['tmp'])
                hbt, khb = hb[i], 'hb0'
                P.emit('dve', lambda e, hbt=hbt: e.tensor_tensor(hbt[:], tmp[:], SH2, ALU.add), reads=['tmp', 'adamod'], writes=[khb])
                P.dma(hd[t * 128:(t + 1) * 128, :], hbt[:], reads=[khb], writes=['hd'], eng='pool')
            barrier(P)
            P.arena_reset()
        load_wd()
        hbl = [P.sb("hbl0", [128, 1024], BF16)] * 2
        hT = P.sb("hT", [128, 8, 512], BF16)
        actT = P.sb("actT", [128, NHC, 512], BF16)
        sg = [P.sb("sg0", [128, 512], F32)] * 2
        x1l = [P.sb("x1l0", [128, 1024], F32)] * 2
        ot = [P.sb("ot0", [128, 1024], F32)] * 2
        junk2 = P.sb("junk2", [128, 1024], BF16)
        tmp2 = P.sb("tmp2", [128, 1024], F32)
        ss2 = P.sb("ss2", [128, 8], F32)
        n2 = 0
        BS = min(512, ntok)
        NTI = BS // 128
        for blk in range(ntok // BS):
            for ti in range(NTI):
                t = blk * NTI + ti
                hl, kl = hbl[t % 2], 'hbl0'
                P.dma(hl[:], hd[t * 128:(t + 1) * 128, :], reads=['hd'], writes=[kl])
                for c in range(8):
                    P.tr(pT[:, c, :], hl[:, c * 128:(c + 1) * 128], C['ident'][:], reads=[kl, 'ident'], writes=['pT'])
                P.emit('act', lambda e, ti=ti: e.copy(hT[:, :, ti * 128:(ti + 1) * 128], pT[:]), reads=['pT'], writes=['hT'])
            for hc in range(NHC):
                i = hc % 2
                hs = slice(hc * 128, (hc + 1) * 128)
                for c in range(8):
                    P.mm(p_g[i][:, 0:BS], wG[:, c, hs], hT[:, c, 0:BS], start=(c == 0), stop=(c == 7), reads=['wG', 'hT'], writes=['p_g%d' % i])
                for c in range(8):
                    P.mm(p_u[i][:, 0:BS], wU[:, c, hs], hT[:, c, 0:BS], start=(c == 0), stop=(c == 7), reads=['wU', 'hT'], writes=['p_u%d' % i])
                P.emit('act', lambda e, i=i: e.activation(sg[i][:, 0:BS], p_g[i][:, 0:BS], AF.Silu), reads=['p_g%d' % i], writes=['sg0'])
                P.emit('dve', lambda e, i=i, hc=hc: e.tensor_tensor(actT[:, hc, 0:BS], sg[i][:, 0:BS], p_u[i][:, 0:BS], ALU.mult), reads=['sg0', 'p_u%d' % i], writes=[('actT', hc)])
            for ti in range(NTI):
                t = blk * NTI + ti
                i = t % 2
                ts = slice(ti * 128, (ti + 1) * 128)
                xl, kxl = x1l[i], 'x1l0'
                P.dma(xl[:], x1d[t * 128:(t + 1) * 128, :], reads=['x1d'], writes=[kxl])
                for nb in range(2):
                    for hc in range(NHC):
                        P.mm(p_a[nb][:], actT[:, hc, ts], wD[:, hc, nb * 512:(nb + 1) * 512], start=(hc == 0), stop=(hc == NHC - 1), reads=[('actT', hc), 'wD'], writes=['p_a%d' % nb])
                for nb in range(2):
                    P.emit('act', lambda e, nb=nb: e.activation(junk2[:, nb * 512:(nb + 1) * 512], p_a[nb][:], AF.Square, accum_out=ss2[:, nb:nb + 1]), reads=['p_a%d' % nb], writes=['junk2', 'ss2'])
                P.emit('dve', lambda e: e.tensor_tensor(ss2[:, 2:3], ss2[:, 0:1], ss2[:, 1:2], ALU.add), reads=['ss2'], writes=['ss2'])
                P.emit('act', lambda e: e.activation(ss2[:, 3:4], ss2[:, 2:3], AF.Ln, bias=EPS, scale=1.0 / 1024), reads=['ss2'], writes=['ss2'])
                P.emit('act', lambda e: e.activation(ss2[:, 3:4], ss2[:, 3:4], AF.Exp, scale=-0.5), reads=['ss2'], writes=['ss2'])
                for nb in range(2):
                    cs = slice(nb * 512, (nb + 1) * 512)
                    P.emit('dve', lambda e, nb=nb, cs=cs: e.scalar_tensor_tensor(tmp2[:, cs], p_a[nb][:], ss2[:, 3:4], GF[:, cs], ALU.mult, ALU.mult), reads=['p_a%d' % nb, 'ss2', 'adamod'], writes=['tmp2'])
                oo, ko = ot[i], 'ot0'
                P.emit('dve', lambda e, oo=oo, xl=xl: e.tensor_tensor(oo[:], tmp2[:], xl[:], ALU.add), reads=['tmp2', kxl], writes=[ko])
                P.dma(xo[t * 128:(t + 1) * 128, :], oo[:], reads=[ko], eng='pool')
        P.finish()
        print("post instructions:", P.n_ins)
    return nc


NEG = -30000.0


STOP = 99
NS = 99


def build_mix1(T, do_gla=True, do_nsa=True):
    nc = bass.Bass("TRN2", target_bir_lowering=False)
    NBLK = T // 512
    NT = T // 128

    def din(name, shape, dt=F32):
        return nc.dram_tensor(name, list(shape), dt, kind="ExternalInput").ap()

    x = din("x", [T, 1024])
    cT = din("cT", [128, 8])
    ada_w = din("ada_w", [1024, 2048])
    ada_b = din("ada_b", [2048])
    pre_g = din("pre_g", [1024])
    w_gla = din("w_gla", [1024, 768])
    w_glr = din("w_glr", [1024, 16])
    w_gk2a = din("w_gk2a", [32, 128])
    gla_g = din("gla_g", [256])
    ogla = nc.dram_tensor("ogla", [T, 256], F32, kind="ExternalOutput").ap()
    if do_nsa:
        w_nsa = din("w_nsa", [1024, 896])
        w_ng = din("w_ng", [1024, 6])
        cmp_w1 = din("cmp_w1", [2, 2048, 256])
        cmp_w2k = din("cmp_w2k", [256, 128])
        cmp_w2v = din("cmp_w2v", [256, 64])
        cmp_posf = din("cmp_posf", [128, 16])
        ts_tab = din("ts_tab", [2, 64 * 4], BF16)
        al2 = din("al2", [2, 256], BF16)
        onsa = nc.dram_tensor("onsa", [T, 128], F32, kind="ExternalOutput").ap()


    with ExitStack() as st:
        P = Prog(nc, st)
        C = consts(P)
        P.stg = P.sb('stg', [128, 2048], F32)

        def load_wk(dst, key, src, ncols):
            sv = src.rearrange("(c p) n -> p c n", p=128)
            for c in range(8):
                stv = P.stg[:, 0:ncols]
                P.dma(stv, sv[:, c, :], writes=['stg'])
                P.emit('pool', lambda e, c=c, stv=stv: e.tensor_copy(dst[:, c, :], stv), reads=['stg'], writes=[key])
        bk_in = P.ps("bk_in", [128, 512], F32)
        bk_s0 = P.ps("bk_s0", [128, 512], F32)
        bk_s1 = P.ps("bk_s1", [128, 512], F32)
        bk_oc = P.ps("bk_oc", [128, 4, 256], F32)
        bk_os = P.ps("bk_os", [128, 512], F32)
        bk_ow = P.ps("bk_ow", [128, 512], F32)
        mod = ada_mod(P, cT, ada_w, ada_b, 2048, "ada", bk_in, "bk_in", cb=256)
        g_bc = P.sb("g_bc", [128, 1024], F32)
        P.dma(g_bc[:], pre_g.partition_broadcast(128), writes=['g_bc'])
        G1 = P.sb("G1", [128, 1024], F32)
        P.emit('dve', lambda e: e.scalar_tensor_tensor(G1[:], mod[:, 1024:2048], 1.0, g_bc[:], ALU.add, ALU.mult), reads=['adamod', 'g_bc'], writes=['G1'])
        SH = mod[:, 0:1024]
        P.last_w['SH'] = P.last_w['adamod']
        pre = PreStage(P, C, G1[:], SH, x)
        hT = [P.sb("hT%d" % i, [128, 8, 512], BF16) for i in range(2)]
        ident = C['ident']
        tri = C['tri']

        wGL = P.sb("wGL", [128, 8, 768], BF16)
        load_wk(wGL, 'wGL', w_gla, 768)
        wGR = P.sb("wGR", [128, 8, 16], BF16)
        load_wk(wGR, 'wGR', w_glr, 16)
        wk2 = P.sb("wk2", [32, 128], F32)
        P.dma(wk2[:], w_gk2a, writes=['wk2'])
        wk2b = P.sb("wk2b", [32, 128], BF16)
        P.emit('dve', lambda e: e.tensor_copy(wk2b[:], wk2[:]), reads=['wk2'], writes=['wk2b'])
        l1h = P.sb("l1h", [128, 2, 128], BF16)
        glra = P.sb("glra", [32, 512], BF16)
        P.emit('pool', lambda e: e.memset(glra[:], 1.0), writes=['glra'])
        gg_bc = P.sb("gg_bc", [128, 256], F32)
        P.dma(gg_bc[:], gla_g.partition_broadcast(128), writes=['gg_bc'])
        qT = P.sb("qT", [128, 512], F32)
        kT = P.sb("kT", [128, 512], F32)
        e1 = P.sb("e1", [128, 128], F32)
        l1 = P.sb("l1", [128, 128], F32)
        Eneg = P.sb("Eneg", [128, 128], F32)
        Epos = P.sb("Epos", [128, 128], F32)
        qtl = P.sb("qtl", [128, 128], BF16)
        ktl = P.sb("ktl", [128, 128], BF16)
        ktok = P.sb("ktok", [128, 128], BF16)
        AT = P.sb("AT", [128, 128], BF16)
        vtok = P.sb("vtok", [128, 256], BF16)
        sgg = P.sb("sgg", [128, 256], F32)
        Sst = P.sb("Sst", [128, 256], F32)
        Sbf = P.sb("Sbf", [128, 256], BF16)
        P.emit('pool', lambda e: e.memset(Sst[:], 0.0), writes=['Sst'])
        P.emit('pool', lambda e: e.memset(Sbf[:], 0.0), writes=['Sbf'])
        gss = P.sb("gss", [128, 4], F32)
        gjunk = P.sb("gjunk", [128, 256], BF16)
        gt1 = P.sb("gt1", [128, 256], F32)
        gout = [P.sb("gout%d" % i, [128, 256], F32) for i in range(2)]
        ngl = [0]

        def gla_block(blk, hTb, kh):
            for (j, dst, kd) in ((0, qT, 'qT'), (1, kT, 'kT')):
                for c in range(8):
                    P.mm(bk_in[:], wGL[:, c, j * 128:(j + 1) * 128], hTb[:, c, :], start=(c == 0), stop=(c == 7), reads=['wGL', kh], writes=['bk_in'])
                P.emit('act', lambda e, dst=dst: e.copy(dst[:], bk_in[:]), reads=['bk_in'], writes=[kd])
            for c in range(8):
                P.mm(bk_in[0:16, :], wGR[:, c, :], hTb[:, c, :], start=(c == 0), stop=(c == 7), reads=['wGR', kh], writes=['bk_in'])
            P.emit('act', lambda e: e.copy(glra[0:16, :], bk_in[0:16, :]), reads=['bk_in'], writes=['glra'])
            if STOP < 1:
                return
            for ti in range(4):
                t0 = blk * 512 + ti * 128
                ts = slice(ti * 128, (ti + 1) * 128)
                for c in range(8):
                    P.mm(bk_in[:], hTb[:, c, ts], wGL[:, c, 256:768], start=(c == 0), stop=(c == 7), reads=[kh, 'wGL'], writes=['bk_in'])
                P.emit('act', lambda e: e.copy(vtok[:], bk_in[:, 0:256]), reads=['bk_in'], writes=['vtok'])
                P.emit('act', lambda e: e.activation(sgg[:], bk_in[:, 256:512], AF.Silu), reads=['bk_in'], writes=['sgg'])
                if STOP < 2:
                    continue
                P.mm(bk_s0[:, 0:128], glra[:, ts], wk2b[:], reads=['glra', 'wk2b'], writes=['bk_s0'])
                P.emit('act', lambda e: e.activation(e1[:], bk_s0[:, 0:128], AF.Exp, scale=-1.0), reads=['bk_s0'], writes=['e1'])
                P.emit('act', lambda e: e.activation(l1[:], e1[:], AF.Ln, bias=1.0), reads=['e1'], writes=['l1'])
                if STOP < 3:
                    continue
                P.emit('dve', lambda e: e.tensor_copy(l1h[:, 0, :], l1[:]), reads=['l1'], writes=['l1h'])
                P.emit('dve', lambda e: e.tensor_tensor(l1h[:, 1, :], l1[:], l1h[:, 0, :], ALU.subtract), reads=['l1', 'l1h'], writes=['l1h'])
                P.mm(bk_s0[:, 128:256], l1h[:, 0, :], C['trib'][:], start=True, stop=False, reads=['l1h', 'trib'], writes=['bk_s0'])
                P.mm(bk_s0[:, 128:256], l1h[:, 1, :], C['trib'][:], start=False, stop=True, reads=['l1h', 'trib'], writes=['bk_s0'])
                P.emit('act', lambda e: e.activation(Eneg[:], bk_s0[:, 128:256], AF.Exp, scale=-1.0 / 16), reads=['bk_s0'], writes=['Eneg'])
                P.emit('act', lambda e: e.activation(Epos[:], bk_s0[:, 128:256], AF.Exp, scale=1.0 / 16), reads=['bk_s0'], writes=['Epos'])
                P.emit('dve', lambda e, ts=ts: e.scalar_tensor_tensor(qtl[:], qT[:, ts], 128.0 ** -0.5, Eneg[:], ALU.mult, ALU.mult), reads=['qT', 'Eneg'], writes=['qtl'])
                P.emit('dve', lambda e, ts=ts: e.tensor_tensor(ktl[:], kT[:, ts], Epos[:], ALU.mult), reads=['kT', 'Epos'], writes=['ktl'])
                if STOP < 4:
                    continue
                P.tr(pre.pT[:, 0, :], ktl[:], ident[:], reads=['ktl', 'ident'], writes=['pre_pT'])
                P.emit('dve', lambda e: e.tensor_copy(ktok[:], pre.pT[:, 0, :]), reads=['pre_pT'], writes=['ktok'])
                P.mm(bk_s1[:, 0:128], ktl[:], qtl[:], reads=['ktl', 'qtl'], writes=['bk_s1'])
                P.emit('dve', lambda e: e.tensor_tensor(AT[:], bk_s1[:, 0:128], tri[:], ALU.mult), reads=['bk_s1', 'tri'], writes=['AT'])
                if STOP < 5:
                    continue
                P.mm(bk_os[:, 0:256], AT[:], vtok[:], start=True, stop=False, reads=['AT', 'vtok'], writes=['bk_os'])
                P.mm(bk_os[:, 0:256], qtl[:], Sbf[:], start=False, stop=True, reads=['qtl', 'Sbf'], writes=['bk_os'])
                P.mm(bk_ow[:, 0:256], ktok[:], vtok[:], reads=['ktok', 'vtok'], writes=['bk_ow'])
                P.emit('dve', lambda e: e.tensor_tensor(Sst[:], Sst[:], bk_ow[:, 0:256], ALU.add), reads=['Sst', 'bk_ow'], writes=['Sst'])
                P.emit('dve', lambda e: e.tensor_scalar(Sst[:], Sst[:], Eneg[:, 127:128], None, ALU.mult), reads=['Sst', 'Eneg'], writes=['Sst'])
                P.emit('act', lambda e: e.copy(Sbf[:], Sst[:]), reads=['Sst'], writes=['Sbf'])
                if STOP < 6:
                    continue
                P.emit('act', lambda e: e.activation(gjunk[:], bk_os[:, 0:256], AF.Square, accum_out=gss[:, 0:1]), reads=['bk_os'], writes=['gjunk', 'gss'])
                P.emit('act', lambda e: e.activation(gss[:, 1:2], gss[:, 0:1], AF.Ln, bias=EPS, scale=1.0 / 256), reads=['gss'], writes=['gss'])
                P.emit('act', lambda e: e.activation(gss[:, 1:2], gss[:, 1:2], AF.Exp, scale=-0.5), reads=['gss'], writes=['gss'])
                P.emit('dve', lambda e: e.scalar_tensor_tensor(gt1[:], bk_os[:, 0:256], gss[:, 1:2], gg_bc[:], ALU.mult, ALU.mult), reads=['bk_os', 'gss', 'gg_bc'], writes=['gt1'])
                go = gout[ngl[0] % 2]
                kgo = 'gout%d' % (ngl[0] % 2)
                ngl[0] += 1
                P.emit('dve', lambda e, go=go: e.tensor_tensor(go[:], gt1[:], sgg[:], ALU.mult), reads=['gt1', 'sgg'], writes=[kgo])
                P.dma(ogla[t0:t0 + 128, :], go[:], reads=[kgo], eng='pool')


        if do_nsa:
            wN = P.sb("wN", [128, 8, 896], BF16)
            load_wk(wN, 'wN', w_nsa, 896)
            wNG = P.sb("wNG", [128, 8, 6], BF16)
            load_wk(wNG, 'wNG', w_ng, 6)
            W1 = P.sb("W1", [128, 2, 16, 256], BF16)
            for kind in range(2):
                for j0 in range(0, 16, 8):
                    stv = P.stg[:, 0:2048].rearrange("p (j n) -> p j n", j=8)
                    P.dma(stv, cmp_w1[kind, j0 * 128:(j0 + 8) * 128, :].rearrange("(j p) n -> p j n", p=128), writes=['stg'])
                    P.emit('pool', lambda e, kind=kind, j0=j0, stv=stv: e.tensor_copy(W1[:, kind, j0:j0 + 8, :], stv), reads=['stg'], writes=['W1'])
            W2k = P.sb("W2k", [128, 2, 128], BF16)
            stv = P.stg[:, 0:256].rearrange("p (c n) -> p c n", c=2)
            P.dma(stv, cmp_w2k.rearrange("(c p) n -> p c n", p=128), writes=['stg'])
            P.emit('pool', lambda e, stv=stv: e.tensor_copy(W2k[:], stv), reads=['stg'], writes=['W2k'])
            W2v = P.sb("W2v", [128, 2, 64], BF16)
            stv2 = P.stg[:, 0:128].rearrange("p (c n) -> p c n", c=2)
            P.dma(stv2, cmp_w2v.rearrange("(c p) n -> p c n", p=128), writes=['stg'])
            P.emit('pool', lambda e, stv2=stv2: e.tensor_copy(W2v[:], stv2), reads=['stg'], writes=['W2v'])
            posf = P.sb("posf", [128, 16], F32)
            P.dma(posf[:], cmp_posf, writes=['posf'])
            posb = P.sb("posb", [128, 16], BF16)
            P.emit('dve', lambda e: e.tensor_copy(posb[:], posf[:]), reads=['posf'], writes=['posb'])
            posw = P.sb("posw", [128, 2, 2], F32)
            for kind in range(2):
                for hc in range(2):
                    for j in range(16):
                        P.mm(bk_in[:, 0:1], W1[:, kind, j, hc * 128:(hc + 1) * 128], posb[:, j:j + 1], start=(j == 0), stop=(j == 15), reads=['W1', 'posb'], writes=['bk_in'])
                    P.emit('dve', lambda e, kind=kind, hc=hc: e.tensor_copy(posw[:, kind, hc:hc + 1], bk_in[:, 0:1]), reads=['bk_in'], writes=['posw'])
            TS = P.sb("TS", [2, 64, 4], BF16)
            P.dma(TS[:].rearrange("p a b -> p (a b)"), ts_tab, writes=['TS'])
            AL2 = P.sb("AL2", [2, 256], BF16)
            P.dma(AL2[:], al2, writes=['AL2'])
            cmask = P.sb("cmask", [128, 17, 128], BF16)
            P.emit('pool', lambda e: e.memset(cmask[:], 0.0), writes=['cmask'])
            for dp in range(17):
                P.emit('pool', lambda e, dp=dp: e.affine_select(cmask[:, dp, :], cmask[:, dp, :], [[1, 128]], ALU.is_ge, NEG, base=128 * dp - 31, channel_multiplier=-16), reads=['cmask'], writes=['cmask'])
            negm = P.sb("negm", [128, 128], BF16)
            P.emit('pool', lambda e: e.memset(negm[:], 0.0), writes=['negm'])
            P.emit('pool', lambda e: e.affine_select(negm[:], negm[:], [[1, 128]], ALU.is_ge, NEG, base=0, channel_multiplier=-1), reads=['negm'], writes=['negm'])
            negw = P.sb("negw", [128, 128], BF16)
            P.emit('pool', lambda e: e.memset(negw[:], 0.0), writes=['negw'])
            P.emit('pool', lambda e: e.affine_select(negw[:], negw[:], [[-1, 128]], ALU.is_gt, NEG, base=0, channel_multiplier=1), reads=['negw'], writes=['negw'])
            Efull = P.sb("Efull", [128, T], BF16)
            P.emit('pool', lambda e: e.memset(Efull[:], 1.0), writes=['Efull'])
            P.emit('pool', lambda e: e.affine_select(Efull[:], Efull[:], [[1, T]], ALU.is_ge, 0.0, base=0, channel_multiplier=-64), reads=['Efull'], writes=['Efull'])
            P.emit('pool', lambda e: e.affine_select(Efull[:], Efull[:], [[-1, T]], ALU.is_ge, 0.0, base=63, channel_multiplier=64), reads=['Efull'], writes=['Efull'])
            NCH = (T // 16 + 127) // 128
            NCP = NCH * 128
            VC = P.sb("VC", [128, NCH, 193], BF16)
            P.emit('pool', lambda e: e.memset(VC[:], 1.0), writes=['VC'])
            P.emit('pool', lambda e: e.memset(VC[:, :, 0:64], 0.0), reads=['VC'], writes=['VC'])
            for m in range(NCH):
                P.emit('pool', lambda e, m=m: e.affine_select(VC[:, m, 65:193], VC[:, m, 65:193], [[-4, 128]], ALU.is_ge, 0.0, base=128 * m + 1, channel_multiplier=1), reads=['VC'], writes=['VC'])
                P.emit('pool', lambda e, m=m: e.affine_select(VC[:, m, 65:193], VC[:, m, 65:193], [[4, 128]], ALU.is_ge, 0.0, base=3 - 128 * m, channel_multiplier=-1), reads=['VC'], writes=['VC'])
            MAw = P.sb("MAw", [128, 256], F32)
            ADw = P.sb("ADw", [128, 256], F32)
            P.emit('pool', lambda e: e.memset(MAw[:], 0.0), writes=['MAw'])
            P.emit('pool', lambda e: e.memset(MAw[:, 0:127], 1.0), reads=['MAw'], writes=['MAw'])
            P.emit('pool', lambda e: e.memset(MAw[64:128, 127:128], 1.0), reads=['MAw'], writes=['MAw'])
            P.emit('pool', lambda e: e.memset(ADw[:], 0.0), writes=['ADw'])
            P.emit('pool', lambda e: e.memset(ADw[:, 128:256], -1.0), reads=['ADw'], writes=['ADw'])
            P.emit('pool', lambda e: e.memset(ADw[0:64, 127:128], 1e4), reads=['ADw'], writes=['ADw'])
            P.emit('pool', lambda e: e.memset(ADw[64:128, 128:129], 1e4), reads=['ADw'], writes=['ADw'])
            X2 = P.sb("X2", [128, 2, 544], BF16)
            P.emit('pool', lambda e: e.memset(X2[:], 0.0), writes=['X2'])
            KS2 = P.sb("KS2", [128, T], BF16)
            KW2 = P.sb("KW2", [128, 1024], BF16)
            KC2 = P.sb("KC2", [128, NCP], BF16)
            P.emit('pool', lambda e: e.memset(KC2[:], 0.0), writes=['KC2'])
            hidT = P.sb("hidT", [128, 2, 2, NCP], BF16)
            P.emit('pool', lambda e: e.memset(hidT[:], 0.0), writes=['hidT'])
            VS = P.sb("VS", [128, NT, 65], BF16)
            P.emit('pool', lambda e: e.memset(VS[:], 1.0), writes=['VA'])
            VW = P.sb("VW", [128, 8, 65], BF16)
            P.emit('pool', lambda e: e.memset(VW[:], 1.0), writes=['VA'])
            QP = P.sb("QP", [128, 4, 512], BF16)
            gts = P.sb("gts", [128, 4, 6], F32)
            PTn = [P.sb("PTn%d" % i, [128, 512], BF16) for i in range(2)]
            rinv = P.sb("rinv", [128, 4, 1], F32)
            imp = P.sb("imp", [128, 128], F32)
            imp2 = P.sb("imp2", [128, 128], F32)
            wrk = P.sb("wrk", [128, 128], F32)
            m8 = P.sb("m8", [128, 16], F32)
            selb = P.sb("selb", [128, 128], BF16)
            nselT = P.sb("nselT", [128, 128], BF16)
            RR = P.sb("RR", [128, 3, 2], F32)
            CO = P.sb("CO", [128, 3, 2], F32)
            nout = [P.sb("nout%d" % i, [128, 2, 64], F32) for i in range(2)]
            nsc = [0]
            nno = [0]

            def score_bank():
                i = nsc[0] % 2
                nsc[0] += 1
                return (bk_s0, bk_s1)[i], ('bk_s0', 'bk_s1')[i], PTn[i], 'PTn%d' % i

            def nsa_block(blk, hTb, kh):
                t0 = blk * 512
                if NS < 1:
                    return
                for hq in range(4):
                    for c in range(8):
                        P.mm(bk_in[0:64, :], wN[:, c, hq * 64:(hq + 1) * 64], hTb[:, c, :], start=(c == 0), stop=(c == 7), reads=['wN', kh], writes=['bk_in'])
                    P.emit('act', lambda e, hq=hq: e.mul(QP[0:64, hq, :], bk_in[0:64, :], 0.125), reads=['bk_in'], writes=['QP'])
                for kind in range(2):
                    for c in range(8):
                        P.mm(bk_in[:], wN[:, c, 256 + kind * 128:256 + (kind + 1) * 128], hTb[:, c, :], start=(c == 0), stop=(c == 7), reads=['wN', kh], writes=['bk_in'])
                    P.emit('act', lambda e, kind=kind: e.copy(X2[0:64, kind, 16:528], bk_in[0:64, :]), reads=['bk_in'], writes=['X2'])
                    P.emit('dve', lambda e, kind=kind: e.tensor_copy(X2[64:128, kind, 15:527], bk_in[64:128, :]), reads=['bk_in'], writes=['X2'])
                r0 = (blk % 2) * 512
                for (off, dst, kd, d0) in ((512, KS2, 'KS2', t0), (640, KW2, 'KW2', r0)):
                    for c in range(8):
                        P.mm(bk_in[:], wN[:, c, off:off + 128], hTb[:, c, :], start=(c == 0), stop=(c == 7), reads=['wN', kh], writes=['bk_in'])
                    P.emit('act', lambda e, dst=dst, d0=d0: e.copy(dst[:, d0:d0 + 512], bk_in[:]), reads=['bk_in'], writes=[kd])
                for ti in range(4):
                    tg = blk * 4 + ti
                    ts = slice(ti * 128, (ti + 1) * 128)
                    for c in range(8):
                        P.mm(bk_in[:, 0:128], hTb[:, c, ts], wN[:, c, 768:896], start=(c == 0), stop=(c == 7), reads=[kh, 'wN'], writes=['bk_in'])
                    P.emit('act', lambda e, tg=tg: e.copy(VS[:, tg, 0:64], bk_in[:, 0:64]), reads=['bk_in'], writes=['VA'])
                    P.emit('act', lambda e, tg=tg: e.copy(VW[:, tg % 8, 0:64], bk_in[:, 64:128]), reads=['bk_in'], writes=['VA'])
                    for c in range(8):
                        P.mm(bk_in[:, 128:134], hTb[:, c, ts], wNG[:, c, :], start=(c == 0), stop=(c == 7), reads=[kh, 'wNG'], writes=['bk_in'])
                    P.emit('act', lambda e, ti=ti: e.activation(gts[:, ti, :], bk_in[:, 128:134], AF.Sigmoid), reads=['bk_in'], writes=['gts'])
                nlo = max(0, 32 * blk - 1)
                nhi = 32 * blk + 30
                cnt = nhi - nlo + 1
                for kind in range(2):
                    for hc in range(2):
                        for j in range(16):
                            c0 = 16 * nlo + 2 * j - t0 + 16
                            P.mm(bk_in[:, 0:cnt], W1[:, kind, j, hc * 128:(hc + 1) * 128], X2[:, kind, c0:c0 + 16 * (cnt - 1) + 1:16], start=(j == 0), stop=(j == 15), reads=['W1', 'X2'], writes=['bk_in'])
                        P.emit('act', lambda e, kind=kind, hc=hc: e.activation(hidT[:, kind, hc, nlo:nhi + 1], bk_in[:, 0:cnt], AF.Silu, bias=posw[:, kind, hc:hc + 1]), reads=['bk_in', 'posw'], writes=['hidT'])
                for hc in range(2):
                    P.mm(bk_in[:, 0:cnt], W2k[:, hc, :], hidT[:, 0, hc, nlo:nhi + 1], start=(hc == 0), stop=(hc == 1), reads=['W2k', 'hidT'], writes=['bk_in'])
                P.emit('act', lambda e: e.copy(KC2[:, nlo:nhi + 1], bk_in[:, 0:cnt]), reads=['bk_in'], writes=['KC2'])
                for m in range(nlo // 128, nhi // 128 + 1):
                    for hc in range(2):
                        P.mm(bk_in[:, 0:64], hidT[:, 1, hc, m * 128:(m + 1) * 128], W2v[:, hc, :], start=(hc == 0), stop=(hc == 1), reads=['hidT', 'W2v'], writes=['bk_in'])
                    P.emit('act', lambda e, m=m: e.copy(VC[:, m, 0:64], bk_in[:, 0:64]), reads=['bk_in'], writes=['VC'])
                P.emit('pool', lambda e: e.tensor_copy(X2[:, :, 0:16], X2[:, :, 512:528]), reads=['X2'], writes=['X2'])
                if NS < 2:
                    return
                for ti in range(4):
                    qi = blk * 4 + ti
                    qs = slice(ti * 128, (ti + 1) * 128)
                    mlast = (8 * qi + 6) // 128
                    for m in range(mlast + 1):
                        dp = qi - 16 * m
                        ps, kps, pt, kpt = score_bank()
                        for h in range(4):
                            rows = slice(0, 64)
                            P.mm(ps[:, h * 128:(h + 1) * 128], KC2[rows, m * 128:(m + 1) * 128], QP[rows, h, qs], start=(h == 0), stop=False, reads=['KC2', 'QP'], writes=[kps])
                        P.mm(ps[:, 0:512].rearrange("p (h q) -> p h q", h=4), AL2[:, 128:256], TS[:, dp, :].unsqueeze(2).to_broadcast([2, 4, 128]), start=False, stop=(dp > 16), reads=['AL2', 'TS'], writes=[kps])
                        if dp <= 16:
                            P.mm(ps[:, 0:512].rearrange("p (h q) -> p h q", h=4), ident[:], cmask[:, dp, :].unsqueeze(1).to_broadcast([128, 4, 128]), start=False, stop=True, reads=['ident', 'cmask'], writes=[kps])
                        P.emit('act', lambda e, ps=ps, pt=pt: e.activation(pt[:], ps[:], AF.Exp), reads=[kps], writes=[kpt])
                        for h in range(4):
                            P.mm(bk_oc[:, h, 0:193], pt[:, h * 128:(h + 1) * 128], VC[:, m, :], start=(m == 0 and h % 2 == 0), stop=(m == mlast), reads=[kpt, 'VC'], writes=['bk_oc'])
                    if NS < 3:
                        continue
                    jlo = max(0, qi - 4)
                    for j in range(jlo, qi + 1):
                        dl = qi - j
                        ps, kps, pt, kpt = score_bank()
                        for h in range(2):
                            rows = slice(0, 64)
                            P.mm(ps[:, h * 128:(h + 1) * 128], KW2[rows, (j % 8) * 128:(j % 8 + 1) * 128], QP[rows, h, qs], start=(h == 0), stop=False, reads=['KW2', 'QP'], writes=[kps])
                        msk = negm if dl == 0 else (negw if dl == 4 else None)
                        P.mm(ps[:, 0:256].rearrange("p (h q) -> p h q", h=2), AL2[:, 0:128], TS[:, dl, 0:2].unsqueeze(2).to_broadcast([2, 2, 128]), start=False, stop=(msk is None), reads=['AL2', 'TS'], writes=[kps])
                        if msk is not None:
                            P.mm(ps[:, 0:256].rearrange("p (h q) -> p h q", h=2), ident[:], msk[:].unsqueeze(1).to_broadcast([128, 2, 128]), start=False, stop=True, reads=['ident', 'negm', 'negw'], writes=[kps])
                        P.emit('act', lambda e, ps=ps, pt=pt: e.activation(pt[:, 0:256], ps[:, 0:256], AF.Exp), reads=[kps], writes=[kpt])
                        for h in range(2):
                            P.mm(bk_ow[:, h * 128:h * 128 + 65], pt[:, h * 128:(h + 1) * 128], VW[:, j % 8, :], start=(j == jlo and h == 0), stop=(j == qi), reads=[kpt, 'VA'], writes=['bk_ow'])
                    if NS < 4:
                        continue
                    P.emit('dve', lambda e: e.tensor_scalar(rinv[:], bk_oc[:, :, 64:65], 1e-30, None, ALU.max), reads=['bk_oc'], writes=['rinv'])
                    P.emit('dve', lambda e: e.reciprocal(rinv[:], rinv[:]), reads=['rinv'], writes=['rinv'])
                    P.emit('dve', lambda e: e.tensor_scalar(imp[:], bk_oc[:, 0, 65:193], rinv[:, 0, :], None, ALU.mult), reads=['bk_oc', 'rinv'], writes=['imp'])
                    for h in range(1, 4):
                        P.emit('dve', lambda e, h=h: e.scalar_tensor_tensor(imp[:], bk_oc[:, h, 65:193], rinv[:, h, :], imp[:], ALU.mult, ALU.add), reads=['bk_oc', 'rinv', 'imp'], writes=['imp'])
                    wsl = slice(127 - 2 * qi, 255 - 2 * qi)
                    P.emit('dve', lambda e, wsl=wsl: e.tensor_tensor(imp2[:], imp[:], MAw[:, wsl], ALU.mult), reads=['imp', 'MAw'], writes=['imp2'])
                    P.emit('dve', lambda e, wsl=wsl: e.tensor_tensor(imp2[:], imp2[:], ADw[:, wsl], ALU.add), reads=['imp2', 'ADw'], writes=['imp2'])
                    P.emit('dve', lambda e: e.memset(imp2[:, 0:1], 1e4), reads=['imp2'], writes=['imp2'])
                    P.emit('dve', lambda e: e.max(m8[:, 0:8], imp2[:]), reads=['imp2'], writes=['m8'])
                    P.emit('dve', lambda e: e.match_replace(wrk[:], m8[:, 0:8], imp2[:], -1e9), reads=['imp2', 'm8'], writes=['wrk'])
                    P.emit('dve', lambda e: e.max(m8[:, 8:16], wrk[:]), reads=['wrk'], writes=['m8'])
                    P.emit('dve', lambda e: e.tensor_scalar(wrk[:], imp2[:], m8[:, 15:16], None, ALU.is_ge), reads=['imp2', 'm8'], writes=['wrk'])
                    P.emit('dve', lambda e: e.tensor_scalar(selb[:], wrk[:], -1.0, -NEG, ALU.add, ALU.mult), reads=['wrk'], writes=['selb'])
                    P.tr(pre.pT[:, 1, :], selb[:], ident[:], reads=['selb', 'ident'], writes=['pre_pT'])
                    P.emit('dve', lambda e: e.tensor_copy(nselT[:], pre.pT[:, 1, :]), reads=['pre_pT'], writes=['nselT'])
                    if NS < 5:
                        continue
                    for j in range(qi + 1):
                        dl = qi - j
                        ps, kps, pt, kpt = score_bank()
                        for h in range(2):
                            rows = slice(0, 64)
                            P.mm(ps[:, h * 128:(h + 1) * 128], KS2[rows, j * 128:(j + 1) * 128], QP[rows, h, qs], start=(h == 0), stop=False, reads=['KS2', 'QP'], writes=[kps])
                        P.mm(ps[:, 0:256].rearrange("p (h q) -> p h q", h=2), AL2[:, 0:128], TS[:, dl, 0:2].unsqueeze(2).to_broadcast([2, 2, 128]), start=False, stop=False, reads=['AL2', 'TS'], writes=[kps])
                        if dl == 0:
                            P.mm(ps[:, 0:256].rearrange("p (h q) -> p h q", h=2), ident[:], negm[:].unsqueeze(1).to_broadcast([128, 2, 128]), start=False, stop=False, reads=['ident', 'negm'], writes=[kps])
                        P.mm(ps[:, 0:256].rearrange("p (h q) -> p h q", h=2), Efull[:, j * 128:(j + 1) * 128], nselT[:].unsqueeze(1).to_broadcast([128, 2, 128]), start=False, stop=True, reads=['Efull', 'nselT'], writes=[kps])
                        P.emit('act', lambda e, ps=ps, pt=pt: e.activation(pt[:, 0:256], ps[:, 0:256], AF.Exp), reads=[kps], writes=[kpt])
                        for h in range(2):
                            P.mm(bk_os[:, h * 128:h * 128 + 65], pt[:, h * 128:(h + 1) * 128], VS[:, j, :], start=(j == 0 and h == 0), stop=(j == qi), reads=[kpt, 'VA'], writes=['bk_os'])
                    if NS < 6:
                        continue
                    P.emit('dve', lambda e: e.tensor_copy(RR[:, 0, :], rinv[:, 0:2, :].rearrange("p h o -> p (h o)")), reads=['rinv'], writes=['RR'])
                    P.emit('dve', lambda e: e.reciprocal(RR[:, 1, :], bk_os[:, 0:256].rearrange("p (h c) -> p h c", h=2)[:, :, 64]), reads=['bk_os', 'RR'], writes=['RR'])
                    P.emit('dve', lambda e: e.reciprocal(RR[:, 2, :], bk_ow[:, 0:256].rearrange("p (h c) -> p h c", h=2)[:, :, 64]), reads=['bk_ow', 'RR'], writes=['RR'])
                    P.emit('dve', lambda e, ti=ti: e.tensor_tensor(CO[:].rearrange("p a b -> p (a b)"), RR[:].rearrange("p a b -> p (a b)"), gts[:, ti, :], ALU.mult), reads=['RR', 'gts'], writes=['CO'])
                    no = nout[nno[0] % 2]
                    kno = 'nout%d' % (nno[0] % 2)
                    nno[0] += 1
                    for h in range(2):
                        P.emit('dve', lambda e, h=h, no=no: e.tensor_scalar(no[:, h, :], bk_oc[:, h, 0:64], CO[:, 0, h:h + 1], None, ALU.mult), reads=['bk_oc', 'CO'], writes=[kno])
                        P.emit('dve', lambda e, h=h, no=no: e.scalar_tensor_tensor(no[:, h, :], bk_os[:, h * 128:h * 128 + 64], CO[:, 1, h:h + 1], no[:, h, :], ALU.mult, ALU.add), reads=['bk_os', 'CO', kno], writes=[kno])
                        P.emit('dve', lambda e, h=h, no=no: e.scalar_tensor_tensor(no[:, h, :], bk_ow[:, h * 128:h * 128 + 64], CO[:, 2, h:h + 1], no[:, h, :], ALU.mult, ALU.add), reads=['bk_ow', 'CO', kno], writes=[kno])
                    P.dma(onsa[qi * 128:(qi + 1) * 128, :], no[:].rearrange("p h d -> p (h d)"), reads=[kno], eng='pool')

        for blk in range(NBLK):
            hTb = hT[blk % 2]
            kh = 'hT%d' % (blk % 2)
            for ti in range(4):
                pre.tile(blk * 512 + ti * 128, hTb[:, :, ti * 128:(ti + 1) * 128], kh)
            if do_gla:
                gla_block(blk, hTb, kh)
            if do_nsa:
                nsa_block(blk, hTb, kh)
        P.finish()
        print("mix1 instructions:", P.n_ins)
    return nc


T_FULL = 8192
_CACHE = {}


def _prog(name, fn):
    if name not in _CACHE:
        _CACHE[name] = fn()
    return _CACHE[name]


def _c(a):
    return np.ascontiguousarray(a)


def _mix0_maps(inp, xin, T):
    maps = []
    win = inp['l0_w_in']
    inv = np.asarray(10000.0 ** (-np.arange(0, 32, 2) / 32), np.float32)
    ri = np.zeros((128, 1), np.float32)
    ri[64:80, 0] = inv
    ri[80:96, 0] = inv
    for core in range(8):
        b, r = core // 4, core % 4
        g = r // 2
        m = {}
        m['x'] = _c(xin[b, :T])
        m['cT'] = _c(inp['c'][b].reshape(128, 8))
        m['ada_w'] = _c(inp['l0_ada_w'][:, 0:2048])
        m['ada_b'] = _c(inp['l0_ada_b'][0:2048])
        m['pre_g'] = _c(inp['l0_mix_pre_g'])
        xo = 1024
        Bo = 2048 + g * 128
        Co = 2048 + 256 + g * 128
        m['w_ssd'] = _c(np.concatenate([win[:, 256 * r:256 * r + 256], win[:, xo + 256 * r:xo + 256 * r + 256], win[:, Bo:Bo + 128], win[:, Co:Co + 128]], axis=1))
        m['w_dt'] = _c(win[:, 2560 + 4 * r:2560 + 4 * r + 4])
        chs = np.concatenate([np.arange(256 * r, 256 * r + 256), np.arange(1024 + g * 128, 1024 + g * 128 + 128), np.arange(1280 + g * 128, 1280 + g * 128 + 128)])
        cw = inp['l0_conv_w'][:, chs]
        m['conv_w'] = _c(cw.T.reshape(4, 128, 4).transpose(1, 0, 2))
        m['conv_b'] = _c(inp['l0_conv_b'][chs].reshape(4, 128).T)
        m['dt_bias'] = _c(inp['l0_dt_bias'][4 * r:4 * r + 4])
        m['a_log'] = _c(inp['l0_a_log'][4 * r:4 * r + 4])
        m['d_skip'] = _c(inp['l0_d_skip'][4 * r:4 * r + 4])
        m['w_qa'] = _c(win[:, 2576:2960])
        m['w_kva'] = _c(win[:, 2960:3216])
        m['w_kpe'] = _c(win[:, 3216:3248])
        m['w_qb'] = _c(inp['l0_w_q_b'][:, 2 * r * 96:(2 * r + 2) * 96])
        m['w_kvb'] = _c(inp['l0_w_kv_b'][:, 2 * r * 128:(2 * r + 2) * 128])
        m['gq'] = _c(inp['l0_q_a_norm_g'].reshape(3, 128).T)
        m['gkv'] = _c(inp['l0_kv_a_norm_g'].reshape(2, 128).T)
        m['rope_inv'] = ri
        m['pos'] = _c(inp['positions'][b, :T])
        maps.append(m)
    return maps


def _post_maps(inp, l, xin, cat, ntok):
    p = 'l%d_' % l
    maps = []
    per_b = xin.shape[1] // ntok
    for core in range(8):
        b, r = core // per_b, core % per_b
        m = {}
        m['x'] = _c(xin[b, r * ntok:(r + 1) * ntok])
        m['cat'] = _c(cat[b, r * ntok:(r + 1) * ntok])
        m['cT'] = _c(inp['c'][b].reshape(128, 8))
        m['ada_w'] = _c(inp[p + 'ada_w'][:, 2048:6144])
        m['ada_b'] = _c(inp[p + 'ada_b'][2048:6144])
        m['post_g'] = _c(inp[p + 'mix_post_g'])
        m['fpre_g'] = _c(inp[p + 'ffn_pre_g'])
        m['fpost_g'] = _c(inp[p + 'ffn_post_g'])
        m['ssm_g'] = _c(inp['l0_ssm_norm_g'])
        m['w_out'] = _c(inp[p + 'w_out'])
        m['w_gate'] = _c(inp[p + 'w_gate'])
        m['w_up'] = _c(inp[p + 'w_up'])
        m['w_down'] = _c(inp[p + 'w_down'])
        maps.append(m)
    return maps


def _nsa_heads(r):
    g, pr = r // 2, r % 2
    own = [4 * g + 2 * pr, 4 * g + 2 * pr + 1]
    oth = [4 * g + 2 * (1 - pr), 4 * g + 2 * (1 - pr) + 1]
    return g, own, oth


def _mix1_maps(inp, xin, T):
    maps = []
    win = inp['l1_w_in']
    cp = inp['l1_cmp_pos']
    pf = _c(cp.reshape(16, 2, 64).transpose(1, 2, 0).reshape(128, 16))
    al = np.zeros((2, 256), np.float32)
    al[0, 0:128] = np.arange(128) - 127
    al[1, 0:128] = 1
    al[0, 128:256] = 16 * np.arange(128) - 96
    al[1, 128:256] = 1
    al = al.astype(ml_dtypes.bfloat16)
    for core in range(8):
        b, r = core // 4, core % 4
        m = {}
        m['x'] = _c(xin[b, :T])
        m['cT'] = _c(inp['c'][b].reshape(128, 8))
        m['ada_w'] = _c(inp['l1_ada_w'][:, 0:2048])
        m['ada_b'] = _c(inp['l1_ada_b'][0:2048])
        m['pre_g'] = _c(inp['l1_mix_pre_g'])
        hh = r
        m['w_gla'] = _c(np.concatenate([win[:, hh * 128:(hh + 1) * 128], win[:, 512 + hh * 128:512 + (hh + 1) * 128], win[:, 1024 + hh * 256:1024 + (hh + 1) * 256], win[:, 2064 + hh * 256:2064 + (hh + 1) * 256]], axis=1))
        m['w_glr'] = _c(win[:, 2048:2064])
        wk = np.zeros((32, 128), np.float32)
        wk[0:16] = inp['l1_w_gk2'][:, hh * 128:(hh + 1) * 128]
        wk[16] = inp['l1_b_gk'][hh * 128:(hh + 1) * 128]
        m['w_gk2a'] = wk
        m['gla_g'] = _c(inp['l1_gla_norm_g'])
        g, own, oth = _nsa_heads(r)
        hs = own + oth
        base = 3600
        kc, vc, ks, vs, kw, vw = [win[:, base + i * 128 + g * 64: base + i * 128 + g * 64 + 64] for i in range(6)]
        m['w_nsa'] = _c(np.concatenate([win[:, 3088 + h * 64: 3088 + (h + 1) * 64] for h in hs] + [kc, kc, vc, vc, ks, ks, kw, kw, vs, vw], axis=1))
        gc = [4368 + g * 12 + (h - 4 * g) * 3 + br for br in range(3) for h in own]
        m['w_ng'] = _c(win[:, gc])
        m['cmp_w1'] = _c(np.stack([inp['l1_cmp_k_w1'], inp['l1_cmp_v_w1']]))
        m['cmp_w2k'] = _c(np.concatenate([inp['l1_cmp_k_w2'], inp['l1_cmp_k_w2']], axis=1))
        m['cmp_w2v'] = _c(inp['l1_cmp_v_w2'])
        m['cmp_posf'] = pf
        sl = np.array([2.0 ** -(h + 1) for h in hs], np.float32)
        ts = np.zeros((2, 64, 4), np.float32)
        ts[0] = sl[None, :]
        ts[1] = -128.0 * np.arange(64)[:, None] * sl[None, :]
        m['ts_tab'] = ts.reshape(2, 256).astype(ml_dtypes.bfloat16)
        m['al2'] = al
        maps.append(m)
    return maps


def kernel(**inputs):
    inp = {k: np.asarray(v) for k, v in inputs.items()}
    T = T_FULL
    B = 2
    cores = list(range(8))
    x = inp['x'].astype(np.float32, copy=False)
    maps0 = _mix0_maps(inp, x, T)
    nc0a = _prog('mix0a', lambda: build_mix0(T, True, False))
    r0a = run_bass_kernel_spmd(nc0a, maps0, core_ids=cores).results
    nc0b = _prog('mix0b', lambda: build_mix0(T, False, True))
    r0b = run_bass_kernel_spmd(nc0b, maps0, core_ids=cores).results
    cat0 = np.empty((B, T, 1536), np.float32)
    for core in range(8):
        b, r = core // 4, core % 4
        cat0[b, :, 256 * r:256 * r + 256] = r0a[core]['yg']
        cat0[b, :, 1024 + 128 * r:1024 + 128 * r + 128] = r0b[core]['omla']
    ncp0 = _prog('post0', lambda: build_post(True, 2048))
    rp0 = run_bass_kernel_spmd(ncp0, _post_maps(inp, 0, x, cat0, 2048), core_ids=cores).results
    x1 = np.empty((B, T, 1024), np.float32)
    for core in range(8):
        b, r = core // 4, core % 4
        x1[b, r * 2048:(r + 1) * 2048] = rp0[core]['xo']
    nc1 = _prog('mix1', lambda: build_mix1(T, True, True))
    r1 = run_bass_kernel_spmd(nc1, _mix1_maps(inp, x1, T), core_ids=cores).results
    cat1 = np.empty((B, T, 1536), np.float32)
    for core in range(8):
        b, r = core // 4, core % 4
        g, own, oth = _nsa_heads(r)
        cat1[b, :, 256 * r:256 * r + 256] = r1[core]['ogla']
        cat1[b, :, 1024 + own[0] * 64:1024 + own[0] * 64 + 128] = r1[core]['onsa']
    ncp1 = _prog('post1', lambda: build_post(False, 2048))
    rp1 = run_bass_kernel_spmd(ncp1, _post_maps(inp, 1, x1, cat1, 2048), core_ids=cores).results
    out = np.empty((B, T, 1024), np.float32)
    for core in range(8):
        b, r = core // 4, core % 4
        out[b, r * 2048:(r + 1) * 2048] = rp1[core]['xo']
    return out
```

```python
import math
import numpy as np
import ml_dtypes
import concourse.bass as bass
import concourse.mybir as mybir
from concourse.bass_utils import run_bass_kernel_spmd
from contextlib import ExitStack

F32 = mybir.dt.float32
BF16 = mybir.dt.bfloat16
I32 = mybir.dt.int32
AF = mybir.ActivationFunctionType
ALU = mybir.AluOpType
AX = mybir.AxisListType


class Prog:
    ENG = ('pe', 'dve', 'act', 'pool', 'sp')

    def __init__(self, nc, stack, n_dma_sems=24):
        self.nc = nc
        self.stack = stack
        self.q = {e: [] for e in self.ENG}
        self.semobj = {}
        for e in self.ENG:
            self.semobj[e] = stack.enter_context(nc.semaphore('s_' + e))
        for i in range(n_dma_sems):
            self.semobj[('dma', i)] = stack.enter_context(nc.semaphore('s_dma%d' % i))
        self.cnt = {e: 0 for e in self.ENG}
        self.dma_cnt = [0] * n_dma_sems
        self.dma_rr = 0
        self.seen = {e: {} for e in self.ENG}
        self.last_w = {}
        self.readers = {}
        self.n_ins = 0

    def sb(self, name, shape, dt):
        return self.stack.enter_context(self.nc.sbuf_tensor(name, list(shape), dt))

    def arena_init(self, nbytes):
        self.arena = self.stack.enter_context(self.nc.sbuf_tensor("arena", [128, nbytes // 2], BF16))
        self.arena_off = 0

    def arena_reset(self):
        self.arena_off = 0

    def ar(self, name, shape, dt):
        n = 1
        for d in shape[1:]:
            n *= d
        size = {F32: 4, BF16: 2, I32: 4}[dt]
        nb = (n * size + 3) // 4 * 4
        off = self.arena_off
        self.arena_off += nb
        assert self.arena_off <= self.arena.shape[1] * 2, ("arena overflow", name, self.arena_off)
        ap = self.arena[:, off // 2:(off + n * size) // 2]
        if dt != BF16:
            ap = ap.bitcast(dt)
        if len(shape) == 3:
            ap = ap.rearrange("p (a b) -> p a b", a=shape[1])
        elif len(shape) == 4:
            ap = ap.rearrange("p (a b c) -> p a b c", a=shape[1], b=shape[2])
        return ap

    def ps(self, name, shape, dt=F32):
        return self.stack.enter_context(self.nc.psum_tensor(name, list(shape), dt))

    def emit(self, eng, fn, reads=(), writes=(), dma=False):
        waits = {}

        def need(t):
            if t is not None:
                if eng == 'pe' and t[0] == 'pe':
                    return
                if waits.get(t[0], 0) < t[1]:
                    waits[t[0]] = t[1]

        for k in reads:
            need(self.last_w.get(k))
        for k in writes:
            need(self.last_w.get(k))
            for s, v in self.readers.get(k, {}).items():
                need((s, v))
        if dma:
            nsw = 8
            nhw = len(self.dma_cnt) - nsw
            if eng == 'pool':
                self.dma_rr_sw = (getattr(self, 'dma_rr_sw', -1) + 1) % nsw
                i = nhw + self.dma_rr_sw
            else:
                i = self.dma_rr
                self.dma_rr = (self.dma_rr + 1) % nhw
            prev = self.dma_cnt[i]
            if prev > 0:
                need((('dma', i), prev))
            self.dma_cnt[i] += 16
            ticket = (('dma', i), self.dma_cnt[i])
            inc = 16
        else:
            self.cnt[eng] += 1
            ticket = (eng, self.cnt[eng])
            inc = 1
        wl = []
        seen = self.seen[eng]
        for s, v in waits.items():
            if seen.get(s, 0) < v:
                seen[s] = v
                wl.append((s, v))
        self.q[eng].append((wl, fn, ticket[0], inc))
        self.n_ins += 1 + len(wl)
        for k in writes:
            self.last_w[k] = ticket
            self.readers[k] = {}
        for k in reads:
            if k in writes:
                continue
            r = self.readers.setdefault(k, {})
            if r.get(ticket[0], 0) < ticket[1]:
                r[ticket[0]] = ticket[1]
        return ticket

    def dma(self, out, in_, reads=(), writes=(), eng='sp', **kw):
        return self.emit(eng, lambda e: e.dma_start(out=out, in_=in_, **kw), reads, writes, dma=True)

    def mm(self, out, lhsT, rhs, start=True, stop=True, reads=(), writes=(), **kw):
        kw.setdefault("skip_group_check", True)
        return self.emit("pe", lambda e: e.matmul(out, lhsT, rhs, start=start, stop=stop, **kw), reads, writes)

    def tr(self, out, in_, ident, reads=(), writes=()):
        return self.emit('pe', lambda e: e.transpose(out, in_, ident), reads, writes)

    def finish(self):
        fin = []
        for e in self.ENG:
            if self.cnt[e] > 0:
                fin.append((e, self.cnt[e]))
        for i, c in enumerate(self.dma_cnt):
            if c > 0:
                fin.append((('dma', i), c))
        self.q['sp'].append((fin, None, None, 0))
        nc = self.nc
        semobj = self.semobj
        q = self.q
        waited = {e: set() for e in self.ENG}
        for name in self.ENG:
            for wl, fn, s, inc in q[name]:
                for ws, wv in wl:
                    if ws in waited:
                        waited[ws].add(wv)
        rank = {}
        for e in self.ENG:
            rank[e] = {v: i + 1 for i, v in enumerate(sorted(waited[e]))}

        def replay(name, eng):
            idx = 0
            for wl, fn, s, inc in q[name]:
                for ws, wv in wl:
                    if ws in rank:
                        eng.wait_ge(semobj[ws], rank[ws][wv])
                    else:
                        eng.wait_ge(semobj[ws], wv)
                if fn is not None:
                    ins = fn(eng)
                    if inc == 16:
                        ins.then_inc(semobj[s], 16)
                    else:
                        idx += 1
                        if idx in rank[name]:
                            ins.then_inc(semobj[s], 1)

        with nc.Block() as block:
            @block.tensor
            def _(eng):
                replay('pe', eng)

            @block.vector
            def _(eng):
                replay('dve', eng)

            @block.scalar
            def _(eng):
                replay('act', eng)

            @block.gpsimd
            def _(eng):
                replay('pool', eng)

            @block.sync
            def _(eng):
                replay('sp', eng)


EPS = 1e-6


def consts(P):
    nc = P.nc
    C = {}
    ident = P.sb("ident", [128, 128], BF16)
    P.emit('pool', lambda e: e.memset(ident[:], 0.0), writes=['ident'])
    P.emit('pool', lambda e: e.affine_select(ident[:], ident[:], [[-1, 128]], ALU.not_equal, 1.0, base=0, channel_multiplier=1), reads=['ident'], writes=['ident'])
    C['ident'] = ident
    tri = P.sb("tri", [128, 128], F32)
    P.emit('pool', lambda e: e.memset(tri[:], 1.0), writes=['tri'])
    P.emit('pool', lambda e: e.affine_select(tri[:], tri[:], [[1, 128]], ALU.is_ge, 0.0, base=0, channel_multiplier=-1), reads=['tri'], writes=['tri'])
    C['tri'] = tri
    strict = P.sb("strict", [128, 128], F32)
    P.emit('pool', lambda e: e.memset(strict[:], 1.0), writes=['strict'])
    P.emit('pool', lambda e: e.affine_select(strict[:], strict[:], [[-1, 128]], ALU.is_gt, 0.0, base=0, channel_multiplier=1), reads=['strict'], writes=['strict'])
    C['strict'] = strict
    ones = P.sb("ones", [128, 128], F32)
    P.emit('pool', lambda e: e.memset(ones[:], 1.0), writes=['ones'])
    C['ones'] = ones
    trib = P.sb("trib", [128, 128], BF16)
    P.emit('dve', lambda e: e.tensor_copy(trib[:], tri[:]), reads=['tri'], writes=['trib'])
    C['trib'] = trib
    oneb = P.sb("oneb", [128, 128], BF16)
    P.emit('pool', lambda e: e.memset(oneb[:], 1.0), writes=['oneb'])
    C['oneb'] = oneb
    return C


def ada_mod(P, cT, ada_w, ada_b, ncols, tag, pm, pmk, cb=512, mod=None):
    nc = P.nc
    if mod is None:
        mod = P.sb(tag + "mod", [128, ncols], F32)
    P.dma(mod[:], ada_b.partition_broadcast(128), writes=[tag + 'mod'])
    csb = P.sb(tag + "c", [128, 8], F32)
    P.dma(csb[:], cT, writes=[tag + 'c'])
    P.emit('act', lambda e: e.activation(csb[:], csb[:], AF.Silu), reads=[tag + 'c'], writes=[tag + 'c'])
    scb = P.sb(tag + "scb", [128, 8, 128], BF16)
    for c in range(8):
        P.emit('dve', lambda e, c=c: e.tensor_copy(scb[:, c, :], csb[:, c:c + 1].to_broadcast([128, 128])), reads=[tag + 'c'], writes=[tag + 'scb'])
    wst = P.stg[:, 0:8 * cb].rearrange("p (c n) -> p c n", c=8)
    wbf = P.sb(tag + "wbf", [128, 8, cb], BF16)
    awv = ada_w.rearrange("(p c) n -> p c n", c=8)
    for j in range(ncols // cb):
        P.dma(wst, awv[:, :, j * cb:(j + 1) * cb], writes=['stg'])
        P.emit('pool', lambda e: e.tensor_copy(wbf[:], wst), reads=['stg'], writes=[tag + 'wbf'])
        for c in range(8):
            P.mm(pm[:, 0:cb], scb[:, c, :], wbf[:, c, :], start=(c == 0), stop=(c == 7), reads=[tag + 'scb', tag + 'wbf'], writes=[pmk])
        P.emit('dve', lambda e, j=j: e.tensor_tensor(mod[:, j * cb:(j + 1) * cb], mod[:, j * cb:(j + 1) * cb], pm[:, 0:cb], ALU.add), reads=[pmk, tag + 'mod'], writes=[tag + 'mod'])
    return mod


def load_w(P, dst, dst_key, src_ap, shape, tag, cast_eng='pool'):
    n = 1
    for d in shape[1:]:
        n *= d
    st = P.stg[:, 0:n]
    if len(shape) == 3:
        st = st.rearrange("p (c n) -> p c n", c=shape[1])
    P.dma(st, src_ap, writes=['stg'])
    P.emit(cast_eng, lambda e: e.tensor_copy(dst, st), reads=['stg'], writes=[dst_key])


class PreStage:
    def __init__(self, P, C, G1, SH, xdram):
        self.P, self.C, self.G1, self.SH, self.x = P, C, G1, SH, xdram
        self.xt = [P.sb("pre_x%d" % i, [128, 1024], F32) for i in range(2)]
        self.junk = P.sb("pre_junk", [128, 1024], BF16)
        self.tmp = P.sb("pre_tmp", [128, 1024], F32)
        self.hb = [P.sb("pre_hb%d" % i, [128, 1024], BF16) for i in range(2)]
        self.ss = P.sb("pre_ss", [128, 8], F32)
        self.pT = P.ps("pre_pT", [128, 8, 128], BF16)
        self.n = 0

    def tile(self, t0, hT_dst, hT_key):
        P = self.P
        i = self.n % 2
        self.n += 1
        xt, hb = self.xt[i], self.hb[i]
        kx, kh = 'pre_x%d' % i, 'pre_hb%d' % i
        ss = self.ss
        P.dma(xt[:], self.x[t0:t0 + 128, :], writes=[kx])
        P.emit('act', lambda e: e.activation(self.junk[:], xt[:], AF.Square, accum_out=ss[:, 0:1]), reads=[kx], writes=['pre_junk', 'pre_ss'])
        P.emit('act', lambda e: e.activation(ss[:, 1:2], ss[:, 0:1], AF.Ln, bias=EPS, scale=1.0 / 1024), reads=['pre_ss'], writes=['pre_ss'])
        P.emit('act', lambda e: e.activation(ss[:, 2:3], ss[:, 1:2], AF.Exp, scale=-0.5), reads=['pre_ss'], writes=['pre_ss'])
        P.emit('dve', lambda e: e.scalar_tensor_tensor(self.tmp[:], xt[:], ss[:, 2:3], self.G1, ALU.mult, ALU.mult), reads=[kx, 'pre_ss', 'G1'], writes=['pre_tmp'])
        P.emit('dve', lambda e: e.tensor_tensor(hb[:], self.tmp[:], self.SH, ALU.add), reads=['pre_tmp', 'SH'], writes=[kh])
        for c in range(8):
            P.tr(self.pT[:, c, :], hb[:, c * 128:(c + 1) * 128], self.C['ident'][:], reads=[kh, 'ident'], writes=['pre_pT'])
        P.emit('act', lambda e: e.copy(hT_dst, self.pT[:]), reads=['pre_pT'], writes=[hT_key])


def build_mix0(T, do_ssd=True, do_mla=True):
    nc = bass.Bass("TRN2", target_bir_lowering=False)
    NBLK = T // 512

    def din(name, shape, dt=F32):
        return nc.dram_tensor(name, list(shape), dt, kind="ExternalInput").ap()

    x = din("x", [T, 1024])
    cT = din("cT", [128, 8])
    ada_w = din("ada_w", [1024, 2048])
    ada_b = din("ada_b", [2048])
    pre_g = din("pre_g", [1024])
    w_ssd = din("w_ssd", [1024, 768])
    w_dt = din("w_dt", [1024, 4])
    conv_w = din("conv_w", [128, 4, 4])
    conv_b = din("conv_b", [128, 4])
    dt_bias = din("dt_bias", [4])
    a_log = din("a_log", [4])
    d_skip = din("d_skip", [4])
    if do_ssd:
        yg = nc.dram_tensor("yg", [T, 256], F32, kind="ExternalOutput").ap()
    w_qa = din("w_qa", [1024, 384])
    w_kva = din("w_kva", [1024, 256])
    w_kpe = din("w_kpe", [1024, 32])
    w_qb = din("w_qb", [384, 192])
    w_kvb = din("w_kvb", [256, 256])
    gq = din("gq", [128, 3])
    gkv = din("gkv", [128, 2])
    rope_inv = din("rope_inv", [128, 1])
    pos = din("pos", [T], I32)
    if do_mla:
        omla = nc.dram_tensor("omla", [T, 128], F32, kind="ExternalOutput").ap()

    with ExitStack() as st:
        P = Prog(nc, st)
        C = consts(P)
        P.stg = P.sb('stg', [128, 6144], F32)
        p_in = [P.ps("p_in%d" % i, [128, 512], F32) for i in range(2)]
        mod = ada_mod(P, cT, ada_w, ada_b, 2048, "ada", p_in[0], "p_in0")
        g_bc = P.sb("g_bc", [128, 1024], F32)
        P.dma(g_bc[:], pre_g.partition_broadcast(128), writes=['g_bc'])
        G1 = P.sb("G1", [128, 1024], F32)
        P.emit('dve', lambda e: e.scalar_tensor_tensor(G1[:], mod[:, 1024:2048], 1.0, g_bc[:], ALU.add, ALU.mult), reads=['adamod', 'g_bc'], writes=['G1'])
        SH = mod[:, 0:1024]
        P.last_w['SH'] = P.last_w['adamod']

        if do_ssd:
            wS = P.sb("wS", [128, 8, 768], BF16)
            load_w(P, wS[:], 'wS', w_ssd.rearrange("(c p) n -> p c n", p=128), [128, 8, 768], "wS")
            wD = P.sb("wD", [128, 8, 4], BF16)
            load_w(P, wD[:], 'wD', w_dt.rearrange("(c p) n -> p c n", p=128), [128, 8, 4], "wD")
            cw = P.sb("cw", [128, 4, 4], F32)
            P.dma(cw[:], conv_w, writes=['cw'])
            cb = P.sb("cb", [128, 4], F32)
            P.dma(cb[:], conv_b, writes=['cb'])
            hp = P.sb("hp", [128, 12], F32)
            P.dma(hp[:, 0:4], dt_bias.partition_broadcast(128), writes=['hp'])
            P.dma(hp[:, 4:8], a_log.partition_broadcast(128), writes=['hp'])
            P.dma(hp[:, 8:12], d_skip.partition_broadcast(128), writes=['hp'])
            P.emit('act', lambda e: e.activation(hp[:, 4:8], hp[:, 4:8], AF.Exp), reads=['hp'], writes=['hp'])
            P.emit('dve', lambda e: e.tensor_scalar(hp[:, 4:8], hp[:, 4:8], -1.0, None, ALU.mult), reads=['hp'], writes=['hp'])

        pre = PreStage(P, C, G1[:], SH, x)
        hT = [P.sb("hT%d" % i, [128, 8, 512], BF16) for i in range(2)]
        if do_ssd:
            xcv = P.sb("xcv", [128, 4, 515], F32)
            P.emit('pool', lambda e: e.memset(xcv[:], 0.0), writes=['xcv'])
            acc = P.sb("acc", [128, 4, 512], F32)
            fT = P.sb("fT", [128, 4, 512], BF16)
            state = P.sb("state", [128, 256], F32)
            state_b = P.sb("state_b", [128, 256], BF16)
            P.emit('pool', lambda e: e.memset(state[:], 0.0), writes=['state'])
            P.emit('pool', lambda e: e.memset(state_b[:], 0.0), writes=['state_b'])

        p_sm = P.ps("p_sm", [128, 512], F32)
        p_d = P.ps("p_d", [128, 512], F32)
        p_y = P.ps("p_y", [128, 512], F32)
        p_o = P.ps("p_o", [128, 4, 65], F32)
        p_g = P.ps("p_g", [128, 512], F32)

        if do_ssd:
            zt = P.sb("zt", [128, 256], F32)
            dtt = P.sb("dtt", [128, 24], F32)
            etot = P.sb("etot", [128, 4], F32)
            ncs = P.sb("ncs", [128, 4], F32)
            xtok = P.sb("xtok", [128, 256], BF16)
            btok = P.sb("btok", [128, 128], BF16)
            xd = P.sb("xd", [128, 256], BF16)
            xdd = P.sb("xdd", [128, 256], BF16)
            lhb = P.sb("lhb", [128, 4, 2, 128], BF16)
            adh = P.sb("adh", [128, 8], BF16)
            Gm = P.sb("Gm", [128, 128], F32)
            El = P.sb("El", [128, 4, 128], F32)
            Mh = P.sb("Mh", [128, 4, 128], BF16)
            ycomb = P.sb("ycomb", [128, 256], F32)
            yout = [P.sb("yout%d" % i, [128, 256], F32) for i in range(2)]
            mask_sl = P.sb("mask_sl", [128, 128], F32)

        if do_mla:
            NT = T // 128
            KT = P.sb("KT", [128, 2, T], BF16)
            Vaug = P.sb("Vaug", [128, NT, 2, 65], BF16)
            P.emit('pool', lambda e: e.memset(Vaug[:], 1.0), writes=['Vaug'])
            QT = P.sb("QT", [128, 2, 512], BF16)
            qaT = P.sb("qaT", [128, 3, 512], BF16)
            sqq = P.sb("sqq", [128, 3, 512], BF16)
            kvaT = P.sb("kvaT", [128, 2, 512], BF16)
            sqk = P.sb("sqk", [128, 2, 512], BF16)
            rq = P.sb("rq", [128, 512], F32)
            rk = P.sb("rk", [128, 512], F32)
            cosF = P.sb("cosF", [128, 512], F32)
            sinF = P.sb("sinF", [128, 512], F32)
            cq = P.sb("cq", [128, 512], F32)
            sq_ = P.sb("sq_", [128, 512], F32)
            posi = P.sb("posi", [128, 512], I32)
            ang = P.sb("ang", [128, 512], F32)
            ang2 = P.sb("ang2", [128, 512], F32)
            rki = P.sb("rki", [128, 512], I32)
            rkf = P.sb("rkf", [128, 512], F32)
            rr = P.sb("rr", [128, 512], F32)
            rt1 = P.sb("rt1", [128, 512], F32)
            rt2 = P.sb("rt2", [128, 512], F32)
            rkt = P.sb("rkt", [128, 2], F32)
            PT = [P.sb("PT%d" % i, [128, 512], BF16) for i in range(2)]
            orec = P.sb("orec", [128, 4, 1], F32)
            oout = [P.sb("oout%d" % i, [128, 4, 64], F32) for i in range(2)]
            ones_bf = P.sb("ones_bf", [128, 128], BF16)
            P.emit('pool', lambda e: e.memset(ones_bf[:], 1.0), writes=['ones_bf'])
            negm = P.sb("negm", [128, 128], BF16)
            P.emit('pool', lambda e: e.memset(negm[:], 0.0), writes=['negm'])
            P.emit('pool', lambda e: e.affine_select(negm[:], negm[:], [[1, 128]], ALU.is_ge, -30000.0, base=0, channel_multiplier=-1), reads=['negm'], writes=['negm'])
            rinv = P.sb("rinv", [128, 1], F32)
            P.dma(rinv[:], rope_inv, writes=['rinv'])
            gqs = P.sb("gqs", [128, 3], F32)
            P.dma(gqs[:], gq, writes=['gqs'])
            gks = P.sb("gks", [128, 2], F32)
            P.dma(gks[:], gkv, writes=['gks'])
            wQA = P.sb("wQA", [128, 8, 384], BF16)
            load_w(P, wQA[:], 'wQA', w_qa.rearrange("(c p) n -> p c n", p=128), [128, 8, 384], "wQA")
            wKVA = P.sb("wKVA", [128, 8, 256], BF16)
            load_w(P, wKVA[:], 'wKVA', w_kva.rearrange("(c p) n -> p c n", p=128), [128, 8, 256], "wKVA")
            kpe_st = P.sb("kpe_st", [128, 8, 32], F32)
            P.dma(kpe_st[:], w_kpe.rearrange("(c p) n -> p c n", p=128), writes=['kpe_st'])
            wKPE = P.sb("wKPE", [128, 8, 96], BF16)
            wKPEr = P.sb("wKPEr", [128, 8, 96], BF16)
            P.emit('pool', lambda e: e.memset(wKPE[:], 0.0), writes=['wKPE'])
            P.emit('pool', lambda e: e.memset(wKPEr[:], 0.0), writes=['wKPEr'])
            P.emit('dve', lambda e: e.tensor_copy(wKPE[:, :, 64:96], kpe_st[:]), reads=['kpe_st', 'wKPE'], writes=['wKPE'])
            P.emit('dve', lambda e: e.tensor_scalar(wKPEr[:, :, 64:80], kpe_st[:, :, 16:32], -1.0, None, ALU.mult), reads=['kpe_st', 'wKPEr'], writes=['wKPEr'])
            P.emit('dve', lambda e: e.tensor_copy(wKPEr[:, :, 80:96], kpe_st[:, :, 0:16]), reads=['kpe_st', 'wKPEr'], writes=['wKPEr'])
            qb_st = P.sb("qb_st", [128, 3, 192], F32)
            P.dma(qb_st[:], w_qb.rearrange("(c p) n -> p c n", p=128), writes=['qb_st'])
            for c in range(3):
                P.emit('dve', lambda e, c=c: e.tensor_scalar(qb_st[:, c, :], qb_st[:, c, :], gqs[:, c:c + 1], None, ALU.mult), reads=['qb_st', 'gqs'], writes=['qb_st'])
            wQ = P.sb("wQ", [128, 2, 3, 96], BF16)
            wQr = P.sb("wQr", [128, 2, 3, 96], BF16)
            P.emit('pool', lambda e: e.memset(wQr[:], 0.0), writes=['wQr'])
            for hh in range(2):
                P.emit('dve', lambda e, hh=hh: e.tensor_copy(wQ[:, hh, :, :], qb_st[:, :, hh * 96:(hh + 1) * 96]), reads=['qb_st'], writes=['wQ'])
                P.emit('dve', lambda e, hh=hh: e.tensor_scalar(wQr[:, hh, :, 64:80], qb_st[:, :, hh * 96 + 80:hh * 96 + 96], -1.0, None, ALU.mult), reads=['qb_st', 'wQr'], writes=['wQr'])
                P.emit('dve', lambda e, hh=hh: e.tensor_copy(wQr[:, hh, :, 80:96], qb_st[:, :, hh * 96 + 64:hh * 96 + 80]), reads=['qb_st', 'wQr'], writes=['wQr'])
            kvb_st = P.sb("kvb_st", [128, 2, 256], F32)
            P.dma(kvb_st[:], w_kvb.rearrange("(c p) n -> p c n", p=128), writes=['kvb_st'])
            for c in range(2):
                P.emit('dve', lambda e, c=c: e.tensor_scalar(kvb_st[:, c, :], kvb_st[:, c, :], gks[:, c:c + 1], None, ALU.mult), reads=['kvb_st', 'gks'], writes=['kvb_st'])
            wKN = P.sb("wKN", [128, 2, 2, 64], BF16)
            wV = P.sb("wV", [128, 2, 2, 64], BF16)
            for hh in range(2):
                P.emit('dve', lambda e, hh=hh: e.tensor_copy(wKN[:, hh, :, :], kvb_st[:, :, hh * 128:hh * 128 + 64]), reads=['kvb_st'], writes=['wKN'])
                P.emit('dve', lambda e, hh=hh: e.tensor_copy(wV[:, :, hh, :], kvb_st[:, :, hh * 128 + 64:hh * 128 + 128]), reads=['kvb_st'], writes=['wV'])
            SC = 96.0 ** -0.5
            natt = [0]

            def rope_reduce(src, dst, key):
                R = slice(64, 96)
                P.emit('dve', lambda e: e.tensor_scalar(rki[R, :], src[R, :], 1.0 / (2 * math.pi), None, ALU.mult), reads=[key], writes=['rki'])
                P.emit('dve', lambda e: e.tensor_copy(rkf[R, :], rki[R, :]), reads=['rki'], writes=['rkf'])
                P.emit('dve', lambda e: e.scalar_tensor_tensor(rr[R, :], rkf[R, :], -2 * math.pi, src[R, :], ALU.mult, ALU.add), reads=['rkf', key], writes=['rr'])
                P.emit('dve', lambda e: e.tensor_scalar(rr[R, :], rr[R, :], -math.pi, math.pi, ALU.max, ALU.min), reads=['rr'], writes=['rr'])
                P.emit('act', lambda e: e.activation(dst[R, :], rr[R, :], AF.Sin), reads=['rr'], writes=[dst.tensor.name if False else key + '_out'])

            def mla_block(blk, hTb, kh):
                R = slice(64, 96)
                cs_ = slice(blk * 512, (blk + 1) * 512)
                for (wt, wk, nchunk, dstT, dsq, kd, ksq) in ((wQA, 'wQA', 3, qaT, sqq, 'qaT', 'sqq'), (wKVA, 'wKVA', 2, kvaT, sqk, 'kvaT', 'sqk')):
                    for j in range(nchunk):
                        pi = p_in[j % 2]
                        kp = 'p_in%d' % (j % 2)
                        for c in range(8):
                            P.mm(pi[:], wt[:, c, j * 128:(j + 1) * 128], hTb[:, c, :], start=(c == 0), stop=(c == 7), reads=[wk, kh], writes=[kp])
                        P.emit('act', lambda e, pi=pi, j=j, dstT=dstT: e.copy(dstT[:, j, :], pi[:]), reads=[kp], writes=[kd])
                        P.emit('act', lambda e, pi=pi, j=j, dsq=dsq: e.activation(dsq[:, j, :], pi[:], AF.Square), reads=[kp], writes=[ksq])
                for (dsq, ksq, nchunk, dim, rdst, krd) in ((sqq, 'sqq', 3, 384, rq, 'rq'), (sqk, 'sqk', 2, 256, rk, 'rk')):
                    for j in range(nchunk):
                        P.mm(p_sm[:], ones_bf[:], dsq[:, j, :], start=(j == 0), stop=(j == nchunk - 1), reads=['ones_bf', ksq], writes=['p_sm'])
                    P.emit('act', lambda e, rdst=rdst, dim=dim: e.activation(rdst[:], p_sm[:], AF.Ln, bias=EPS, scale=1.0 / dim), reads=['p_sm'], writes=[krd])
                    P.emit('act', lambda e, rdst=rdst: e.activation(rdst[:], rdst[:], AF.Exp, scale=-0.5), reads=[krd], writes=[krd])
                P.dma(posi[R, :], pos[blk * 512:(blk + 1) * 512].partition_broadcast(32), writes=['posi'])
                P.emit('dve', lambda e: e.tensor_copy(ang[R, :], posi[R, :]), reads=['posi'], writes=['ang'])
                P.emit('dve', lambda e: e.tensor_scalar(ang[R, :], ang[R, :], rinv[R, 0:1], None, ALU.mult), reads=['ang', 'rinv'], writes=['ang'])
                P.emit('dve', lambda e: e.tensor_scalar(ang2[R, :], ang[R, :], math.pi / 2, None, ALU.add), reads=['ang'], writes=['ang2'])
                rope_reduce(ang, sinF, 'ang')
                rope_reduce(ang2, cosF, 'ang2')
                P.emit('dve', lambda e: e.tensor_tensor(cq[R, :], rq[R, :], cosF[R, :], ALU.mult), reads=['rq', 'ang2_out'], writes=['cq'])
                P.emit('dve', lambda e: e.tensor_tensor(sq_[R, :], rq[R, :], sinF[R, :], ALU.mult), reads=['rq', 'ang_out'], writes=['sq_'])
                for c in range(8):
                    P.mm(p_y[0:96, :], wKPE[:, c, :], hTb[:, c, :], start=(c == 0), stop=(c == 7), reads=['wKPE', kh], writes=['p_y'])
                for c in range(8):
                    P.mm(p_d[0:96, :], wKPEr[:, c, :], hTb[:, c, :], start=(c == 0), stop=(c == 7), reads=['wKPEr', kh], writes=['p_d'])
                P.emit('dve', lambda e: e.tensor_tensor(rt1[R, :], p_y[R, :], cosF[R, :], ALU.mult), reads=['p_y', 'ang2_out'], writes=['rt1'])
                P.emit('dve', lambda e: e.tensor_tensor(rt2[R, :], p_d[R, :], sinF[R, :], ALU.mult), reads=['p_d', 'ang_out'], writes=['rt2'])
                P.emit('dve', lambda e: e.tensor_tensor(KT[R, 0, cs_], rt1[R, :], rt2[R, :], ALU.add), reads=['rt1', 'rt2'], writes=['KT'])
                P.emit('dve', lambda e: e.tensor_copy(KT[R, 1, cs_], KT[R, 0, cs_]), reads=['KT'], writes=['KT'])
                for hh in range(2):
                    pi = p_in[hh]
                    kp = 'p_in%d' % hh
                    for c in range(2):
                        P.mm(pi[0:64, :], wKN[:, hh, c, :], kvaT[:, c, :], start=(c == 0), stop=(c == 1), reads=['wKN', 'kvaT'], writes=[kp])
                    P.emit('dve', lambda e, pi=pi, hh=hh: e.tensor_tensor(KT[0:64, hh, cs_], pi[0:64, :], rk[0:64, :], ALU.mult), reads=[kp, 'rk'], writes=['KT'])
                for hh in range(2):
                    for c in range(3):
                        P.mm(p_in[0][0:96, :], wQ[:, hh, c, :], qaT[:, c, :], start=(c == 0), stop=(c == 2), reads=['wQ', 'qaT'], writes=['p_in0'])
                    for c in range(3):
                        P.mm(p_in[1][0:96, :], wQr[:, hh, c, :], qaT[:, c, :], start=(c == 0), stop=(c == 2), reads=['wQr', 'qaT'], writes=['p_in1'])
                    P.emit('dve', lambda e, hh=hh: e.tensor_tensor(QT[0:64, hh, :], p_in[0][0:64, :], rq[0:64, :], ALU.mult), reads=['p_in0', 'rq'], writes=['QT'])
                    P.emit('dve', lambda e: e.tensor_tensor(rt1[R, :], p_in[0][R, :], cq[R, :], ALU.mult), reads=['p_in0', 'cq'], writes=['rt1'])
                    P.emit('dve', lambda e: e.tensor_tensor(rt2[R, :], p_in[1][R, :], sq_[R, :], ALU.mult), reads=['p_in1', 'sq_'], writes=['rt2'])
                    P.emit('dve', lambda e, hh=hh: e.tensor_tensor(QT[R, hh, :], rt1[R, :], rt2[R, :], ALU.add), reads=['rt1', 'rt2'], writes=['QT'])
                for ti in range(4):
                    tg = blk * 4 + ti
                    ts = slice(ti * 128, (ti + 1) * 128)
                    for c in range(2):
                        P.mm(p_g[:, 0:128], kvaT[:, c, ts], wV[:, c, :, :].rearrange("p h d -> p (h d)"), start=(c == 0), stop=(c == 1), reads=['kvaT', 'wV'], writes=['p_g'])
                    for c in range(2):
                        P.mm(p_g[:, 128:129], sqk[:, c, ts], ones_bf[:, 0:1], start=(c == 0), stop=(c == 1), reads=['sqk', 'ones_bf'], writes=['p_g'])
                    P.emit('act', lambda e: e.activation(rkt[:, 0:1], p_g[:, 128:129], AF.Ln, bias=EPS, scale=1.0 / 256), reads=['p_g'], writes=['rkt'])
                    P.emit('act', lambda e: e.activation(rkt[:, 1:2], rkt[:, 0:1], AF.Exp, scale=-0.5), reads=['rkt'], writes=['rkt'])
                    P.emit('dve', lambda e, tg=tg: e.tensor_scalar(Vaug[:, tg, :, 0:64], p_g[:, 0:128].rearrange("p (h d) -> p h d", h=2), rkt[:, 1:2], None, ALU.mult), reads=['p_g', 'rkt'], writes=['Vaug'])
                for hh in range(2):
                    nj = 4 * blk + 4
                    for j in range(nj):
                        jj = j - 4 * blk
                        q0 = max(jj, 0) * 128
                        ps = (p_d, p_y)[natt[0] % 2]
                        kps = ('p_d', 'p_y')[natt[0] % 2]
                        pt = PT[natt[0] % 2]
                        kpt = 'PT%d' % (natt[0] % 2)
                        natt[0] += 1
                        kslice = slice(j * 128, (j + 1) * 128)
                        if jj >= 0:
                            P.mm(ps[:, q0:q0 + 128], C['ident'][:], negm[:], start=True, stop=False, reads=['ident', 'negm'], writes=[kps])
                            P.mm(ps[:, q0:q0 + 128], KT[0:96, hh, kslice], QT[0:96, hh, q0:q0 + 128], start=False, stop=True, reads=['KT', 'QT'], writes=[kps])
                            if q0 + 128 < 512:
                                P.mm(ps[:, q0 + 128:512], KT[0:96, hh, kslice], QT[0:96, hh, q0 + 128:512], start=True, stop=True, reads=['KT', 'QT'], writes=[kps])
                        else:
                            P.mm(ps[:, 0:512], KT[0:96, hh, kslice], QT[0:96, hh, 0:512], start=True, stop=True, reads=['KT', 'QT'], writes=[kps])
                        P.emit('act', lambda e, ps=ps, pt=pt, q0=q0: e.activation(pt[:, q0:512], ps[:, q0:512], AF.Exp, scale=SC), reads=[kps], writes=[kpt])
                        for qi in range(max(jj, 0), 4):
                            P.mm(p_o[:, qi, :], pt[:, qi * 128:(qi + 1) * 128], Vaug[:, j, hh, :], start=(j == 0 and qi == 0), stop=(j == 4 * blk + qi), reads=[kpt, 'Vaug'], writes=['p_o'])
                    oo = oout[hh]
                    ko = 'oout%d' % hh
                    P.emit('dve', lambda e: e.reciprocal(orec[:], p_o[:, :, 64:65]), reads=['p_o'], writes=['orec'])
                    P.emit('dve', lambda e, oo=oo: e.tensor_tensor(oo[:], p_o[:, :, 0:64], orec[:].to_broadcast([128, 4, 64]), ALU.mult), reads=['p_o', 'orec'], writes=[ko])
                    P.dma(omla[blk * 512:(blk + 1) * 512, hh * 64:(hh + 1) * 64].rearrange("(q p) d -> p q d", p=128), oo[:], reads=[ko], eng='pool')

        nt = 0
        for blk in range(NBLK):
            hTb = hT[blk % 2]
            kh = 'hT%d' % (blk % 2)
            for ti in range(4):
                pre.tile(blk * 512 + ti * 128, hTb[:, :, ti * 128:(ti + 1) * 128], kh)
            if do_ssd:
                for j in range(4):
                    pi = p_in[j % 2]
                    kp = 'p_in%d' % (j % 2)
                    for c in range(8):
                        P.mm(pi[:], wS[:, c, 256 + j * 128:256 + (j + 1) * 128], hTb[:, c, :], start=(c == 0), stop=(c == 7), reads=['wS', kh], writes=[kp])
                    P.emit('act', lambda e, j=j, pi=pi: e.copy(xcv[:, j, 3:515], pi[:]), reads=[kp], writes=['xcv'])
                for j in range(4):
                    P.emit('dve', lambda e, j=j: e.tensor_scalar(acc[:, j, :], xcv[:, j, 0:512], cw[:, j, 0:1], cb[:, j:j + 1], ALU.mult, ALU.add), reads=['xcv', 'cw', 'cb'], writes=[('acc', j)])
                    for k in range(1, 4):
                        P.emit('dve', lambda e, j=j, k=k: e.scalar_tensor_tensor(acc[:, j, :], xcv[:, j, k:k + 512], cw[:, j, k:k + 1], acc[:, j, :], ALU.mult, ALU.add), reads=['xcv', 'cw', ('acc', j)], writes=[('acc', j)])
                P.emit('act', lambda e: e.activation(fT[:], acc[:], AF.Silu), reads=[('acc', j) for j in range(4)], writes=['fT'])
                P.emit('pool', lambda e: e.tensor_copy(xcv[:, :, 0:3], xcv[:, :, 512:515]), reads=['xcv'] + [('acc', j) for j in range(4)], writes=['xcv'])
                for ti in range(4):
                    t0 = blk * 512 + ti * 128
                    ts = slice(ti * 128, (ti + 1) * 128)
                    for c in range(8):
                        P.mm(p_sm[:, 0:256], hTb[:, c, ts], wS[:, c, 0:256], start=(c == 0), stop=(c == 7), reads=[kh, 'wS'], writes=['p_sm'])
                    for c in range(8):
                        P.mm(p_sm[:, 256:260], hTb[:, c, ts], wD[:, c, :], start=(c == 0), stop=(c == 7), reads=[kh, 'wD'], writes=['p_sm'])
                    P.emit('act', lambda e: e.activation(zt[:], p_sm[:, 0:256], AF.Silu), reads=['p_sm'], writes=['zt'])
                    P.emit('dve', lambda e: e.tensor_tensor(dtt[:, 0:4], p_sm[:, 256:260], hp[:, 0:4], ALU.add), reads=['p_sm', 'hp'], writes=['dtt'])
                    P.emit('act', lambda e: e.activation(dtt[:, 0:4], dtt[:, 0:4], AF.Exp), reads=['dtt'], writes=['dtt'])
                    P.emit('act', lambda e: e.activation(dtt[:, 0:4], dtt[:, 0:4], AF.Ln, bias=1.0), reads=['dtt'], writes=['dtt'])
                    P.emit('dve', lambda e: e.tensor_tensor(dtt[:, 4:8], dtt[:, 0:4], hp[:, 4:8], ALU.mult), reads=['dtt', 'hp'], writes=['dtt'])
                    P.emit('dve', lambda e: e.tensor_copy(adh[:, 0:4], dtt[:, 4:8]), reads=['dtt'], writes=['adh'])
                    P.emit('dve', lambda e: e.tensor_tensor(adh[:, 4:8], dtt[:, 4:8], adh[:, 0:4], ALU.subtract), reads=['dtt', 'adh'], writes=['adh'])
                    P.mm(p_sm[:, 264:268], C['trib'][:], adh[:, 0:4], start=True, stop=False, reads=['trib', 'adh'], writes=['p_sm2'])
                    P.mm(p_sm[:, 264:268], C['trib'][:], adh[:, 4:8], start=False, stop=True, reads=['trib', 'adh'], writes=['p_sm2'])
                    P.mm(p_sm[:, 268:272], C['oneb'][:], adh[:, 0:4], start=True, stop=False, reads=['oneb', 'adh'], writes=['p_sm2'])
                    P.mm(p_sm[:, 268:272], C['oneb'][:], adh[:, 4:8], start=False, stop=True, reads=['oneb', 'adh'], writes=['p_sm2'])
                    P.emit('dve', lambda e: e.tensor_copy(dtt[:, 8:16], p_sm[:, 264:272]), reads=['p_sm2', 'dtt'], writes=['dtt'])
                    P.emit('act', lambda e: e.activation(dtt[:, 16:20], dtt[:, 8:12], AF.Exp), reads=['dtt'], writes=['dtt2'])
                    P.emit('dve', lambda e: e.tensor_tensor(dtt[:, 20:24], dtt[:, 12:16], dtt[:, 8:12], ALU.subtract), reads=['dtt'], writes=['dtt3'])
                    P.emit('act', lambda e: e.activation(dtt[:, 20:24], dtt[:, 20:24], AF.Exp), reads=['dtt3'], writes=['dtt3'])
                    P.emit('act', lambda e: e.activation(etot[:], dtt[:, 12:16], AF.Exp), reads=['dtt'], writes=['etot'])
                    for j in range(2):
                        P.tr(pre.pT[:, j, :], fT[:, j, ts], C['ident'][:], reads=['fT', 'ident'], writes=['pre_pT'])
                    P.tr(pre.pT[:, 2, :], fT[:, 2, ts], C['ident'][:], reads=['fT', 'ident'], writes=['pre_pT'])
                    P.emit('dve', lambda e: e.tensor_copy(xtok[:], pre.pT[:, 0:2, :].rearrange('p a b -> p (a b)')), reads=['pre_pT'], writes=['xtok'])
                    P.emit('act', lambda e: e.copy(btok[:], pre.pT[:, 2, :]), reads=['pre_pT'], writes=['btok'])
                    P.emit('dve', lambda e: e.tensor_tensor(xd[:].rearrange("p (h d) -> p h d", h=4), xtok[:].rearrange("p (h d) -> p h d", h=4), dtt[:, 0:4].unsqueeze(2).to_broadcast([128, 4, 64]), ALU.mult), reads=['xtok', 'dtt'], writes=['xd'])
                    P.emit('dve', lambda e: e.tensor_tensor(xdd[:].rearrange("p (h d) -> p h d", h=4), xd[:].rearrange("p (h d) -> p h d", h=4), dtt[:, 20:24].unsqueeze(2).to_broadcast([128, 4, 64]), ALU.mult), reads=['xd', 'dtt3'], writes=['xdd'])
                    P.mm(p_g[:, 0:128], fT[:, 2, ts], fT[:, 3, ts], reads=['fT'], writes=['p_g'])
                    P.emit('dve', lambda e: e.tensor_tensor(Gm[:], p_g[:, 0:128], C['tri'][:], ALU.mult), reads=['p_g', 'tri'], writes=['Gm'])
                    for h in range(4):
                        P.emit('pool', lambda e, h=h: e.tensor_scalar(lhb[:, h, 0, :], C['strict'][:], adh[:, h:h + 1], None, ALU.mult), reads=['strict', 'adh'], writes=[('lh', h)])
                        P.emit('pool', lambda e, h=h: e.tensor_scalar(lhb[:, h, 1, :], C['strict'][:], adh[:, 4 + h:5 + h], None, ALU.mult), reads=['strict', 'adh'], writes=[('lh', h)])
                        P.mm(p_d[:, h * 128:(h + 1) * 128], lhb[:, h, 0, :], C['trib'][:], start=True, stop=False, reads=[('lh', h), 'trib'], writes=['p_d'])
                        P.mm(p_d[:, h * 128:(h + 1) * 128], lhb[:, h, 1, :], C['trib'][:], start=False, stop=True, reads=[('lh', h), 'trib'], writes=['p_d'])
                    P.emit('act', lambda e: e.activation(El[:].rearrange("p h l -> p (h l)"), p_d[:], AF.Exp), reads=['p_d'], writes=['El'])
                    P.emit('dve', lambda e: e.tensor_tensor(Mh[:], El[:], Gm[:].unsqueeze(1).to_broadcast([128, 4, 128]), ALU.mult), reads=['El', 'Gm'], writes=['Mh'])
                    for h in range(4):
                        P.mm(p_y[:, h * 64:(h + 1) * 64], Mh[:, h, :], xd[:, h * 64:(h + 1) * 64], reads=['Mh', 'xd'], writes=['p_y'])
                    P.mm(p_y[:, 256:512], fT[:, 3, ts], state_b[:], reads=['fT', 'state_b'], writes=['p_y'])
                    P.emit('dve', lambda e: e.tensor_tensor(ycomb[:].rearrange("p (h d) -> p h d", h=4), p_y[:, 256:512].rearrange("p (h d) -> p h d", h=4), dtt[:, 16:20].unsqueeze(2).to_broadcast([128, 4, 64]), ALU.mult), reads=['p_y', 'dtt2'], writes=['ycomb'])
                    P.emit('dve', lambda e: e.tensor_tensor(ycomb[:], ycomb[:], p_y[:, 0:256], ALU.add), reads=['p_y', 'ycomb'], writes=['ycomb'])
                    yo = yout[nt % 2]
                    ky = 'yout%d' % (nt % 2)
                    P.emit('dve', lambda e, yo=yo: e.tensor_tensor(yo[:].rearrange("p (h d) -> p h d", h=4), xtok[:].rearrange("p (h d) -> p h d", h=4), hp[:, 8:12].unsqueeze(2).to_broadcast([128, 4, 64]), ALU.mult), reads=['xtok', 'hp'], writes=[ky])
                    P.emit('dve', lambda e, yo=yo: e.tensor_tensor(yo[:], yo[:], ycomb[:], ALU.add), reads=['ycomb', ky], writes=[ky])
                    P.emit('dve', lambda e, yo=yo: e.tensor_tensor(yo[:], yo[:], zt[:], ALU.mult), reads=['zt', ky], writes=[ky])
                    P.dma(yg[t0:t0 + 128, :], yo[:], reads=[ky], eng='pool')
                    P.mm(p_g[:, 256:512], btok[:], xdd[:], reads=['btok', 'xdd'], writes=['p_g2'])
                    P.emit('dve', lambda e: e.tensor_tensor(state[:].rearrange("p (h d) -> p h d", h=4), state[:].rearrange("p (h d) -> p h d", h=4), etot[:].unsqueeze(2).to_broadcast([128, 4, 64]), ALU.mult), reads=['state', 'etot'], writes=['state'])
                    P.emit('dve', lambda e: e.tensor_tensor(state[:], state[:], p_g[:, 256:512], ALU.add), reads=['state', 'p_g2'], writes=['state'])
                    P.emit('act', lambda e: e.copy(state_b[:], state[:]), reads=['state'], writes=['state_b'])
                    nt += 1
            if do_mla:
                mla_block(blk, hTb, kh)
        P.finish()
        print("mix0 instructions:", P.n_ins)
    return nc


NTOK = 2048
FH = 2816
NHC = 22


def barrier(P):
    fin = []
    for e in P.ENG:
        if P.cnt[e] > 0:
            fin.append((e, P.cnt[e]))
    for i, c in enumerate(P.dma_cnt):
        if c > 0:
            fin.append((('dma', i), c))
    for e in P.ENG:
        wl = []
        for s, v in fin:
            if P.seen[e].get(s, 0) < v:
                P.seen[e][s] = v
                wl.append((s, v))
        P.q[e].append((wl, None, None, 0))


def build_post(ssd_norm, ntok=NTOK):
    nc = bass.Bass("TRN2", target_bir_lowering=False)

    def din(name, shape, dt=F32):
        return nc.dram_tensor(name, list(shape), dt, kind="ExternalInput").ap()

    x = din("x", [ntok, 1024])
    cat = din("cat", [ntok, 1536])
    cT = din("cT", [128, 8])
    ada_w = din("ada_w", [1024, 4096])
    ada_b = din("ada_b", [4096])
    post_g = din("post_g", [1024])
    fpre_g = din("fpre_g", [1024])
    fpost_g = din("fpost_g", [1024])
    ssm_g = din("ssm_g", [1024])
    w_out = din("w_out", [1536, 1024])
    w_gate = din("w_gate", [1024, FH])
    w_up = din("w_up", [1024, FH])
    w_down = din("w_down", [FH, 1024])
    xo = nc.dram_tensor("xo", [ntok, 1024], F32, kind="ExternalOutput").ap()
    x1d = nc.dram_tensor("x1d", [ntok, 1024], F32).ap()
    hd = nc.dram_tensor("hd", [ntok, 1024], BF16).ap()
    NT = ntok // 128

    with ExitStack() as st:
        P = Prog(nc, st)
        C = consts(P)
        P.stg = P.sb('stg', [128, 2048], F32)
        p_a = [P.ps("p_a%d" % i, [128, 512], F32) for i in range(2)]
        pT = P.ps("pT", [128, 8, 128], BF16)
        p_g = [P.ps("p_g%d" % i, [128, 512], F32) for i in range(2)]
        p_u = [P.ps("p_u%d" % i, [128, 512], F32) for i in range(2)]
        mod = P.sb("adamod", [128, 4096], F32)
        wG = P.sb("wG", [128, 8, FH], BF16)
        wU = P.sb("wU", [128, 8, FH], BF16)
        wD = P.sb("wD", [128, NHC, 1024], BF16)
        P.arena_init(49184 + 64)
        if True:
            P.sb_save = P.sb
            P.sb = P.ar
            ada_mod(P, cT, ada_w, ada_b, 4096, "ada", p_a[0], "p_a0", cb=256, mod=mod)
            gb = P.sb("gb", [128, 1024], F32)
            P.dma(gb[:], post_g.partition_broadcast(128), writes=['gb'])
            P.emit('dve', lambda e: e.tensor_tensor(mod[:, 0:1024], mod[:, 0:1024], gb[:], ALU.mult), reads=['adamod', 'gb'], writes=['adamod'])
            P.dma(gb[:], fpre_g.partition_broadcast(128), writes=['gb'])
            P.emit('dve', lambda e: e.scalar_tensor_tensor(mod[:, 2048:3072], mod[:, 2048:3072], 1.0, gb[:], ALU.add, ALU.mult), reads=['adamod', 'gb'], writes=['adamod'])
            P.dma(gb[:], fpost_g.partition_broadcast(128), writes=['gb'])
            P.emit('dve', lambda e: e.tensor_tensor(mod[:, 3072:4096], mod[:, 3072:4096], gb[:], ALU.mult), reads=['adamod', 'gb'], writes=['adamod'])
            barrier(P)
            P.arena_reset()
        GP, SH2, G2, GF = mod[:, 0:1024], mod[:, 1024:2048], mod[:, 2048:3072], mod[:, 3072:4096]


        def load_ffn():
            ncast = 0
            for (wt, key, src) in ((wG, 'wG', w_gate), (wU, 'wU', w_up)):
                sv = src.rearrange("(c p) n -> p c n", p=128)
                for c in range(8):
                    for n0 in range(0, FH, 2048):
                        n1 = min(FH, n0 + 2048)
                        stv = P.stg[:, 0:n1 - n0]
                        P.dma(stv, sv[:, c, n0:n1], writes=['stg'])
                        P.emit('pool', lambda e, wt=wt, c=c, n0=n0, n1=n1, stv=stv: e.tensor_copy(wt[:, c, n0:n1], stv), reads=['stg'], writes=[key])

        def load_wd():
            sv = w_down.rearrange("(c p) n -> p c n", p=128)
            for c in range(0, NHC, 2):
                stv = P.stg[:, 0:2048].rearrange("p (c n) -> p c n", c=2)
                P.dma(stv, sv[:, c:c + 2, :], writes=['stg'])
                P.emit('pool', lambda e, c=c, stv=stv: e.tensor_copy(wD[:, c:c + 2, :], stv), reads=['stg'], writes=['wD'])

        if True:
            wO = wD[:, 0:12, :]
            if ssd_norm:
                gbs = P.sb("gb2", [128, 1024], F32)
                P.dma(gbs[:], ssm_g.partition_broadcast(128), writes=['gbs'])
            scr = wD[:, 12:22, :].rearrange("p a b -> p (a b)").bitcast(F32)
            sv = w_out.rearrange("(c p) n -> p c n", p=128)
            for c in range(0, 12, 2):
                stv = P.stg[:, 0:2048].rearrange("p (c n) -> p c n", c=2)
                P.dma(stv, sv[:, c:c + 2, :], writes=['stg'])
                P.emit('pool', lambda e, c=c, stv=stv: e.tensor_copy(wD[:, c:c + 2, :], stv), reads=['stg'], writes=['wO'])
            load_ffn()
            catt = [scr[:, 0:1536], scr[:, 1536:3072]]
            xt = [scr[:, 3072:4096]] * 2
            catb = P.sb("catb", [128, 1536], BF16)
            catT = P.sb("catT", [128, 12, 128], BF16)
            junk = P.sb("junk", [128, 1024], BF16)
            ss = P.sb("ss", [128, 16], F32)
            x1 = [scr[:, 4096:5120]] * 2
            tmp = P.sb("tmp", [128, 1024], F32)
            hb = [P.sb("hb0", [128, 1024], BF16)] * 2
            for t in range(NT):
                i = t % 2
                ct, kc = catt[i], 'catt%d' % i
                xx, kx = xt[i], 'xt0'
                P.dma(ct, cat[t * 128:(t + 1) * 128, :], writes=[kc])
                P.dma(xx, x[t * 128:(t + 1) * 128, :], writes=[kx])
                if ssd_norm:
                    for g in range(2):
                        P.emit('act', lambda e, g=g, ct=ct: e.activation(junk[:, 0:512], ct[:, g * 512:(g + 1) * 512], AF.Square, accum_out=ss[:, g:g + 1]), reads=[kc], writes=['junk', 'ss'])
                    P.emit('act', lambda e: e.activation(ss[:, 2:4], ss[:, 0:2], AF.Ln, bias=EPS, scale=1.0 / 512), reads=['ss'], writes=['ss'])
                    P.emit('act', lambda e: e.activation(ss[:, 2:4], ss[:, 2:4], AF.Exp, scale=-0.5), reads=['ss'], writes=['ss'])
                    for g in range(2):
                        P.emit('dve', lambda e, g=g, ct=ct: e.scalar_tensor_tensor(catb[:, g * 512:(g + 1) * 512], ct[:, g * 512:(g + 1) * 512], ss[:, 2 + g:3 + g], gbs[:, g * 512:(g + 1) * 512], ALU.mult, ALU.mult), reads=[kc, 'ss', 'gbs'], writes=['catb'])
                    P.emit('dve', lambda e, ct=ct: e.tensor_copy(catb[:, 1024:1536], ct[:, 1024:1536]), reads=[kc], writes=['catb'])
                else:
                    P.emit('dve', lambda e, ct=ct: e.tensor_copy(catb[:], ct), reads=[kc], writes=['catb'])
                for c0 in (0, 8):
                    n = min(8, 12 - c0)
                    for c in range(n):
                        P.tr(pT[:, c, :], catb[:, (c0 + c) * 128:(c0 + c + 1) * 128], C['ident'][:], reads=['catb', 'ident'], writes=['pT'])
                    P.emit('act', lambda e, c0=c0, n=n: e.copy(catT[:, c0:c0 + n, :], pT[:, 0:n, :]), reads=['pT'], writes=['catT'])
                for nb in range(2):
                    for c in range(12):
                        P.mm(p_a[nb][:], catT[:, c, :], wD[:, c, nb * 512:(nb + 1) * 512], start=(c == 0), stop=(c == 11), reads=['catT', 'wO'], writes=['p_a%d' % nb])
                for nb in range(2):
                    P.emit('act', lambda e, nb=nb: e.activation(junk[:, nb * 512:(nb + 1) * 512], p_a[nb][:], AF.Square, accum_out=ss[:, 4 + nb:5 + nb]), reads=['p_a%d' % nb], writes=['junk', 'ss'])
                P.emit('dve', lambda e: e.tensor_tensor(ss[:, 6:7], ss[:, 4:5], ss[:, 5:6], ALU.add), reads=['ss'], writes=['ss'])
                P.emit('act', lambda e: e.activation(ss[:, 7:8], ss[:, 6:7], AF.Ln, bias=EPS, scale=1.0 / 1024), reads=['ss'], writes=['ss'])
                P.emit('act', lambda e: e.activation(ss[:, 7:8], ss[:, 7:8], AF.Exp, scale=-0.5), reads=['ss'], writes=['ss'])
                x1t, k1 = x1[i], 'x1_0'
                for nb in range(2):
                    cs = slice(nb * 512, (nb + 1) * 512)
                    P.emit('dve', lambda e, nb=nb, cs=cs: e.scalar_tensor_tensor(tmp[:, cs], p_a[nb][:], ss[:, 7:8], GP[:, cs], ALU.mult, ALU.mult), reads=['p_a%d' % nb, 'ss', 'adamod'], writes=['tmp'])
                P.emit('dve', lambda e, x1t=x1t, xx=xx: e.tensor_tensor(x1t, tmp[:], xx, ALU.add), reads=['tmp', kx], writes=[k1])
                P.dma(x1d[t * 128:(t + 1) * 128, :], x1t, reads=[k1], writes=['x1d'], eng='pool')
                P.emit('act', lambda e, x1t=x1t: e.activation(junk[:], x1t, AF.Square, accum_out=ss[:, 8:9]), reads=[k1], writes=['junk', 'ss'])
                P.emit('act', lambda e: e.activation(ss[:, 9:10], ss[:, 8:9], AF.Ln, bias=EPS, scale=1.0 / 1024), reads=['ss'], writes=['ss'])
                P.emit('act', lambda e: e.activation(ss[:, 9:10], ss[:, 9:10], AF.Exp, scale=-0.5), reads=['ss'], writes=['ss'])
                P.emit('dve', lambda e, x1t=x1t: e.scalar_tensor_tensor(tmp[:], x1t, ss[:, 9:10], G2, ALU.mult, ALU.mult), reads=[k1, 'ss', 'adamod'], writes=['tmp'])
                hbt, khb = hb[i], 'hb0'
                P.emit('dve', lambda e, hbt=hbt: e.tensor_tensor(hbt[:], tmp[:], SH2, ALU.add), reads=['tmp', 'adamod'], writes=[khb])
                P.dma(hd[t * 128:(t + 1) * 128, :], hbt[:], reads=[khb], writes=['hd'], eng='pool')
            barrier(P)
            P.arena_reset()
        load_wd()
        hbl = [P.sb("hbl0", [128, 1024], BF16)] * 2
        hT = P.sb("hT", [128, 8, 512], BF16)
        actT = P.sb("actT", [128, NHC, 512], BF16)
        sg = [P.sb("sg0", [128, 512], F32)] * 2
        x1l = [P.sb("x1l0", [128, 1024], F32)] * 2
        ot = [P.sb("ot0", [128, 1024], F32)] * 2
        junk2 = P.sb("junk2", [128, 1024], BF16)
        tmp2 = P.sb("tmp2", [128, 1024], F32)
        ss2 = P.sb("ss2", [128, 8], F32)
        n2 = 0
        BS = min(512, ntok)
        NTI = BS // 128
        for blk in range(ntok // BS):
            for ti in range(NTI):
                t = blk * NTI + ti
                hl, kl = hbl[t % 2], 'hbl0'
                P.dma(hl[:], hd[t * 128:(t + 1) * 128, :], reads=['hd'], writes=[kl])
                for c in range(8):
                    P.tr(pT[:, c, :], hl[:, c * 128:(c + 1) * 128], C['ident'][:], reads=[kl, 'ident'], writes=['pT'])
                P.emit('act', lambda e, ti=ti: e.copy(hT[:, :, ti * 128:(ti + 1) * 128], pT[:]), reads=['pT'], writes=['hT'])
            for hc in range(NHC):
                i = hc % 2
                hs = slice(hc * 128, (hc + 1) * 128)
                for c in range(8):
                    P.mm(p_g[i][:, 0:BS], wG[:, c, hs], hT[:, c, 0:BS], start=(c == 0), stop=(c == 7), reads=['wG', 'hT'], writes=['p_g%d' % i])
                for c in range(8):
                    P.mm(p_u[i][:, 0:BS], wU[:, c, hs], hT[:, c, 0:BS], start=(c == 0), stop=(c == 7), reads=['wU', 'hT'], writes=['p_u%d' % i])
                P.emit('act', lambda e, i=i: e.activation(sg[i][:, 0:BS], p_g[i][:, 0:BS], AF.Silu), reads=['p_g%d' % i], writes=['sg0'])
                P.emit('dve', lambda e, i=i, hc=hc: e.tensor_tensor(actT[:, hc, 0:BS], sg[i][:, 0:BS], p_u[i][:, 0:BS], ALU.mult), reads=['sg0', 'p_u%d' % i], writes=[('actT', hc)])
            for ti in range(NTI):
                t = blk * NTI + ti
                i = t % 2
                ts = slice(ti * 128, (ti + 1) * 128)
                xl, kxl = x1l[i], 'x1l0'
                P.dma(xl[:], x1d[t * 128:(t + 1) * 128, :], reads=['x1d'], writes=[kxl])
                for nb in range(2):
                    for hc in range(NHC):
                        P.mm(p_a[nb][:], actT[:, hc, ts], wD[:, hc, nb * 512:(nb + 1) * 512], start=(hc == 0), stop=(hc == NHC - 1), reads=[('actT', hc), 'wD'], writes=['p_a%d' % nb])
                for nb in range(2):
                    P.emit('act', lambda e, nb=nb: e.activation(junk2[:, nb * 512:(nb + 1) * 512], p_a[nb][:], AF.Square, accum_out=ss2[:, nb:nb + 1]), reads=['p_a%d' % nb], writes=['junk2', 'ss2'])
                P.emit('dve', lambda e: e.tensor_tensor(ss2[:, 2:3], ss2[:, 0:1], ss2[:, 1:2], ALU.add), reads=['ss2'], writes=['ss2'])
                P.emit('act', lambda e: e.activation(ss2[:, 3:4], ss2[:, 2:3], AF.Ln, bias=EPS, scale=1.0 / 1024), reads=['ss2'], writes=['ss2'])
                P.emit('act', lambda e: e.activation(ss2[:, 3:4], ss2[:, 3:4], AF.Exp, scale=-0.5), reads=['ss2'], writes=['ss2'])
                for nb in range(2):
                    cs = slice(nb * 512, (nb + 1) * 512)
                    P.emit('dve', lambda e, nb=nb, cs=cs: e.scalar_tensor_tensor(tmp2[:, cs], p_a[nb][:], ss2[:, 3:4], GF[:, cs], ALU.mult, ALU.mult), reads=['p_a%d' % nb, 'ss2', 'adamod'], writes=['tmp2'])
                oo, ko = ot[i], 'ot0'
                P.emit('dve', lambda e, oo=oo, xl=xl: e.tensor_tensor(oo[:], tmp2[:], xl[:], ALU.add), reads=['tmp2', kxl], writes=[ko])
                P.dma(xo[t * 128:(t + 1) * 128, :], oo[:], reads=[ko], eng='pool')
        P.finish()
        print("post instructions:", P.n_ins)
    return nc


NEG = -30000.0


STOP = 99
NS = 99


def build_mix1(T, do_gla=True, do_nsa=True):
    nc = bass.Bass("TRN2", target_bir_lowering=False)
    NBLK = T // 512
    NT = T // 128

    def din(name, shape, dt=F32):
        return nc.dram_tensor(name, list(shape), dt, kind="ExternalInput").ap()

    x = din("x", [T, 1024])
    cT = din("cT", [128, 8])
    ada_w = din("ada_w", [1024, 2048])
    ada_b = din("ada_b", [2048])
    pre_g = din("pre_g", [1024])
    w_gla = din("w_gla", [1024, 768])
    w_glr = din("w_glr", [1024, 16])
    w_gk2a = din("w_gk2a", [32, 128])
    gla_g = din("gla_g", [256])
    ogla = nc.dram_tensor("ogla", [T, 256], F32, kind="ExternalOutput").ap()
    if do_nsa:
        w_nsa = din("w_nsa", [1024, 896])
        w_ng = din("w_ng", [1024, 6])
        cmp_w1 = din("cmp_w1", [2, 2048, 256])
        cmp_w2k = din("cmp_w2k", [256, 128])
        cmp_w2v = din("cmp_w2v", [256, 64])
        cmp_posf = din("cmp_posf", [128, 16])
        ts_tab = din("ts_tab", [2, 64 * 4], BF16)
        al2 = din("al2", [2, 256], BF16)
        onsa = nc.dram_tensor("onsa", [T, 128], F32, kind="ExternalOutput").ap()


    with ExitStack() as st:
        P = Prog(nc, st)
        C = consts(P)
        P.stg = P.sb('stg', [128, 2048], F32)

        def load_wk(dst, key, src, ncols):
            sv = src.rearrange("(c p) n -> p c n", p=128)
            for c in range(8):
                stv = P.stg[:, 0:ncols]
                P.dma(stv, sv[:, c, :], writes=['stg'])
                P.emit('pool', lambda e, c=c, stv=stv: e.tensor_copy(dst[:, c, :], stv), reads=['stg'], writes=[key])
        bk_in = P.ps("bk_in", [128, 512], F32)
        bk_s0 = P.ps("bk_s0", [128, 512], F32)
        bk_s1 = P.ps("bk_s1", [128, 512], F32)
        bk_oc = P.ps("bk_oc", [128, 4, 256], F32)
        bk_os = P.ps("bk_os", [128, 512], F32)
        bk_ow = P.ps("bk_ow", [128, 512], F32)
        mod = ada_mod(P, cT, ada_w, ada_b, 2048, "ada", bk_in, "bk_in", cb=256)
        g_bc = P.sb("g_bc", [128, 1024], F32)
        P.dma(g_bc[:], pre_g.partition_broadcast(128), writes=['g_bc'])
        G1 = P.sb("G1", [128, 1024], F32)
        P.emit('dve', lambda e: e.scalar_tensor_tensor(G1[:], mod[:, 1024:2048], 1.0, g_bc[:], ALU.add, ALU.mult), reads=['adamod', 'g_bc'], writes=['G1'])
        SH = mod[:, 0:1024]
        P.last_w['SH'] = P.last_w['adamod']
        pre = PreStage(P, C, G1[:], SH, x)
        hT = [P.sb("hT%d" % i, [128, 8, 512], BF16) for i in range(2)]
        ident = C['ident']
        tri = C['tri']

        wGL = P.sb("wGL", [128, 8, 768], BF16)
        load_wk(wGL, 'wGL', w_gla, 768)
        wGR = P.sb("wGR", [128, 8, 16], BF16)
        load_wk(wGR, 'wGR', w_glr, 16)
        wk2 = P.sb("wk2", [32, 128], F32)
        P.dma(wk2[:], w_gk2a, writes=['wk2'])
        wk2b = P.sb("wk2b", [32, 128], BF16)
        P.emit('dve', lambda e: e.tensor_copy(wk2b[:], wk2[:]), reads=['wk2'], writes=['wk2b'])
        l1h = P.sb("l1h", [128, 2, 128], BF16)
        glra = P.sb("glra", [32, 512], BF16)
        P.emit('pool', lambda e: e.memset(glra[:], 1.0), writes=['glra'])
        gg_bc = P.sb("gg_bc", [128, 256], F32)
        P.dma(gg_bc[:], gla_g.partition_broadcast(128), writes=['gg_bc'])
        qT = P.sb("qT", [128, 512], F32)
        kT = P.sb("kT", [128, 512], F32)
        e1 = P.sb("e1", [128, 128], F32)
        l1 = P.sb("l1", [128, 128], F32)
        Eneg = P.sb("Eneg", [128, 128], F32)
        Epos = P.sb("Epos", [128, 128], F32)
        qtl = P.sb("qtl", [128, 128], BF16)
        ktl = P.sb("ktl", [128, 128], BF16)
        ktok = P.sb("ktok", [128, 128], BF16)
        AT = P.sb("AT", [128, 128], BF16)
        vtok = P.sb("vtok", [128, 256], BF16)
        sgg = P.sb("sgg", [128, 256], F32)
        Sst = P.sb("Sst", [128, 256], F32)
        Sbf = P.sb("Sbf", [128, 256], BF16)
        P.emit('pool', lambda e: e.memset(Sst[:], 0.0), writes=['Sst'])
        P.emit('pool', lambda e: e.memset(Sbf[:], 0.0), writes=['Sbf'])
        gss = P.sb("gss", [128, 4], F32)
        gjunk = P.sb("gjunk", [128, 256], BF16)
        gt1 = P.sb("gt1", [128, 256], F32)
        gout = [P.sb("gout%d" % i, [128, 256], F32) for i in range(2)]
        ngl = [0]

        def gla_block(blk, hTb, kh):
            for (j, dst, kd) in ((0, qT, 'qT'), (1, kT, 'kT')):
                for c in range(8):
                    P.mm(bk_in[:], wGL[:, c, j * 128:(j + 1) * 128], hTb[:, c, :], start=(c == 0), stop=(c == 7), reads=['wGL', kh], writes=['bk_in'])
                P.emit('act', lambda e, dst=dst: e.copy(dst[:], bk_in[:]), reads=['bk_in'], writes=[kd])
            for c in range(8):
                P.mm(bk_in[0:16, :], wGR[:, c, :], hTb[:, c, :], start=(c == 0), stop=(c == 7), reads=['wGR', kh], writes=['bk_in'])
            P.emit('act', lambda e: e.copy(glra[0:16, :], bk_in[0:16, :]), reads=['bk_in'], writes=['glra'])
            if STOP < 1:
                return
            for ti in range(4):
                t0 = blk * 512 + ti * 128
                ts = slice(ti * 128, (ti + 1) * 128)
                for c in range(8):
                    P.mm(bk_in[:], hTb[:, c, ts], wGL[:, c, 256:768], start=(c == 0), stop=(c == 7), reads=[kh, 'wGL'], writes=['bk_in'])
                P.emit('act', lambda e: e.copy(vtok[:], bk_in[:, 0:256]), reads=['bk_in'], writes=['vtok'])
                P.emit('act', lambda e: e.activation(sgg[:], bk_in[:, 256:512], AF.Silu), reads=['bk_in'], writes=['sgg'])
                if STOP < 2:
                    continue
                P.mm(bk_s0[:, 0:128], glra[:, ts], wk2b[:], reads=['glra', 'wk2b'], writes=['bk_s0'])
                P.emit('act', lambda e: e.activation(e1[:], bk_s0[:, 0:128], AF.Exp, scale=-1.0), reads=['bk_s0'], writes=['e1'])
                P.emit('act', lambda e: e.activation(l1[:], e1[:], AF.Ln, bias=1.0), reads=['e1'], writes=['l1'])
                if STOP < 3:
                    continue
                P.emit('dve', lambda e: e.tensor_copy(l1h[:, 0, :], l1[:]), reads=['l1'], writes=['l1h'])
                P.emit('dve', lambda e: e.tensor_tensor(l1h[:, 1, :], l1[:], l1h[:, 0, :], ALU.subtract), reads=['l1', 'l1h'], writes=['l1h'])
                P.mm(bk_s0[:, 128:256], l1h[:, 0, :], C['trib'][:], start=True, stop=False, reads=['l1h', 'trib'], writes=['bk_s0'])
                P.mm(bk_s0[:, 128:256], l1h[:, 1, :], C['trib'][:], start=False, stop=True, reads=['l1h', 'trib'], writes=['bk_s0'])
                P.emit('act', lambda e: e.activation(Eneg[:], bk_s0[:, 128:256], AF.Exp, scale=-1.0 / 16), reads=['bk_s0'], writes=['Eneg'])
                P.emit('act', lambda e: e.activation(Epos[:], bk_s0[:, 128:256], AF.Exp, scale=1.0 / 16), reads=['bk_s0'], writes=['Epos'])
                P.emit('dve', lambda e, ts=ts: e.scalar_tensor_tensor(qtl[:], qT[:, ts], 128.0 ** -0.5, Eneg[:], ALU.mult, ALU.mult), reads=['qT', 'Eneg'], writes=['qtl'])
                P.emit('dve', lambda e, ts=ts: e.tensor_tensor(ktl[:], kT[:, ts], Epos[:], ALU.mult), reads=['kT', 'Epos'], writes=['ktl'])
                if STOP < 4:
                    continue
                P.tr(pre.pT[:, 0, :], ktl[:], ident[:], reads=['ktl', 'ident'], writes=['pre_pT'])
                P.emit('dve', lambda e: e.tensor_copy(ktok[:], pre.pT[:, 0, :]), reads=['pre_pT'], writes=['ktok'])
                P.mm(bk_s1[:, 0:128], ktl[:], qtl[:], reads=['ktl', 'qtl'], writes=['bk_s1'])
                P.emit('dve', lambda e: e.tensor_tensor(AT[:], bk_s1[:, 0:128], tri[:], ALU.mult), reads=['bk_s1', 'tri'], writes=['AT'])
                if STOP < 5:
                    continue
                P.mm(bk_os[:, 0:256], AT[:], vtok[:], start=True, stop=False, reads=['AT', 'vtok'], writes=['bk_os'])
                P.mm(bk_os[:, 0:256], qtl[:], Sbf[:], start=False, stop=True, reads=['qtl', 'Sbf'], writes=['bk_os'])
                P.mm(bk_ow[:, 0:256], ktok[:], vtok[:], reads=['ktok', 'vtok'], writes=['bk_ow'])
                P.emit('dve', lambda e: e.tensor_tensor(Sst[:], Sst[:], bk_ow[:, 0:256], ALU.add), reads=['Sst', 'bk_ow'], writes=['Sst'])
                P.emit('dve', lambda e: e.tensor_scalar(Sst[:], Sst[:], Eneg[:, 127:128], None, ALU.mult), reads=['Sst', 'Eneg'], writes=['Sst'])
                P.emit('act', lambda e: e.copy(Sbf[:], Sst[:]), reads=['Sst'], writes=['Sbf'])
                if STOP < 6:
                    continue
                P.emit('act', lambda e: e.activation(gjunk[:], bk_os[:, 0:256], AF.Square, accum_out=gss[:, 0:1]), reads=['bk_os'], writes=['gjunk', 'gss'])
                P.emit('act', lambda e: e.activation(gss[:, 1:2], gss[:, 0:1], AF.Ln, bias=EPS, scale=1.0 / 256), reads=['gss'], writes=['gss'])
                P.emit('act', lambda e: e.activation(gss[:, 1:2], gss[:, 1:2], AF.Exp, scale=-0.5), reads=['gss'], writes=['gss'])
                P.emit('dve', lambda e: e.scalar_tensor_tensor(gt1[:], bk_os[:, 0:256], gss[:, 1:2], gg_bc[:], ALU.mult, ALU.mult), reads=['bk_os', 'gss', 'gg_bc'], writes=['gt1'])
                go = gout[ngl[0] % 2]
                kgo = 'gout%d' % (ngl[0] % 2)
                ngl[0] += 1
                P.emit('dve', lambda e, go=go: e.tensor_tensor(go[:], gt1[:], sgg[:], ALU.mult), reads=['gt1', 'sgg'], writes=[kgo])
                P.dma(ogla[t0:t0 + 128, :], go[:], reads=[kgo], eng='pool')


        if do_nsa:
            wN = P.sb("wN", [128, 8, 896], BF16)
            load_wk(wN, 'wN', w_nsa, 896)
            wNG = P.sb("wNG", [128, 8, 6], BF16)
            load_wk(wNG, 'wNG', w_ng, 6)
            W1 = P.sb("W1", [128, 2, 16, 256], BF16)
            for kind in range(2):
                for j0 in range(0, 16, 8):
                    stv = P.stg[:, 0:2048].rearrange("p (j n) -> p j n", j=8)
                    P.dma(stv, cmp_w1[kind, j0 * 128:(j0 + 8) * 128, :].rearrange("(j p) n -> p j n", p=128), writes=['stg'])
                    P.emit('pool', lambda e, kind=kind, j0=j0, stv=stv: e.tensor_copy(W1[:, kind, j0:j0 + 8, :], stv), reads=['stg'], writes=['W1'])
            W2k = P.sb("W2k", [128, 2, 128], BF16)
            stv = P.stg[:, 0:256].rearrange("p (c n) -> p c n", c=2)
            P.dma(stv, cmp_w2k.rearrange("(c p) n -> p c n", p=128), writes=['stg'])
            P.emit('pool', lambda e, stv=stv: e.tensor_copy(W2k[:], stv), reads=['stg'], writes=['W2k'])
            W2v = P.sb("W2v", [128, 2, 64], BF16)
            stv2 = P.stg[:, 0:128].rearrange("p (c n) -> p c n", c=2)
            P.dma(stv2, cmp_w2v.rearrange("(c p) n -> p c n", p=128), writes=['stg'])
            P.emit('pool', lambda e, stv2=stv2: e.tensor_copy(W2v[:], stv2), reads=['stg'], writes=['W2v'])
            posf = P.sb("posf", [128, 16], F32)
            P.dma(posf[:], cmp_posf, writes=['posf'])
            posb = P.sb("posb", [128, 16], BF16)
            P.emit('dve', lambda e: e.tensor_copy(posb[:], posf[:]), reads=['posf'], writes=['posb'])
            posw = P.sb("posw", [128, 2, 2], F32)
            for kind in range(2):
                for hc in range(2):
                    for j in range(16):
                        P.mm(bk_in[:, 0:1], W1[:, kind, j, hc * 128:(hc + 1) * 128], posb[:, j:j + 1], start=(j == 0), stop=(j == 15), reads=['W1', 'posb'], writes=['bk_in'])
                    P.emit('dve', lambda e, kind=kind, hc=hc: e.tensor_copy(posw[:, kind, hc:hc + 1], bk_in[:, 0:1]), reads=['bk_in'], writes=['posw'])
            TS = P.sb("TS", [2, 64, 4], BF16)
            P.dma(TS[:].rearrange("p a b -> p (a b)"), ts_tab, writes=['TS'])
            AL2 = P.sb("AL2", [2, 256], BF16)
            P.dma(AL2[:], al2, writes=['AL2'])
            cmask = P.sb("cmask", [128, 17, 128], BF16)
            P.emit('pool', lambda e: e.memset(cmask[:], 0.0), writes=['cmask'])
            for dp in range(17):
                P.emit('pool', lambda e, dp=dp: e.affine_select(cmask[:, dp, :], cmask[:, dp, :], [[1, 128]], ALU.is_ge, NEG, base=128 * dp - 31, channel_multiplier=-16), reads=['cmask'], writes=['cmask'])
            negm = P.sb("negm", [128, 128], BF16)
            P.emit('pool', lambda e: e.memset(negm[:], 0.0), writes=['negm'])
            P.emit('pool', lambda e: e.affine_select(negm[:], negm[:], [[1, 128]], ALU.is_ge, NEG, base=0, channel_multiplier=-1), reads=['negm'], writes=['negm'])
            negw = P.sb("negw", [128, 128], BF16)
            P.emit('pool', lambda e: e.memset(negw[:], 0.0), writes=['negw'])
            P.emit('pool', lambda e: e.affine_select(negw[:], negw[:], [[-1, 128]], ALU.is_gt, NEG, base=0, channel_multiplier=1), reads=['negw'], writes=['negw'])
            Efull = P.sb("Efull", [128, T], BF16)
            P.emit('pool', lambda e: e.memset(Efull[:], 1.0), writes=['Efull'])
            P.emit('pool', lambda e: e.affine_select(Efull[:], Efull[:], [[1, T]], ALU.is_ge, 0.0, base=0, channel_multiplier=-64), reads=['Efull'], writes=['Efull'])
            P.emit('pool', lambda e: e.affine_select(Efull[:], Efull[:], [[-1, T]], ALU.is_ge, 0.0, base=63, channel_multiplier=64), reads=['Efull'], writes=['Efull'])
            NCH = (T // 16 + 127) // 128
            NCP = NCH * 128
            VC = P.sb("VC", [128, NCH, 193], BF16)
            P.emit('pool', lambda e: e.memset(VC[:], 1.0), writes=['VC'])
            P.emit('pool', lambda e: e.memset(VC[:, :, 0:64], 0.0), reads=['VC'], writes=['VC'])
            for m in range(NCH):
                P.emit('pool', lambda e, m=m: e.affine_select(VC[:, m, 65:193], VC[:, m, 65:193], [[-4, 128]], ALU.is_ge, 0.0, base=128 * m + 1, channel_multiplier=1), reads=['VC'], writes=['VC'])
                P.emit('pool', lambda e, m=m: e.affine_select(VC[:, m, 65:193], VC[:, m, 65:193], [[4, 128]], ALU.is_ge, 0.0, base=3 - 128 * m, channel_multiplier=-1), reads=['VC'], writes=['VC'])
            MAw = P.sb("MAw", [128, 256], F32)
            ADw = P.sb("ADw", [128, 256], F32)
            P.emit('pool', lambda e: e.memset(MAw[:], 0.0), writes=['MAw'])
            P.emit('pool', lambda e: e.memset(MAw[:, 0:127], 1.0), reads=['MAw'], writes=['MAw'])
            P.emit('pool', lambda e: e.memset(MAw[64:128, 127:128], 1.0), reads=['MAw'], writes=['MAw'])
            P.emit('pool', lambda e: e.memset(ADw[:], 0.0), writes=['ADw'])
            P.emit('pool', lambda e: e.memset(ADw[:, 128:256], -1.0), reads=['ADw'], writes=['ADw'])
            P.emit('pool', lambda e: e.memset(ADw[0:64, 127:128], 1e4), reads=['ADw'], writes=['ADw'])
            P.emit('pool', lambda e: e.memset(ADw[64:128, 128:129], 1e4), reads=['ADw'], writes=['ADw'])
            X2 = P.sb("X2", [128, 2, 544], BF16)
            P.emit('pool', lambda e: e.memset(X2[:], 0.0), writes=['X2'])
            KS2 = P.sb("KS2", [128, T], BF16)
            KW2 = P.sb("KW2", [128, 1024], BF16)
            KC2 = P.sb("KC2", [128, NCP], BF16)
            P.emit('pool', lambda e: e.memset(KC2[:], 0.0), writes=['KC2'])
            hidT = P.sb("hidT", [128, 2, 2, NCP], BF16)
            P.emit('pool', lambda e: e.memset(hidT[:], 0.0), writes=['hidT'])
            VS = P.sb("VS", [128, NT, 65], BF16)
            P.emit('pool', lambda e: e.memset(VS[:], 1.0), writes=['VA'])
            VW = P.sb("VW", [128, 8, 65], BF16)
            P.emit('pool', lambda e: e.memset(VW[:], 1.0), writes=['VA'])
            QP = P.sb("QP", [128, 4, 512], BF16)
            gts = P.sb("gts", [128, 4, 6], F32)
            PTn = [P.sb("PTn%d" % i, [128, 512], BF16) for i in range(2)]
            rinv = P.sb("rinv", [128, 4, 1], F32)
            imp = P.sb("imp", [128, 128], F32)
            imp2 = P.sb("imp2", [128, 128], F32)
            wrk = P.sb("wrk", [128, 128], F32)
            m8 = P.sb("m8", [128, 16], F32)
            selb = P.sb("selb", [128, 128], BF16)
            nselT = P.sb("nselT", [128, 128], BF16)
            RR = P.sb("RR", [128, 3, 2], F32)
            CO = P.sb("CO", [128, 3, 2], F32)
            nout = [P.sb("nout%d" % i, [128, 2, 64], F32) for i in range(2)]
            nsc = [0]
            nno = [0]

            def score_bank():
                i = nsc[0] % 2
                nsc[0] += 1
                return (bk_s0, bk_s1)[i], ('bk_s0', 'bk_s1')[i], PTn[i], 'PTn%d' % i

            def nsa_block(blk, hTb, kh):
                t0 = blk * 512
                if NS < 1:
                    return
                for hq in range(4):
                    for c in range(8):
                        P.mm(bk_in[0:64, :], wN[:, c, hq * 64:(hq + 1) * 64], hTb[:, c, :], start=(c == 0), stop=(c == 7), reads=['wN', kh], writes=['bk_in'])
                    P.emit('act', lambda e, hq=hq: e.mul(QP[0:64, hq, :], bk_in[0:64, :], 0.125), reads=['bk_in'], writes=['QP'])
                for kind in range(2):
                    for c in range(8):
                        P.mm(bk_in[:], wN[:, c, 256 + kind * 128:256 + (kind + 1) * 128], hTb[:, c, :], start=(c == 0), stop=(c == 7), reads=['wN', kh], writes=['bk_in'])
                    P.emit('act', lambda e, kind=kind: e.copy(X2[0:64, kind, 16:528], bk_in[0:64, :]), reads=['bk_in'], writes=['X2'])
                    P.emit('dve', lambda e, kind=kind: e.tensor_copy(X2[64:128, kind, 15:527], bk_in[64:128, :]), reads=['bk_in'], writes=['X2'])
                r0 = (blk % 2) * 512
                for (off, dst, kd, d0) in ((512, KS2, 'KS2', t0), (640, KW2, 'KW2', r0)):
                    for c in range(8):
                        P.mm(bk_in[:], wN[:, c, off:off + 128], hTb[:, c, :], start=(c == 0), stop=(c == 7), reads=['wN', kh], writes=['bk_in'])
                    P.emit('act', lambda e, dst=dst, d0=d0: e.copy(dst[:, d0:d0 + 512], bk_in[:]), reads=['bk_in'], writes=[kd])
                for ti in range(4):
                    tg = blk * 4 + ti
                    ts = slice(ti * 128, (ti + 1) * 128)
                    for c in range(8):
                        P.mm(bk_in[:, 0:128], hTb[:, c, ts], wN[:, c, 768:896], start=(c == 0), stop=(c == 7), reads=[kh, 'wN'], writes=['bk_in'])
                    P.emit('act', lambda e, tg=tg: e.copy(VS[:, tg, 0:64], bk_in[:, 0:64]), reads=['bk_in'], writes=['VA'])
                    P.emit('act', lambda e, tg=tg: e.copy(VW[:, tg % 8, 0:64], bk_in[:, 64:128]), reads=['bk_in'], writes=['VA'])
                    for c in range(8):
                        P.mm(bk_in[:, 128:134], hTb[:, c, ts], wNG[:, c, :], start=(c == 0), stop=(c == 7), reads=[kh, 'wNG'], writes=['bk_in'])
                    P.emit('act', lambda e, ti=ti: e.activation(gts[:, ti, :], bk_in[:, 128:134], AF.Sigmoid), reads=['bk_in'], writes=['gts'])
                nlo = max(0, 32 * blk - 1)
                nhi = 32 * blk + 30
                cnt = nhi - nlo + 1
                for kind in range(2):
                    for hc in range(2):
                        for j in range(16):
                            c0 = 16 * nlo + 2 * j - t0 + 16
                            P.mm(bk_in[:, 0:cnt], W1[:, kind, j, hc * 128:(hc + 1) * 128], X2[:, kind, c0:c0 + 16 * (cnt - 1) + 1:16], start=(j == 0), stop=(j == 15), reads=['W1', 'X2'], writes=['bk_in'])
                        P.emit('act', lambda e, kind=kind, hc=hc: e.activation(hidT[:, kind, hc, nlo:nhi + 1], bk_in[:, 0:cnt], AF.Silu, bias=posw[:, kind, hc:hc + 1]), reads=['bk_in', 'posw'], writes=['hidT'])
                for hc in range(2):
                    P.mm(bk_in[:, 0:cnt], W2k[:, hc, :], hidT[:, 0, hc, nlo:nhi + 1], start=(hc == 0), stop=(hc == 1), reads=['W2k', 'hidT'], writes=['bk_in'])
                P.emit('act', lambda e: e.copy(KC2[:, nlo:nhi + 1], bk_in[:, 0:cnt]), reads=['bk_in'], writes=['KC2'])
                for m in range(nlo // 128, nhi // 128 + 1):
                    for hc in range(2):
                        P.mm(bk_in[:, 0:64], hidT[:, 1, hc, m * 128:(m + 1) * 128], W2v[:, hc, :], start=(hc == 0), stop=(hc == 1), reads=['hidT', 'W2v'], writes=['bk_in'])
                    P.emit('act', lambda e, m=m: e.copy(VC[:, m, 0:64], bk_in[:, 0:64]), reads=['bk_in'], writes=['VC'])
                P.emit('pool', lambda e: e.tensor_copy(X2[:, :, 0:16], X2[:, :, 512:528]), reads=['X2'], writes=['X2'])
                if NS < 2:
                    return
                for ti in range(4):
                    qi = blk * 4 + ti
                    qs = slice(ti * 128, (ti + 1) * 128)
                    mlast = (8 * qi + 6) // 128
                    for m in range(mlast + 1):
                        dp = qi - 16 * m
                        ps, kps, pt, kpt = score_bank()
                        for h in range(4):
                            rows = slice(0, 64)
                            P.mm(ps[:, h * 128:(h + 1) * 128], KC2[rows, m * 128:(m + 1) * 128], QP[rows, h, qs], start=(h == 0), stop=False, reads=['KC2', 'QP'], writes=[kps])
                        P.mm(ps[:, 0:512].rearrange("p (h q) -> p h q", h=4), AL2[:, 128:256], TS[:, dp, :].unsqueeze(2).to_broadcast([2, 4, 128]), start=False, stop=(dp > 16), reads=['AL2', 'TS'], writes=[kps])
                        if dp <= 16:
                            P.mm(ps[:, 0:512].rearrange("p (h q) -> p h q", h=4), ident[:], cmask[:, dp, :].unsqueeze(1).to_broadcast([128, 4, 128]), start=False, stop=True, reads=['ident', 'cmask'], writes=[kps])
                        P.emit('act', lambda e, ps=ps, pt=pt: e.activation(pt[:], ps[:], AF.Exp), reads=[kps], writes=[kpt])
                        for h in range(4):
                            P.mm(bk_oc[:, h, 0:193], pt[:, h * 128:(h + 1) * 128], VC[:, m, :], start=(m == 0 and h % 2 == 0), stop=(m == mlast), reads=[kpt, 'VC'], writes=['bk_oc'])
                    if NS < 3:
                        continue
                    jlo = max(0, qi - 4)
                    for j in range(jlo, qi + 1):
                        dl = qi - j
                        ps, kps, pt, kpt = score_bank()
                        for h in range(2):
                            rows = slice(0, 64)
                            P.mm(ps[:, h * 128:(h + 1) * 128], KW2[rows, (j % 8) * 128:(j % 8 + 1) * 128], QP[rows, h, qs], start=(h == 0), stop=False, reads=['KW2', 'QP'], writes=[kps])
                        msk = negm if dl == 0 else (negw if dl == 4 else None)
                        P.mm(ps[:, 0:256].rearrange("p (h q) -> p h q", h=2), AL2[:, 0:128], TS[:, dl, 0:2].unsqueeze(2).to_broadcast([2, 2, 128]), start=False, stop=(msk is None), reads=['AL2', 'TS'], writes=[kps])
                        if msk is not None:
                            P.mm(ps[:, 0:256].rearrange("p (h q) -> p h q", h=2), ident[:], msk[:].unsqueeze(1).to_broadcast([128, 2, 128]), start=False, stop=True, reads=['ident', 'negm', 'negw'], writes=[kps])
                        P.emit('act', lambda e, ps=ps, pt=pt: e.activation(pt[:, 0:256], ps[:, 0:256], AF.Exp), reads=[kps], writes=[kpt])
                        for h in range(2):
                            P.mm(bk_ow[:, h * 128:h * 128 + 65], pt[:, h * 128:(h + 1) * 128], VW[:, j % 8, :], start=(j == jlo and h == 0), stop=(j == qi), reads=[kpt, 'VA'], writes=['bk_ow'])
                    if NS < 4:
                        continue
                    P.emit('dve', lambda e: e.tensor_scalar(rinv[:], bk_oc[:, :, 64:65], 1e-30, None, ALU.max), reads=['bk_oc'], writes=['rinv'])
                    P.emit('dve', lambda e: e.reciprocal(rinv[:], rinv[:]), reads=['rinv'], writes=['rinv'])
                    P.emit('dve', lambda e: e.tensor_scalar(imp[:], bk_oc[:, 0, 65:193], rinv[:, 0, :], None, ALU.mult), reads=['bk_oc', 'rinv'], writes=['imp'])
                    for h in range(1, 4):
                        P.emit('dve', lambda e, h=h: e.scalar_tensor_tensor(imp[:], bk_oc[:, h, 65:193], rinv[:, h, :], imp[:], ALU.mult, ALU.add), reads=['bk_oc', 'rinv', 'imp'], writes=['imp'])
                    wsl = slice(127 - 2 * qi, 255 - 2 * qi)
                    P.emit('dve', lambda e, wsl=wsl: e.tensor_tensor(imp2[:], imp[:], MAw[:, wsl], ALU.mult), reads=['imp', 'MAw'], writes=['imp2'])
                    P.emit('dve', lambda e, wsl=wsl: e.tensor_tensor(imp2[:], imp2[:], ADw[:, wsl], ALU.add), reads=['imp2', 'ADw'], writes=['imp2'])
                    P.emit('dve', lambda e: e.memset(imp2[:, 0:1], 1e4), reads=['imp2'], writes=['imp2'])
                    P.emit('dve', lambda e: e.max(m8[:, 0:8], imp2[:]), reads=['imp2'], writes=['m8'])
                    P.emit('dve', lambda e: e.match_replace(wrk[:], m8[:, 0:8], imp2[:], -1e9), reads=['imp2', 'm8'], writes=['wrk'])
                    P.emit('dve', lambda e: e.max(m8[:, 8:16], wrk[:]), reads=['wrk'], writes=['m8'])
                    P.emit('dve', lambda e: e.tensor_scalar(wrk[:], imp2[:], m8[:, 15:16], None, ALU.is_ge), reads=['imp2', 'm8'], writes=['wrk'])
                    P.emit('dve', lambda e: e.tensor_scalar(selb[:], wrk[:], -1.0, -NEG, ALU.add, ALU.mult), reads=['wrk'], writes=['selb'])
                    P.tr(pre.pT[:, 1, :], selb[:], ident[:], reads=['selb', 'ident'], writes=['pre_pT'])
                    P.emit('dve', lambda e: e.tensor_copy(nselT[:], pre.pT[:, 1, :]), reads=['pre_pT'], writes=['nselT'])
                    if NS < 5:
                        continue
                    for j in range(qi + 1):
                        dl = qi - j
                        ps, kps, pt, kpt = score_bank()
                        for h in range(2):
                            rows = slice(0, 64)
                            P.mm(ps[:, h * 128:(h + 1) * 128], KS2[rows, j * 128:(j + 1) * 128], QP[rows, h, qs], start=(h == 0), stop=False, reads=['KS2', 'QP'], writes=[kps])
                        P.mm(ps[:, 0:256].rearrange("p (h q) -> p h q", h=2), AL2[:, 0:128], TS[:, dl, 0:2].unsqueeze(2).to_broadcast([2, 2, 128]), start=False, stop=False, reads=['AL2', 'TS'], writes=[kps])
                        if dl == 0:
                            P.mm(ps[:, 0:256].rearrange("p (h q) -> p h q", h=2), ident[:], negm[:].unsqueeze(1).to_broadcast([128, 2, 128]), start=False, stop=False, reads=['ident', 'negm'], writes=[kps])
                        P.mm(ps[:, 0:256].rearrange("p (h q) -> p h q", h=2), Efull[:, j * 128:(j + 1) * 128], nselT[:].unsqueeze(1).to_broadcast([128, 2, 128]), start=False, stop=True, reads=['Efull', 'nselT'], writes=[kps])
                        P.emit('act', lambda e, ps=ps, pt=pt: e.activation(pt[:, 0:256], ps[:, 0:256], AF.Exp), reads=[kps], writes=[kpt])
                        for h in range(2):
                            P.mm(bk_os[:, h * 128:h * 128 + 65], pt[:, h * 128:(h + 1) * 128], VS[:, j, :], start=(j == 0 and h == 0), stop=(j == qi), reads=[kpt, 'VA'], writes=['bk_os'])
                    if NS < 6:
                        continue
                    P.emit('dve', lambda e: e.tensor_copy(RR[:, 0, :], rinv[:, 0:2, :].rearrange("p h o -> p (h o)")), reads=['rinv'], writes=['RR'])
                    P.emit('dve', lambda e: e.reciprocal(RR[:, 1, :], bk_os[:, 0:256].rearrange("p (h c) -> p h c", h=2)[:, :, 64]), reads=['bk_os', 'RR'], writes=['RR'])
                    P.emit('dve', lambda e: e.reciprocal(RR[:, 2, :], bk_ow[:, 0:256].rearrange("p (h c) -> p h c", h=2)[:, :, 64]), reads=['bk_ow', 'RR'], writes=['RR'])
                    P.emit('dve', lambda e, ti=ti: e.tensor_tensor(CO[:].rearrange("p a b -> p (a b)"), RR[:].rearrange("p a b -> p (a b)"), gts[:, ti, :], ALU.mult), reads=['RR', 'gts'], writes=['CO'])
                    no = nout[nno[0] % 2]
                    kno = 'nout%d' % (nno[0] % 2)
                    nno[0] += 1
                    for h in range(2):
                        P.emit('dve', lambda e, h=h, no=no: e.tensor_scalar(no[:, h, :], bk_oc[:, h, 0:64], CO[:, 0, h:h + 1], None, ALU.mult), reads=['bk_oc', 'CO'], writes=[kno])
                        P.emit('dve', lambda e, h=h, no=no: e.scalar_tensor_tensor(no[:, h, :], bk_os[:, h * 128:h * 128 + 64], CO[:, 1, h:h + 1], no[:, h, :], ALU.mult, ALU.add), reads=['bk_os', 'CO', kno], writes=[kno])
                        P.emit('dve', lambda e, h=h, no=no: e.scalar_tensor_tensor(no[:, h, :], bk_ow[:, h * 128:h * 128 + 64], CO[:, 2, h:h + 1], no[:, h, :], ALU.mult, ALU.add), reads=['bk_ow', 'CO', kno], writes=[kno])
                    P.dma(onsa[qi * 128:(qi + 1) * 128, :], no[:].rearrange("p h d -> p (h d)"), reads=[kno], eng='pool')

        for blk in range(NBLK):
            hTb = hT[blk % 2]
            kh = 'hT%d' % (blk % 2)
            for ti in range(4):
                pre.tile(blk * 512 + ti * 128, hTb[:, :, ti * 128:(ti + 1) * 128], kh)
            if do_gla:
                gla_block(blk, hTb, kh)
            if do_nsa:
                nsa_block(blk, hTb, kh)
        P.finish()
        print("mix1 instructions:", P.n_ins)
    return nc


T_FULL = 8192
_CACHE = {}


def _prog(name, fn):
    if name not in _CACHE:
        _CACHE[name] = fn()
    return _CACHE[name]


def _c(a):
    return np.ascontiguousarray(a)


def _mix0_maps(inp, xin, T):
    maps = []
    win = inp['l0_w_in']
    inv = np.asarray(10000.0 ** (-np.arange(0, 32, 2) / 32), np.float32)
    ri = np.zeros((128, 1), np.float32)
    ri[64:80, 0] = inv
    ri[80:96, 0] = inv
    for core in range(8):
        b, r = core // 4, core % 4
        g = r // 2
        m = {}
        m['x'] = _c(xin[b, :T])
        m['cT'] = _c(inp['c'][b].reshape(128, 8))
        m['ada_w'] = _c(inp['l0_ada_w'][:, 0:2048])
        m['ada_b'] = _c(inp['l0_ada_b'][0:2048])
        m['pre_g'] = _c(inp['l0_mix_pre_g'])
        xo = 1024
        Bo = 2048 + g * 128
        Co = 2048 + 256 + g * 128
        m['w_ssd'] = _c(np.concatenate([win[:, 256 * r:256 * r + 256], win[:, xo + 256 * r:xo + 256 * r + 256], win[:, Bo:Bo + 128], win[:, Co:Co + 128]], axis=1))
        m['w_dt'] = _c(win[:, 2560 + 4 * r:2560 + 4 * r + 4])
        chs = np.concatenate([np.arange(256 * r, 256 * r + 256), np.arange(1024 + g * 128, 1024 + g * 128 + 128), np.arange(1280 + g * 128, 1280 + g * 128 + 128)])
        cw = inp['l0_conv_w'][:, chs]
        m['conv_w'] = _c(cw.T.reshape(4, 128, 4).transpose(1, 0, 2))
        m['conv_b'] = _c(inp['l0_conv_b'][chs].reshape(4, 128).T)
        m['dt_bias'] = _c(inp['l0_dt_bias'][4 * r:4 * r + 4])
        m['a_log'] = _c(inp['l0_a_log'][4 * r:4 * r + 4])
        m['d_skip'] = _c(inp['l0_d_skip'][4 * r:4 * r + 4])
        m['w_qa'] = _c(win[:, 2576:2960])
        m['w_kva'] = _c(win[:, 2960:3216])
        m['w_kpe'] = _c(win[:, 3216:3248])
        m['w_qb'] = _c(inp['l0_w_q_b'][:, 2 * r * 96:(2 * r + 2) * 96])
        m['w_kvb'] = _c(inp['l0_w_kv_b'][:, 2 * r * 128:(2 * r + 2) * 128])
        m['gq'] = _c(inp['l0_q_a_norm_g'].reshape(3, 128).T)
        m['gkv'] = _c(inp['l0_kv_a_norm_g'].reshape(2, 128).T)
        m['rope_inv'] = ri
        m['pos'] = _c(inp['positions'][b, :T])
        maps.append(m)
    return maps


def _post_maps(inp, l, xin, cat, ntok):
    p = 'l%d_' % l
    maps = []
    per_b = xin.shape[1] // ntok
    for core in range(8):
        b, r = core // per_b, core % per_b
        m = {}
        m['x'] = _c(xin[b, r * ntok:(r + 1) * ntok])
        m['cat'] = _c(cat[b, r * ntok:(r + 1) * ntok])
        m['cT'] = _c(inp['c'][b].reshape(128, 8))
        m['ada_w'] = _c(inp[p + 'ada_w'][:, 2048:6144])
        m['ada_b'] = _c(inp[p + 'ada_b'][2048:6144])
        m['post_g'] = _c(inp[p + 'mix_post_g'])
        m['fpre_g'] = _c(inp[p + 'ffn_pre_g'])
        m['fpost_g'] = _c(inp[p + 'ffn_post_g'])
        m['ssm_g'] = _c(inp['l0_ssm_norm_g'])
        m['w_out'] = _c(inp[p + 'w_out'])
        m['w_gate'] = _c(inp[p + 'w_gate'])
        m['w_up'] = _c(inp[p + 'w_up'])
        m['w_down'] = _c(inp[p + 'w_down'])
        maps.append(m)
    return maps


def _nsa_heads(r):
    g, pr = r // 2, r % 2
    own = [4 * g + 2 * pr, 4 * g + 2 * pr + 1]
    oth = [4 * g + 2 * (1 - pr), 4 * g + 2 * (1 - pr) + 1]
    return g, own, oth


def _mix1_maps(inp, xin, T):
    maps = []
    win = inp['l1_w_in']
    cp = inp['l1_cmp_pos']
    pf = _c(cp.reshape(16, 2, 64).transpose(1, 2, 0).reshape(128, 16))
    al = np.zeros((2, 256), np.float32)
    al[0, 0:128] = np.arange(128) - 127
    al[1, 0:128] = 1
    al[0, 128:256] = 16 * np.arange(128) - 96
    al[1, 128:256] = 1
    al = al.astype(ml_dtypes.bfloat16)
    for core in range(8):
        b, r = core // 4, core % 4
        m = {}
        m['x'] = _c(xin[b, :T])
        m['cT'] = _c(inp['c'][b].reshape(128, 8))
        m['ada_w'] = _c(inp['l1_ada_w'][:, 0:2048])
        m['ada_b'] = _c(inp['l1_ada_b'][0:2048])
        m['pre_g'] = _c(inp['l1_mix_pre_g'])
        hh = r
        m['w_gla'] = _c(np.concatenate([win[:, hh * 128:(hh + 1) * 128], win[:, 512 + hh * 128:512 + (hh + 1) * 128], win[:, 1024 + hh * 256:1024 + (hh + 1) * 256], win[:, 2064 + hh * 256:2064 + (hh + 1) * 256]], axis=1))
        m['w_glr'] = _c(win[:, 2048:2064])
        wk = np.zeros((32, 128), np.float32)
        wk[0:16] = inp['l1_w_gk2'][:, hh * 128:(hh + 1) * 128]
        wk[16] = inp['l1_b_gk'][hh * 128:(hh + 1) * 128]
        m['w_gk2a'] = wk
        m['gla_g'] = _c(inp['l1_gla_norm_g'])
        g, own, oth = _nsa_heads(r)
        hs = own + oth
        base = 3600
        kc, vc, ks, vs, kw, vw = [win[:, base + i * 128 + g * 64: base + i * 128 + g * 64 + 64] for i in range(6)]
        m['w_nsa'] = _c(np.concatenate([win[:, 3088 + h * 64: 3088 + (h + 1) * 64] for h in hs] + [kc, kc, vc, vc, ks, ks, kw, kw, vs, vw], axis=1))
        gc = [4368 + g * 12 + (h - 4 * g) * 3 + br for br in range(3) for h in own]
        m['w_ng'] = _c(win[:, gc])
        m['cmp_w1'] = _c(np.stack([inp['l1_cmp_k_w1'], inp['l1_cmp_v_w1']]))
        m['cmp_w2k'] = _c(np.concatenate([inp['l1_cmp_k_w2'], inp['l1_cmp_k_w2']], axis=1))
        m['cmp_w2v'] = _c(inp['l1_cmp_v_w2'])
        m['cmp_posf'] = pf
        sl = np.array([2.0 ** -(h + 1) for h in hs], np.float32)
        ts = np.zeros((2, 64, 4), np.float32)
        ts[0] = sl[None, :]
        ts[1] = -128.0 * np.arange(64)[:, None] * sl[None, :]
        m['ts_tab'] = ts.reshape(2, 256).astype(ml_dtypes.bfloat16)
        m['al2'] = al
        maps.append(m)
    return maps


def kernel(**inputs):
    inp = {k: np.asarray(v) for k, v in inputs.items()}
    T = T_FULL
    B = 2
    cores = list(range(8))
    x = inp['x'].astype(np.float32, copy=False)
    maps0 = _mix0_maps(inp, x, T)
    nc0a = _prog('mix0a', lambda: build_mix0(T, True, False))
    r0a = run_bass_kernel_spmd(nc0a, maps0, core_ids=cores).results
    nc0b = _prog('mix0b', lambda: build_mix0(T, False, True))
    r0b = run_bass_kernel_spmd(nc0b, maps0, core_ids=cores).results
    cat0 = np.empty((B, T, 1536), np.float32)
    for core in range(8):
        b, r = core // 4, core % 4
        cat0[b, :, 256 * r:256 * r + 256] = r0a[core]['yg']
        cat0[b, :, 1024 + 128 * r:1024 + 128 * r + 128] = r0b[core]['omla']
    ncp0 = _prog('post0', lambda: build_post(True, 2048))
    rp0 = run_bass_kernel_spmd(ncp0, _post_maps(inp, 0, x, cat0, 2048), core_ids=cores).results
    x1 = np.empty((B, T, 1024), np.float32)
    for core in range(8):
        b, r = core // 4, core % 4
        x1[b, r * 2048:(r + 1) * 2048] = rp0[core]['xo']
    nc1 = _prog('mix1', lambda: build_mix1(T, True, True))
    r1 = run_bass_kernel_spmd(nc1, _mix1_maps(inp, x1, T), core_ids=cores).results
    cat1 = np.empty((B, T, 1536), np.float32)
    for core in range(8):
        b, r = core // 4, core % 4
        g, own, oth = _nsa_heads(r)
        cat1[b, :, 256 * r:256 * r + 256] = r1[core]['ogla']
        cat1[b, :, 1024 + own[0] * 64:1024 + own[0] * 64 + 128] = r1[core]['onsa']
    ncp1 = _prog('post1', lambda: build_post(False, 2048))
    rp1 = run_bass_kernel_spmd(ncp1, _post_maps(inp, 1, x1, cat1, 2048), core_ids=cores).results
    out = np.empty((B, T, 1024), np.float32)
    for core in range(8):
        b, r = core // 4, core % 4
        out[b, r * 2048:(r + 1) * 2048] = rp1[core]['xo']
    return out
```
